# Optimizing a Trainium2 kernel written in Bass

```python
import math
import jax, jax.numpy as jnp
from jax import lax
import numpy as np

D_MODEL = 1024
BATCH = 16
SEQ = 256
DEPTH = 2
DEC_BATCH = 2
DEC_SEQ = 2048
PAST_LEN = 512

GRID_W = 64
ROPE_BASE = 10000.0
QB = 128
CONV_W = 256
CONV_K = 3
DIFF_HEADS = 4
DIFF_DK = 32
DIFF_DV = 2 * DIFF_DK
DIFF_WIDTH = DIFF_HEADS * DIFF_DV
DIFF_QK_COLS = DIFF_HEADS * 2 * DIFF_DK
DIFF_SCALE = DIFF_DK ** -0.5
MLA_HEADS = 8
MLA_Q_RANK = 384
MLA_KV_RANK = 256
MLA_NOPE = 64
MLA_ROPE = 32
MLA_V = 64
MLA_QK = MLA_NOPE + MLA_ROPE
MLA_WIDTH = MLA_HEADS * MLA_V
MLA_SCALE = MLA_QK ** -0.5
MIX_WIDTH = CONV_W + DIFF_WIDTH + MLA_WIDTH
IN_SIZES = (CONV_W, CONV_W, CONV_W, DIFF_QK_COLS, DIFF_QK_COLS, DIFF_WIDTH, MLA_Q_RANK, MLA_KV_RANK, MLA_ROPE)
IN_COLS = 3 * CONV_W + 2 * DIFF_QK_COLS + DIFF_WIDTH + MLA_Q_RANK + MLA_KV_RANK + MLA_ROPE
D_FF = ((8 * D_MODEL // 3 + 255) // 256) * 256
DEEPNORM_ALPHA = (2 * DEPTH) ** 0.25
DEEPNORM_BETA = (8 * DEPTH) ** -0.25

kernel_name = "hybrid_diffusion_trunk_ctx_prefix_step"


def layer_norm(x, g, b, eps=1e-5):
    xf = x.astype(jnp.float32)
    mu = jnp.mean(xf, axis=-1, keepdims=True)
    var = jnp.mean(jnp.square(xf - mu), axis=-1, keepdims=True)
    return ((xf - mu) * lax.rsqrt(var + eps)).astype(x.dtype) * g + b


def rms_norm(x, w, eps=1e-6):
    xf = x.astype(jnp.float32)
    ms = jnp.mean(jnp.square(xf), axis=-1, keepdims=True)
    return (xf * lax.rsqrt(ms + eps)).astype(x.dtype) * w


def rope_1d(x, pos):
    half = x.shape[-1] // 2
    freqs = ROPE_BASE ** (-jnp.arange(half, dtype=jnp.float32) / half)
    ang = pos.astype(jnp.float32)[:, None] * freqs[None, :]
    cos = jnp.cos(ang).astype(x.dtype)
    sin = jnp.sin(ang).astype(x.dtype)
    x1, x2 = x[..., :half], x[..., half:]
    return jnp.concatenate([x1 * cos - x2 * sin, x1 * sin + x2 * cos], axis=-1)


def axial_rope(x, n_tokens):
    rows = n_tokens // GRID_W
    row = jnp.repeat(jnp.arange(rows), GRID_W)
    col = jnp.tile(jnp.arange(GRID_W), rows)
    h = x.shape[-1] // 2
    return jnp.concatenate([rope_1d(x[..., :h], row), rope_1d(x[..., h:], col)], axis=-1)


def map_query_blocks(fn, q):
    s = q.shape[-2]
    nb = s // QB
    qb = jnp.moveaxis(q.reshape(q.shape[:-2] + (nb, QB, q.shape[-1])), -3, 0)
    o = lax.map(fn, qb)
    o = jnp.moveaxis(o, 0, -3)
    return o.reshape(o.shape[:-3] + (s, o.shape[-1]))


def short_conv(u, w):
    up = jnp.pad(u, ((0, 0), (1, 1), (0, 0)))
    return up[:, :-2] * w[0] + up[:, 1:-1] * w[1] + up[:, 2:] * w[2]


def mixers(h, lp, lam, lam_init, ctx_cache):
    bsz, s, _ = h.shape
    offs = [int(o) for o in np.cumsum(IN_SIZES)[:-1]]
    proj = jnp.einsum("bsd,de->bse", h, lp["w_in"])
    a_x, a_b, a_c, d_q, d_k, d_v, m_cq, m_ckv, m_kpe = jnp.split(proj, offs, axis=-1)

    y_a = a_b * short_conv(a_c * a_x, lp["conv_w"])

    q = d_q.reshape(bsz, s, DIFF_HEADS, 2, DIFF_DK).transpose(0, 2, 3, 1, 4)
    k = d_k.reshape(bsz, s, DIFF_HEADS, 2, DIFF_DK).transpose(0, 2, 3, 1, 4)
    v = d_v.reshape(bsz, s, DIFF_HEADS, DIFF_DV).transpose(0, 2, 1, 3)

    cq = rms_norm(m_cq, lp["q_norm_w"])
    qc = jnp.einsum("bsr,re->bse", cq, lp["w_uq"]).reshape(bsz, s, MLA_HEADS, MLA_QK).transpose(0, 2, 1, 3)
    q_nope, q_pe = qc[..., :MLA_NOPE], qc[..., MLA_NOPE:]
    ckv = rms_norm(m_ckv, lp["kv_norm_w"])
    kpe = m_kpe

    if ctx_cache is None:
        new_ctx = (k, v, ckv, kpe)
        k_all, v_all, ckv_all, kpe_all = k, v, ckv, kpe
    else:
        q = axial_rope(q, s)
        k = axial_rope(k, s)
        q_pe = axial_rope(q_pe, s)
        kpe = axial_rope(kpe, s)
        ck, cv, cckv, ckpe = ctx_cache
        k_all = jnp.concatenate([k, ck], axis=3)
        v_all = jnp.concatenate([v, cv], axis=2)
        ckv_all = jnp.concatenate([ckv, cckv], axis=1)
        kpe_all = jnp.concatenate([kpe, ckpe], axis=1)
        new_ctx = None
    n_keys = ckv_all.shape[1]

    def diff_block(qb):
        sc = jnp.einsum("bhmqd,bhmkd->bhmqk", qb, k_all).astype(jnp.float32) * DIFF_SCALE
        p = jax.nn.softmax(sc, axis=-1)
        a = p[:, :, 0] - lam * p[:, :, 1]
        return jnp.einsum("bhqk,bhkd->bhqd", a.astype(v_all.dtype), v_all)

    o_b = map_query_blocks(diff_block, q)
    o_b = rms_norm(o_b, lp["diff_norm_w"]) * (1.0 - lam_init)
    y_b = o_b.transpose(0, 2, 1, 3).reshape(bsz, s, DIFF_WIDTH)

    kv = jnp.einsum("bnr,re->bne", ckv_all, lp["w_ukv"]).reshape(bsz, n_keys, MLA_HEADS, MLA_NOPE + MLA_V).transpose(0, 2, 1, 3)
    k_m = jnp.concatenate([kv[..., :MLA_NOPE], jnp.broadcast_to(kpe_all[:, None], (bsz, MLA_HEADS, n_keys, MLA_ROPE))], axis=-1)
    v_m = kv[..., MLA_NOPE:]
    q_m = jnp.concatenate([q_nope, q_pe], axis=-1)

    def mla_block(qb):
        sc = jnp.einsum("bhqd,bhkd->bhqk", qb, k_m).astype(jnp.float32) * MLA_SCALE
        p = jax.nn.softmax(sc, axis=-1)
        return jnp.einsum("bhqk,bhkd->bhqd", p.astype(v_m.dtype), v_m)

    o_c = map_query_blocks(mla_block, q_m)
    y_c = o_c.transpose(0, 2, 1, 3).reshape(bsz, s, MLA_WIDTH)

    y = jnp.einsum("bse,ed->bsd", jnp.concatenate([y_a, y_b, y_c], axis=-1), lp["w_out"])
    return y, new_ctx


def layer(x, cond, lp, layer_idx, ctx_cache):
    mod = jax.nn.silu(cond) @ lp["w_ada"] + lp["b_ada"]
    mod = mod.reshape((-1, 1, mod.shape[-1]))
    sh1, sc1, g1, sh2, sc2, g2 = jnp.split(mod, 6, axis=-1)
    lam_init = 0.8 - 0.6 * math.exp(-0.3 * layer_idx)
    lam = (jnp.exp(jnp.sum(lp["lam_q1"].astype(jnp.float32) * lp["lam_k1"].astype(jnp.float32)))
           - jnp.exp(jnp.sum(lp["lam_q2"].astype(jnp.float32) * lp["lam_k2"].astype(jnp.float32)))
           + lam_init)
    y, new_ctx = mixers(x * (1.0 + sc1) + sh1, lp, lam, lam_init, ctx_cache)
    x = layer_norm(DEEPNORM_ALPHA * x + g1 * y, lp["ln1_g"], lp["ln1_b"])
    hf = x * (1.0 + sc2) + sh2
    f = (jax.nn.silu(hf @ lp["w_ff1"]) * (hf @ lp["w_ff3"])) @ lp["w_ff2"]
    x = layer_norm(DEEPNORM_ALPHA * x + g2 * f, lp["ln2_g"], lp["ln2_b"])
    return x, new_ctx


def setup_inputs(seed: int = 0) -> dict:
    key = jax.random.key(seed)
    ks = jax.random.split(key, 32)
    f32 = jnp.float32
    L = DEPTH
    D = D_MODEL

    def nrm(k, shape, s):
        return jax.random.normal(k, shape, f32) * s

    return {
        "x_prompt": nrm(ks[0], (BATCH, SEQ, D), 1.0),
        "x_sample": nrm(ks[1], (DEC_BATCH, DEC_SEQ, D), 1.0),
        "cache_diff_k": nrm(ks[2], (DEC_BATCH, L, DIFF_HEADS, 2, PAST_LEN, DIFF_DK), 1.0),
        "cache_diff_v": nrm(ks[3], (DEC_BATCH, L, DIFF_HEADS, PAST_LEN, DIFF_DV), 1.0),
        "cache_mla_ckv": nrm(ks[4], (DEC_BATCH, L, PAST_LEN, MLA_KV_RANK), 1.0),
        "cache_mla_kpe": nrm(ks[5], (DEC_BATCH, L, PAST_LEN, MLA_ROPE), 1.0),
        "c": nrm(ks[6], (DEC_BATCH, D), 1.0),
        "c_ctx": nrm(ks[7], (D,), 1.0),
        "w_ada": nrm(ks[8], (L, D, 6 * D), D ** -0.5),
        "b_ada": nrm(ks[9], (L, 6 * D), 0.02),
        "w_in": nrm(ks[10], (L, D, IN_COLS), D ** -0.5),
        "conv_w": nrm(ks[11], (L, CONV_K, CONV_W), CONV_K ** -0.5),
        "lam_q1": nrm(ks[12], (L, DIFF_DK), 0.1),
        "lam_k1": nrm(ks[13], (L, DIFF_DK), 0.1),
        "lam_q2": nrm(ks[14], (L, DIFF_DK), 0.1),
        "lam_k2": nrm(ks[15], (L, DIFF_DK), 0.1),
        "diff_norm_w": 1.0 + nrm(ks[16], (L, DIFF_DV), 0.02),
        "q_norm_w": 1.0 + nrm(ks[17], (L, MLA_Q_RANK), 0.02),
        "w_uq": nrm(ks[18], (L, MLA_Q_RANK, MLA_HEADS * MLA_QK), MLA_Q_RANK ** -0.5),
        "kv_norm_w": 1.0 + nrm(ks[19], (L, MLA_KV_RANK), 0.02),
        "w_ukv": nrm(ks[20], (L, MLA_KV_RANK, MLA_HEADS * (MLA_NOPE + MLA_V)), MLA_KV_RANK ** -0.5),
        "w_out": nrm(ks[21], (L, MIX_WIDTH, D), MIX_WIDTH ** -0.5 * DEEPNORM_BETA),
        "ln1_g": 1.0 + nrm(ks[22], (L, D), 0.02),
        "ln1_b": nrm(ks[23], (L, D), 0.02),
        "w_ff1": nrm(ks[24], (L, D, D_FF), D ** -0.5),
        "w_ff3": nrm(ks[25], (L, D, D_FF), D ** -0.5),
        "w_ff2": nrm(ks[26], (L, D_FF, D), D_FF ** -0.5 * DEEPNORM_BETA),
        "ln2_g": 1.0 + nrm(ks[27], (L, D), 0.02),
        "ln2_b": nrm(ks[28], (L, D), 0.02),
    }


def reference(x_prompt, x_sample, cache_diff_k, cache_diff_v, cache_mla_ckv, cache_mla_kpe, c, c_ctx,
              w_ada, b_ada, w_in, conv_w, lam_q1, lam_k1, lam_q2, lam_k2, diff_norm_w, q_norm_w, w_uq,
              kv_norm_w, w_ukv, w_out, ln1_g, ln1_b, w_ff1, w_ff3, w_ff2, ln2_g, ln2_b):
    xp = x_prompt
    xs = x_sample
    st_k, st_v, st_ckv, st_kpe = [], [], [], []
    for l in range(DEPTH):
        lp = {
            "w_ada": w_ada[l], "b_ada": b_ada[l], "w_in": w_in[l], "conv_w": conv_w[l],
            "lam_q1": lam_q1[l], "lam_k1": lam_k1[l], "lam_q2": lam_q2[l], "lam_k2": lam_k2[l],
            "diff_norm_w": diff_norm_w[l], "q_norm_w": q_norm_w[l], "w_uq": w_uq[l],
            "kv_norm_w": kv_norm_w[l], "w_ukv": w_ukv[l], "w_out": w_out[l],
            "ln1_g": ln1_g[l], "ln1_b": ln1_b[l], "w_ff1": w_ff1[l], "w_ff3": w_ff3[l],
            "w_ff2": w_ff2[l], "ln2_g": ln2_g[l], "ln2_b": ln2_b[l],
        }
        xp, ctx = layer(xp, c_ctx, lp, l, None)
        st_k.append(ctx[0])
        st_v.append(ctx[1])
        st_ckv.append(ctx[2])
        st_kpe.append(ctx[3])
        cache_l = (cache_diff_k[:, l], cache_diff_v[:, l], cache_mla_ckv[:, l], cache_mla_kpe[:, l])
        xs, _ = layer(xs, c, lp, l, cache_l)
    state_diff_k = jnp.stack(st_k, axis=1)
    state_diff_v = jnp.stack(st_v, axis=1)
    state_mla_ckv = jnp.stack(st_ckv, axis=1)
    state_mla_kpe = jnp.stack(st_kpe, axis=1)
    return (xp, xs, state_diff_k, state_diff_v, state_mla_ckv, state_mla_kpe)
```

```python
import math
from contextlib import ExitStack

import numpy as np
import concourse.bass as bass
import concourse.mybir as mybir
from concourse.bass_utils import run_bass_kernel_spmd

F32 = mybir.dt.float32
BF16 = mybir.dt.bfloat16
AF = mybir.ActivationFunctionType
ALU = mybir.AluOpType
AX = mybir.AxisListType

D = 1024
L = 2
NT = 512
DFF = 2816
NJ = DFF // 128
ALPHA = (2 * L) ** 0.25
DIFF_SCALE = 32 ** -0.5
MLA_SCALE = 96 ** -0.5
SLABW = 3588
S_ORDER_A = [8, 9, 18, 19, 13, 14, 15, 20, 0, 1, 4, 5]
S_ORDER_B = [2, 3, 6, 7, 16, 17, 10, 11, 12]
DEBUG = False
STOP = 0
SKIP1 = 0
NKEY_S = 2560

VPL = 140
V_CONV, V_QNW, V_KVW, V_LN1G, V_LN1B, V_LN2G, V_LN2B, V_DNW, V_BADA = 0, 6, 9, 11, 19, 27, 35, 43, 44
V_MASK = 2 * VPL
V_HALO = V_MASK + 4
NV = V_HALO + 8

def _wtiles():
    tiles = []
    for l in range(L):
        for t in range(12):
            tiles.append((('wada', l, t), 4096))
    for l in range(L):
        for t in range(6):
            tiles.append((('wins', l, t), 4096))
        tiles.append((('wv', l), 2048))
        tiles.append((('wuq', l), 2304))
        tiles.append((('wuqsw', l), 2304))
        tiles.append((('wukv', l), 2048))
        for t in range(4):
            tiles.append((('wout', l, t), 3584))
        for t in range(11):
            tiles.append((('ffa', l, t), 4096))
        for t in range(8):
            tiles.append((('ff2', l, t), 2816))
    offs = {}
    o = 0
    for k, n in tiles:
        offs[k] = (o, n)
        o += n
    return tiles, offs, o


WT_TILES, WT_OFFS, WT_TOTAL = _wtiles()


def _consumption_order():
    order = []
    for t in range(4):
        order.append(('wada', 0, t))
    for l in range(L):
        for t in range(6):
            order.append(('wins', l, t))
        order += [('wv', l), ('wuq', l), ('wukv', l)]
        for t in range(4, 12):
            order.append(('wada', l, t))
        for t in range(3):
            order.append(('wins', l, t))
        order.append(('wv', l))
        for t in range(3, 6):
            order.append(('wins', l, t))
        order += [('wuq', l), ('wuqsw', l)]
        for t in range(4):
            order.append(('wout', l, t))
        order.append(('wukv', l))
        for t in range(4):
            order.append(('wout', l, t))
        for t in range(11):
            order.append(('ffa', l, t))
        if l + 1 < L:
            for t in range(4):
                order.append(('wada', l + 1, t))
        for g in range(2):
            for t in range(8):
                order.append(('ff2', l, t))
    return order


class Res:
    __slots__ = ('name', 'lw', 'rd', 'ov', 'lo', 'hi', 'excl')

    def __init__(self, name, lo=0, hi=0, excl=False):
        self.name = name
        self.excl = excl
        self.lw = None
        self.rd = {}
        self.ov = [self]
        self.lo = lo
        self.hi = hi


class Queue:
    def __init__(self, name, sems):
        self.name = name
        self.sems = sems
        self.epoch = 0
        self.count = 0
        self.ops = []
        self.waited = {}


EPOCH_LIMIT = 12000


class Sched:
    def __init__(self, nc, es):
        self.nc = nc
        self.semh = {}
        self.q = {}
        for qn in ('pe', 'act', 'dve', 'pool', 'sp'):
            sems = []
            for e in range(4):
                key = ('q', qn, e)
                self.semh[key] = es.enter_context(nc.semaphore(f"s_{qn}_{e}"))
                sems.append(key)
            self.q[qn] = Queue(qn, sems)
        self.ndma = 24
        self.dma_keys = []
        self.dma_val = []
        for i in range(self.ndma):
            key = ('d', i)
            self.semh[key] = es.enter_context(nc.semaphore(f"s_dma_{i}"))
            self.dma_keys.append(key)
            self.dma_val.append(0)
        self.dma_pools = {'sp': [0, list(range(0, 16))], 'pool': [0, list(range(16, 24))]}
        self.cc_key = ('cc',)
        self.semh[self.cc_key] = es.enter_context(nc.semaphore("s_cc"))
        self.cc_val = 0

    def _wait(self, q, key, val):
        if q.waited.get(key, 0) >= val:
            return
        if key[0] == 'q':
            pq = self.q[key[1]]
            if pq.sems[pq.epoch] == key and val > pq.count:
                raise RuntimeError(f"forward wait: {q.name} waits {key} >= {val} but only {pq.count} signalled")
        q.waited[key] = val
        h = self.semh[key]
        q.ops.append(lambda e, h=h, v=val: e.wait_ge(h, v))

    def _deps(self, q, reads, writes):
        deps = set()
        for r in reads:
            for o in r.ov:
                if o.lw is not None:
                    deps.add(o.lw)
                if o.excl:
                    for qn_, x in o.rd.items():
                        if qn_ != q.name:
                            deps.add(x)
        for w in writes:
            for o in w.ov:
                if o.lw is not None:
                    deps.add(o.lw)
                for x in o.rd.values():
                    deps.add(x)
        return deps

    def _record(self, myid, qname, reads, writes):
        for r in reads:
            r.rd[qname] = myid
        for w in writes:
            w.lw = myid
            w.rd = {}

    def emit(self, qn, fn, reads=(), writes=(), signal=True, skip_self=False):
        q = self.q[qn]
        for (key, val) in self._deps(q, reads, writes):
            if skip_self and key[0] == 'q' and key[1] == qn:
                continue
            self._wait(q, key, val)
        if q.count >= EPOCH_LIMIT:
            q.epoch += 1
            q.count = 0
        key = q.sems[q.epoch]
        myid = (key, q.count + 1)
        if signal:
            q.count += 1
            h = self.semh[key]
            q.ops.append(lambda e, fn=fn, h=h: fn(e).then_inc(h, 1))
        else:
            q.ops.append(lambda e, fn=fn: fn(e))
        self._record(myid, qn, reads, writes)

    def dma(self, qn, out, in_, reads=(), writes=()):
        q = self.q[qn]
        for (key, val) in self._deps(q, reads, writes):
            self._wait(q, key, val)
        st = self.dma_pools[qn]
        i = st[1][st[0] % len(st[1])]
        st[0] += 1
        key = self.dma_keys[i]
        if self.dma_val[i] > 0:
            self._wait(q, key, self.dma_val[i])
        self.dma_val[i] += 16
        myid = (key, self.dma_val[i])
        h = self.semh[key]
        q.ops.append(lambda e, o=out, i_=in_, h=h: e.dma_start(out=o, in_=i_).then_inc(h, 16))
        self._record(myid, 'dma%d' % i, reads, writes)
        return myid

    def collective(self, fn, reads=(), writes=()):
        q = self.q['pool']
        for (key, val) in self._deps(q, reads, writes):
            self._wait(q, key, val)
        self.cc_val += 1
        myid = (self.cc_key, self.cc_val)
        h = self.semh[self.cc_key]
        q.ops.append(lambda e, fn=fn, h=h: fn(e).then_inc(h))
        self._record(myid, 'cc', reads, writes)

    def final_wait(self, qn, res_list):
        q = self.q[qn]
        for r in res_list:
            for o in r.ov:
                if o.lw is not None:
                    self._wait(q, o.lw[0], o.lw[1])


class Tile:
    def __init__(self, ap, nres, name, lo=None, nbytes=None, arena=None, ranges=None):
        self.ap = ap
        self.name = name
        self.res = []
        if ranges is not None:
            for i, (a, b) in enumerate(ranges):
                self.res.append(Res(f"{name}.{i}", lo + a, lo + b))
            nres = 0
        for i in range(nres):
            if lo is None:
                self.res.append(Res(f"{name}.{i}"))
            else:
                sz = nbytes // nres
                self.res.append(Res(f"{name}.{i}", lo + i * sz, lo + (i + 1) * sz))
        if arena is not None:
            for r in self.res:
                for o in arena:
                    if o.lo < r.hi and r.lo < o.hi:
                        o.ov.append(r)
                        r.ov.append(o)
                arena.append(r)

    def r(self, i=0):
        return self.res[i]

    def all(self):
        return list(self.res)


DT_SIZE = {F32: 4, BF16: 2}


def build_program():
    nc = bass.Bass("TRN2", target_bir_lowering=False)
    es = ExitStack()
    S = Sched(nc, es)

    def din(name, shape, dt=F32):
        return nc.dram_tensor(name, list(shape), dt, kind="ExternalInput").ap()

    def dout(name, shape, dt=F32):
        return nc.dram_tensor(name, list(shape), dt, kind="ExternalOutput").ap()

    d_xp = din("xp", [128, 8, NT])
    d_xs = din("xs", [128, 8, NT])
    d_cond = din("cond", [128, 8, 2])
    d_vec = din("vec", [128, NV])
    d_rope = din("rope", [128, 2, NT])
    d_lamv = din("lamv", [64, L * 4 * 32])
    d_ws = din("ws", [128, WT_TOTAL])
    d_ckT = din("ckT", [L, 128, 2, 512])
    d_cv = din("cv", [L, 128, 4, 256])
    d_cckvT = din("cckvT", [L, 128, 2, 512])
    d_ckpeT = din("ckpeT", [L, 32, 512])

    o_yp = dout("yp", [128, 8, NT])
    o_ys = dout("ys", [128, 8, NT])
    o_sk = dout("sk", [L, 128, 2, NT])
    o_sv = dout("sv", [L, 128, 4, 256])
    o_sckv = dout("sckv", [L, 128, 2, NT])
    o_skpe = dout("skpe", [L, 32, NT])
    out_res = []
    o_dbg = dout("dbg", [8, 128, 8, NT]) if DEBUG else None
    dbg_n = [0]

    o_dbgb = dout("dbgb", [12, 128, 6144], BF16) if DEBUG else None
    dbgb_n = [0]

    def tapb(ap2d, parts, n, reads):
        if not DEBUG:
            return
        r = Res("out")
        S.dma('sp', o_dbgb[dbgb_n[0], 0:parts, 0:n], ap2d, reads=reads, writes=[r])
        out_res.append(r)
        dbgb_n[0] += 1

    def tap(tile):
        if not DEBUG:
            return
        r = Res("out")
        S.dma('sp', o_dbg[dbg_n[0]], tile.ap, reads=tile.all(), writes=[r])
        out_res.append(r)
        dbg_n[0] += 1

    slab_in = [nc.dram_tensor(f"slab_in{l}", [128, SLABW], BF16) for l in range(L)]
    slab_all = [nc.dram_tensor(f"slab_all{l}", [512, SLABW], BF16) for l in range(L)]
    slab_in_res = [Res(f"slab_in{l}") for l in range(L)]
    slab_all_res = [Res(f"slab_all{l}") for l in range(L)]

    def sb(name, shape, dt):
        return es.enter_context(nc.sbuf_tensor(name, list(shape), dt))

    def T(name, shape, dt, nres=1):
        return Tile(sb(name, shape, dt)[:], nres, name)

    X = [T("XP", [128, 8, NT], F32, 8), T("XS", [128, 8, NT], F32, 8)]
    HT0 = T("HT0", [128, 8, NT], BF16, 8)
    NRING = 4
    RING = [T(f"RING{i}", [128, 4096], BF16, 1) for i in range(NRING)]
    QM = T("QM", [128, 2, 4, NT], BF16, 8)
    D0 = T("D0", [64, NT], F32, 1)
    QMH = T("QMH", [96, 8, NT], BF16, 8)
    YA = T("YA", [128, 2, NT], BF16, 2)
    YH = T("YH", [64, 12, NT], BF16, 12)
    ROPE = T("ROPE", [128, 2, NT], F32, 1)
    VEC = T("VEC", [128, NV], F32, 1)
    CND = T("CND", [128, 8, 2], F32, 1)
    SCB = T("SCB", [128, 8, 2], BF16, 1)
    MOD = T("MOD", [128, L * 6, 8, 2], F32, L * 6)
    LAMV = T("LAMV", [64, L * 4 * 32], F32, 1)
    LAMT = T("LAMT", [64, 2 * L * 32], F32, 1)
    LAMS = T("LAMS", [64, 2 * L], F32, 1)
    LAME = T("LAME", [64, 2 * L], F32, 1)
    NEGLAM = T("NEGLAM", [64, L], F32, 1)
    DNW = T("DNW", [64, L], F32, 1)
    ONES = T("ONES", [128, 128], BF16, 1)
    ET = [T(f"ET{i}", [128, NT], BF16, 1) for i in range(4)]
    TF = [T(f"TF{i}", [128, NT], F32, 1) for i in range(5)]
    TB = [T(f"TB{i}", [128, NT], BF16, 1) for i in range(4)]
    AB = T("AB", [128, 2, NT], F32, 2)
    UP = T("UP", [128, 2, 2, 258], F32, 2)
    US = T("US", [128, 2, 1, 514], F32, 2)
    UBALL = T("UBALL", [128, 4, 4], BF16, 1)
    UBF = T("UBF", [128, 4, 4], F32, 1)

    REG_BYTES = 67072
    reg = sb("REG", [128, REG_BYTES // 4], F32)
    arena = []

    def RT(name, off, shape, dt, nres=1, parts=128, ranges=None):
        n = 1
        for s_ in shape:
            n *= s_
        nbytes = n * DT_SIZE[dt]
        assert off % 4 == 0 and nbytes % 4 == 0 and off + nbytes <= REG_BYTES, (name, off, nbytes)
        v = reg[0:parts, off // 4:(off + nbytes) // 4]
        if dt == BF16:
            v = v.bitcast(BF16)
        if len(shape) == 2:
            v = v.rearrange("p (a b) -> p a b", a=shape[0])
        elif len(shape) == 3:
            v = v.rearrange("p (a b c) -> p a b c", a=shape[0], b=shape[1])
        return Tile(v, nres, name, lo=off, nbytes=nbytes, arena=arena, ranges=ranges), off + nbytes

    o = 0
    AXT, o = RT("AXT", o, [2, NT], F32, 2)
    CQ, o = RT("CQ", o, [3, NT], F32, 3)
    CKV, o = RT("CKV", o, [2, NT], F32, 2)
    CQN, o = RT("CQN", o, [3, NT], BF16, 3)
    QR, o = RT("QR", o, [2, NT], F32, 2)
    KR, o = RT("KR", o, [2, NT], F32, 2)
    KPR, o = RT("KPR", o, [NT], F32, 1)
    _sec = [(0, 512), (512, 1024), (1024, 1536), (1536, 2048), (2048, 2304), (2304, 2560), (2560, 2816),
            (2816, 3072), (3072, 3584), (3584, 3588)]
    SLAB, o = RT("SLAB", o, [SLABW], BF16, 1, ranges=[(2 * a, 2 * b) for a, b in _sec])
    SL_K, SL_CKV, SL_V, SL_KPE, SL_UB = 0, 2, 4, 8, 9
    KMP, o = RT("KMP", o, [8, NT], BF16, 8, parts=96)
    VMP, o = RT("VMP", o, [4, 8, 65], BF16, 4)
    VAP, o = RT("VAP", o, [4, 4, 65], BF16, 4)
    KTP, o = RT("KTP", o, [2, NT], BF16, 2)
    CKVNP, o = RT("CKVNP", o, [2, NT], BF16, 2)
    assert o <= REG_BYTES, o
    D1, o = RT("D1", o, [NT], F32, 1, parts=64)
    RS1, o = RT("RS1", o, [NT], F32, 1, parts=64)
    RL1, o = RT("RL1", o, [NT], F32, 1, parts=64)
    assert o <= REG_BYTES, o
    o = 0
    KTA, o = RT("KTA", o, [2, NKEY_S], BF16, 10)
    VAA, o = RT("VAA", o, [20, 4, 65], BF16, 20)
    CKVTA, o = RT("CKVTA", o, [2, NKEY_S], BF16, 10)
    KMH0, o = RT("KMH0", o, [NKEY_S], BF16, 1, parts=96)
    KMH1, o = RT("KMH1", o, [NKEY_S], BF16, 1, parts=96)
    KMH = [KMH0, KMH1]
    VMA, o = RT("VMA", o, [20, 8, 65], BF16, 20)
    KPEA, o = RT("KPEA", o, [NKEY_S], BF16, 5, parts=96)
    assert o <= REG_BYTES, o
    o = 0
    G0, o = RT("G0", o, [NJ, NT], BF16, NJ)
    G1, o = RT("G1", o, [NJ, NT], BF16, NJ)
    HT1, o = RT("HT1", o, [8, NT], BF16, 8)
    assert o <= REG_BYTES, o
    GT = [G0, G1]
    HT = [HT0, HT1]

    psum = es.enter_context(nc.psum_tensor("PS", [128, 8, 512], F32))
    PSR = [Res(f"bank{i}", excl=True) for i in range(8)]
    bank_rr = {'G': [0, list(range(8))], 'S': [0, [0, 1, 2]], 'PV': [0, [3, 4, 5]], 'M': [0, [6, 7]],
               'S4': [0, [0, 1, 2, 3]], 'PV2': [0, [4, 5]]}

    def bank(pool='G'):
        st = bank_rr[pool]
        b = st[1][st[0] % len(st[1])]
        st[0] += 1
        return b

    tf_rr = [0]

    def tf():
        t = TF[tf_rr[0] % len(TF)]
        tf_rr[0] += 1
        return t

    tb_rr = [0]

    def tb():
        t = TB[tb_rr[0] % len(TB)]
        tb_rr[0] += 1
        return t

    et_rr = [0]

    def et():
        t = ET[et_rr[0] % len(ET)]
        et_rr[0] += 1
        return t

    def mm(b, out_ap, lhsT, rhs, reads, start, stop, sig=False):
        S.emit('pe', lambda e: e.matmul(out_ap, lhsT=lhsT, rhs=rhs, start=start, stop=stop),
               reads=reads, writes=[PSR[b]], signal=(stop or sig), skip_self=True)

    def act(out, in_, func, reads, writes, scale=1.0, bias=0.0):
        S.emit('act', lambda e: e.activation(out=out, in_=in_, func=func, bias=bias, scale=scale),
               reads=reads, writes=writes)

    def tt(eng, out, in0, in1, op, reads, writes):
        S.emit(eng, lambda e: e.tensor_tensor(out=out, in0=in0, in1=in1, op=op), reads=reads, writes=writes)

    def ts(eng, out, in0, s1, s2, op0, op1, reads, writes):
        if op1 is None:
            S.emit(eng, lambda e: e.tensor_scalar(out=out, in0=in0, scalar1=s1, scalar2=None, op0=op0),
                   reads=reads, writes=writes)
        else:
            S.emit(eng, lambda e: e.tensor_scalar(out=out, in0=in0, scalar1=s1, scalar2=s2, op0=op0, op1=op1),
                   reads=reads, writes=writes)

    def stt(out, in0, scalar, in1, op0, op1, reads, writes):
        S.emit('dve', lambda e: e.scalar_tensor_tensor(out=out, in0=in0, scalar=scalar, in1=in1, op0=op0, op1=op1),
               reads=reads, writes=writes)

    def cp(eng, out, in_, reads, writes):
        if eng == 'act':
            S.emit('act', lambda e: e.copy(out=out, in_=in_), reads=reads, writes=writes)
        else:
            S.emit(eng, lambda e: e.tensor_copy(out=out, in_=in_), reads=reads, writes=writes)

    def memset(eng, ap, val, writes):
        S.emit(eng, lambda e: e.memset(ap, val), writes=writes)

    vres = VEC.all()

    def vcol(i, parts=128, p0=0):
        return VEC.ap[p0:p0 + parts, i:i + 1]

    class WView:
        def __init__(self, tile, idx):
            self.tile = tile
            self.idx = idx
            self.ap = tile.ap

        def all(self):
            assert self.tile.cur == self.idx, ("weight ring slot recycled while in use", self.idx, self.tile.cur)
            return self.tile.all()

    order = _consumption_order()
    wstate = {'next_emit': 0, 'next_get': 0}

    def w_emit_upto(n):
        while wstate['next_emit'] < min(n, len(order)):
            i = wstate['next_emit']
            key = order[i]
            off, ne = WT_OFFS[key]
            rb = RING[i % NRING]
            if ne > 2048:
                half = ne // 2
                src = d_ws[:, off:off + ne].rearrange("p (a b) -> p a b", b=half)
                dst = rb.ap[:, 0:ne].rearrange("p (a b) -> p a b", b=half)
            else:
                src = d_ws[:, off:off + ne]
                dst = rb.ap[:, 0:ne]
            S.dma('pool', dst, src, reads=[], writes=rb.all())
            rb.cur = i
            wstate['next_emit'] += 1

    def wget(expect):
        i = wstate['next_get']
        assert order[i][0] == expect, (order[i], expect)
        w_emit_upto(i + NRING - 1)
        wstate['next_get'] += 1
        return WView(RING[i % NRING], i)

    S.dma('sp', X[0].ap, d_xp, writes=X[0].all())
    S.dma('sp', X[1].ap, d_xs, writes=X[1].all())
    S.dma('sp', CND.ap, d_cond, writes=CND.all())
    S.dma('sp', VEC.ap, d_vec, writes=VEC.all())
    S.dma('sp', ROPE.ap, d_rope, writes=ROPE.all())
    S.dma('sp', LAMV.ap, d_lamv, writes=LAMV.all())
    w_emit_upto(3)

    memset('dve', ONES.ap, 1.0, ONES.all())
    nreg = REG_BYTES // 4
    for a0 in range(0, nreg, 2096):
        memset('dve', reg[:, a0:min(nreg, a0 + 2096)], 0.0, list(arena))
    memset('dve', UP.ap, 0.0, UP.all())
    memset('dve', US.ap, 0.0, US.all())
    COS = ROPE.ap[:, 0, :]
    SIN = ROPE.ap[:, 1, :]

    act(SCB.ap, CND.ap, AF.Silu, CND.all(), SCB.all())
    def mod_piece(l, t0, t1, pool='G'):
        b = bank(pool)
        for t in range(t0, t1):
            W = wget('wada')
            wv = W.ap[:, 0:4096].rearrange("p (j k c) -> p j k c", j=4, k=8)
            for ji in range(4):
                j = t * 4 + ji
                for k in range(8):
                    mm(b, psum[:, b, 2 * j:2 * j + 2], wv[:, ji, k, :], SCB.ap[:, k, :],
                       W.all() + SCB.all(), start=(k == 0), stop=(k == 7))
        for v in range(t0 // 2, t1 // 2):
            mr = MOD.r(l * 6 + v)
            tt('dve', MOD.ap[:, l * 6 + v].rearrange("p j c -> p (j c)"), psum[:, b, 16 * v:16 * v + 16],
               VEC.ap[:, l * VPL + V_BADA + 16 * v:l * VPL + V_BADA + 16 * v + 16], ALU.add, [PSR[b]] + vres, [mr])
            if v in (1, 4):
                ts('dve', MOD.ap[:, l * 6 + v], MOD.ap[:, l * 6 + v], 1.0, None, ALU.add, None, [mr], [mr])

    mod_piece(0, 0, 4)

    def modv(l, v, ch, g):
        return MOD.ap[:, l * 6 + v, ch, g:g + 1]

    lv = LAMV.ap.rearrange("p (l f d) -> p l f d", l=L, f=4)
    lt = LAMT.ap.rearrange("p (l t d) -> p l t d", l=L, t=2)
    for l in range(L):
        for t_ in range(2):
            tt('dve', lt[:, l, t_, :], lv[:, l, 2 * t_, :], lv[:, l, 2 * t_ + 1, :], ALU.mult,
               LAMV.all(), LAMT.all())
    S.emit('dve', lambda e: e.tensor_reduce(out=LAMS.ap, in_=LAMT.ap.rearrange("p (a d) -> p a d", d=32),
                                            axis=AX.X, op=ALU.add), reads=LAMT.all(), writes=LAMS.all())
    act(LAME.ap, LAMS.ap, AF.Exp, LAMS.all(), LAME.all())
    for l in range(L):
        lam_init = 0.8 - 0.6 * math.exp(-0.3 * l)
        tt('dve', NEGLAM.ap[:, l:l + 1], LAME.ap[:, 2 * l + 1:2 * l + 2], LAME.ap[:, 2 * l:2 * l + 1],
           ALU.subtract, LAME.all(), NEGLAM.all())
        ts('dve', NEGLAM.ap[:, l:l + 1], NEGLAM.ap[:, l:l + 1], -lam_init, None, ALU.add, None,
           NEGLAM.all(), NEGLAM.all())
        ts('dve', DNW.ap[:, l:l + 1], VEC.ap[0:64, l * VPL + V_DNW:l * VPL + V_DNW + 1], 1.0 - lam_init, None,
           ALU.mult, None, vres, DNW.all())

    class _Stop(Exception):
        pass

    def chk(stage):
        if STOP and stage >= STOP - 1e-9:
            raise _Stop()

    def modulate(l, g, which, dst):
        vs, vh = (1, 0) if which == 1 else (4, 3)
        for ch in range(8):
            if ch % 2 == 0:
                ts('dve', dst.ap[:, ch, :], X[g].ap[:, ch, :], modv(l, vs, ch, g), modv(l, vh, ch, g),
                   ALU.mult, ALU.add, [X[g].r(ch), MOD.r(l * 6 + vs), MOD.r(l * 6 + vh)], [dst.r(ch)])
            else:
                act(dst.ap[:, ch, :], X[g].ap[:, ch, :], AF.Identity, [X[g].r(ch), MOD.r(l * 6 + vs), MOD.r(l * 6 + vh)], [dst.r(ch)],
                    scale=modv(l, vs, ch, g), bias=modv(l, vh, ch, g))

    def rstd_from_sumsq(b, parts, n, ncols, eps):
        t1 = tf()
        act(t1.ap[0:parts, 0:ncols], psum[0:parts, b, 0:ncols], AF.Ln, [PSR[b]], t1.all(), scale=1.0 / n, bias=eps)
        t2 = tf()
        act(t2.ap[0:parts, 0:ncols], t1.ap[0:parts, 0:ncols], AF.Exp, t1.all(), t2.all(), scale=-0.5)
        return t2

    def layer_norm(l, g, vg, vb):
        bs, bq = bank('G'), bank('G')
        for ch in range(8):
            zb, zs = tb(), tb()
            cp('act', zb.ap, X[g].ap[:, ch, :], [X[g].r(ch)], zb.all())
            act(zs.ap, X[g].ap[:, ch, :], AF.Square, [X[g].r(ch)], zs.all())
            mm(bs, psum[:, bs, :], ONES.ap, zb.ap, ONES.all() + zb.all(), start=(ch == 0), stop=(ch == 7), sig=True)
            mm(bq, psum[:, bq, :], ONES.ap, zs.ap, ONES.all() + zs.all(), start=(ch == 0), stop=(ch == 7), sig=True)
        mean = tf()
        act(mean.ap, psum[:, bs, :], AF.Copy, [PSR[bs]], mean.all(), scale=1.0 / D)
        msq = tf()
        tt('dve', msq.ap, mean.ap, mean.ap, ALU.mult, mean.all(), msq.all())
        var = tf()
        stt(var.ap, psum[:, bq, :], 1.0 / D, msq.ap, ALU.mult, ALU.subtract, [PSR[bq]] + msq.all(), var.all())
        lnv = tf()
        act(lnv.ap, var.ap, AF.Ln, var.all(), lnv.all(), bias=1e-5)
        rstd = tf()
        act(rstd.ap, lnv.ap, AF.Exp, lnv.all(), rstd.all(), scale=-0.5)
        for ch in range(8):
            xr = X[g].r(ch)
            xa = X[g].ap[:, ch, :]
            tt('dve', xa, xa, mean.ap, ALU.subtract, [xr] + mean.all(), [xr])
            tt('dve', xa, xa, rstd.ap, ALU.mult, [xr] + rstd.all(), [xr])
            act(xa, xa, AF.Identity, [xr] + vres, [xr], scale=vcol(l * VPL + vg + ch), bias=vcol(l * VPL + vb + ch))

    def stage_out(dst_dram, src_ap, reads, parts=128, ncols=NT):
        st = tf()
        cp('dve', st.ap[0:parts, 0:ncols], src_ap, reads, st.all())
        r = Res("out")
        S.dma('sp', dst_dram, st.ap[0:parts, 0:ncols], reads=st.all(), writes=[r])
        out_res.append(r)

    def rope_finish(dst_ap, part_ap, psw_ap, reads, writes, p0, p1):
        t1 = tf()
        tt('dve', t1.ap[p0:p1], psw_ap, SIN[p0:p1], ALU.mult, reads + ROPE.all(), t1.all())
        tt('dve', dst_ap, t1.ap[p0:p1], part_ap, ALU.add, t1.all() + reads, writes)

    premod = set()

    def mixer_project(l, g, mid_hook=None):
        H = HT[0]
        if (l, g) not in premod:
            modulate(l, g, 1, H)
        isS = (g == 1)
        U = US if isS else UP
        st_ = {'statq': None, 'statk': None}
        corder = S_ORDER_A + S_ORDER_B
        for pos, c in enumerate(corder):
            if isS and pos == len(S_ORDER_A):
                v_token_major(l, g, H)
                mid_hook()
            if pos % 4 == 0:
                W = wget('wins')
                wv = W.ap[:, 0:4096].rearrange("p (j k c) -> p j k c", j=4, k=8)
            if c >= 16 and not isS:
                continue
            b = bank('G')
            pb = psum[:, b, :]
            for k in range(8):
                mm(b, pb, wv[:, pos % 4, k, :], H.ap[:, k, :], W.all() + [H.r(k)], start=(k == 0), stop=(k == 7))
            R = [PSR[b]]
            statq, statk = st_['statq'], st_['statk']
            if c in (0, 1):
                cp('act', AXT.ap[:, c, :], pb, R, [AXT.r(c)])
            elif c in (2, 3):
                cp('act', AB.ap[:, c - 2, :], pb, R, [AB.r(c - 2)])
            elif c in (4, 5):
                cc = c - 4
                uin = U.ap[:, cc, :, 1:1 + (NT if isS else 256)]
                tt('dve', uin, pb.rearrange("p (s t) -> p s t", s=(1 if isS else 2)),
                   AXT.ap[:, cc, :].rearrange("p (s t) -> p s t", s=(1 if isS else 2)), ALU.mult,
                   R + [AXT.r(cc)], [U.r(cc)])
            elif c in (6, 7):
                cc = c - 6
                if isS:
                    tt('dve', QR.ap[:, cc, :], pb, COS, ALU.mult, R + ROPE.all(), [QR.r(cc)])
                else:
                    for j in range(4):
                        act(QM.ap[:, cc, j, :], pb, AF.Identity, R + vres, [QM.r(cc * 4 + j)], scale=vcol(V_MASK + j))
            elif c in (8, 9):
                cc = c - 8
                if isS:
                    tt('dve', KR.ap[:, cc, :], pb, COS, ALU.mult, R + ROPE.all(), [KR.r(cc)])
                else:
                    cp('act', KTP.ap[:, cc, :], pb, R, [KTP.r(cc)])
                    if not SKIP1:
                        stage_out(o_sk[l, :, cc, :], pb, R)
            elif c in (10, 11, 12):
                cc = c - 10
                cp('act', CQ.ap[:, cc, :], pb, R, [CQ.r(cc)])
                sq = tb()
                act(sq.ap, pb, AF.Square, R, sq.all())
                if cc == 0:
                    statq = st_['statq'] = bank('G')
                mm(statq, psum[:, statq, :], ONES.ap, sq.ap, ONES.all() + sq.all(), start=(cc == 0), stop=(cc == 2), sig=True)
                if cc == 2:
                    rs = rstd_from_sumsq(statq, 128, 384.0, NT, 1e-6)
                    for c3 in range(3):
                        stt(CQN.ap[:, c3, :], CQ.ap[:, c3, :], vcol(l * VPL + V_QNW + c3), rs.ap, ALU.mult, ALU.mult,
                            [CQ.r(c3)] + vres + rs.all(), [CQN.r(c3)])
            elif c in (13, 14):
                cc = c - 13
                cp('act', CKV.ap[:, cc, :], pb, R, [CKV.r(cc)])
                sq = tb()
                act(sq.ap, pb, AF.Square, R, sq.all())
                if cc == 0:
                    statk = st_['statk'] = bank('G')
                mm(statk, psum[:, statk, :], ONES.ap, sq.ap, ONES.all() + sq.all(), start=(cc == 0), stop=(cc == 1), sig=True)
                if cc == 1:
                    rs = rstd_from_sumsq(statk, 128, 256.0, NT, 1e-6)
                    for c2 in range(2):
                        if isS:
                            stt(SLAB.ap[:, 1024 + c2 * NT:1024 + (c2 + 1) * NT], CKV.ap[:, c2, :],
                                vcol(l * VPL + V_KVW + c2), rs.ap, ALU.mult, ALU.mult,
                                [CKV.r(c2)] + vres + rs.all(), [SLAB.r(SL_CKV + c2)])
                        else:
                            stt(CKV.ap[:, c2, :], CKV.ap[:, c2, :], vcol(l * VPL + V_KVW + c2), rs.ap,
                                ALU.mult, ALU.mult, [CKV.r(c2)] + vres + rs.all(), [CKV.r(c2)])
                            cp('act', CKVNP.ap[:, c2, :], CKV.ap[:, c2, :], [CKV.r(c2)], [CKVNP.r(c2)])
                            r = Res("out")
                            S.dma('sp', o_sckv[l, :, c2, :], CKV.ap[:, c2, :], reads=[CKV.r(c2)], writes=[r])
                            out_res.append(r)
            elif c == 15:
                if isS:
                    tt('dve', KPR.ap[64:96, :], pb[64:96], COS[64:96], ALU.mult, R + ROPE.all(), KPR.all())
                else:
                    for h in range(8):
                        cp('act' if h % 2 else 'dve', KMP.ap[64:96, h, :], pb[64:96], R, [KMP.r(h)])
                    st = tf()
                    cp('dve', st.ap[64:96, :], pb[64:96], R, st.all())
                    r = Res("out")
                    S.dma('sp', o_skpe[l], st.ap[64:96, :], reads=st.all(), writes=[r])
                    out_res.append(r)
            elif c in (16, 17):
                cc = c - 16
                rope_finish(QR.ap[:, cc, :], QR.ap[:, cc, :], pb, R + [QR.r(cc)], [QR.r(cc)], 0, 128)
                for j in range(4):
                    act(QM.ap[:, cc, j, :], QR.ap[:, cc, :], AF.Identity, [QR.r(cc)] + vres, [QM.r(cc * 4 + j)],
                        scale=vcol(V_MASK + j))
            elif c in (18, 19):
                cc = c - 18
                rope_finish(SLAB.ap[:, cc * NT:(cc + 1) * NT], KR.ap[:, cc, :], pb, R + [KR.r(cc)], [SLAB.r(SL_K + cc)], 0, 128)
            elif c == 20:
                rope_finish(SLAB.ap[64:96, 3072:3072 + NT], KPR.ap[64:96, :], pb[64:96], R + KPR.all(), [SLAB.r(SL_KPE)], 64, 96)
            chk((1 if g == 0 else 5) + (c + 1) / 100.0)
        if not isS:
            v_token_major(l, g, H)
        uq_project(l, g)

    def v_token_major(l, g, H):
        isS = (g == 1)
        W = wget('wv')
        wvv = W.ap[:, 0:2048].rearrange("p (k c) -> p k c", k=8)
        for t4 in range(4):
            b = bank('G')
            for k in range(8):
                mm(b, psum[:, b, 0:256], H.ap[:, k, t4 * 128:(t4 + 1) * 128], wvv[:, k, :], W.all() + [H.r(k)],
                   start=(k == 0), stop=(k == 7))
            if isS:
                cp('act', SLAB.ap[:, 2048 + t4 * 256:2048 + (t4 + 1) * 256], psum[:, b, 0:256], [PSR[b]], [SLAB.r(SL_V + t4)])
            else:
                cp('act', VAP.ap[:, t4, :, 0:64], psum[:, b, 0:256].rearrange("p (h d) -> p h d", h=4), [PSR[b]], [VAP.r(t4)])
                st = tf()
                cp('dve', st.ap[:, 0:256], psum[:, b, 0:256], [PSR[b]], st.all())
                r = Res("out")
                S.dma('sp', o_sv[l, :, t4, :], st.ap[:, 0:256], reads=st.all(), writes=[r])
                out_res.append(r)
        if isS:
            for cc in range(2):
                cp('dve', SLAB.ap[:, 3584 + 2 * cc:3584 + 2 * cc + 1], US.ap[:, cc, 0, 1:2], [US.r(cc)], [SLAB.r(SL_UB)])
                cp('dve', SLAB.ap[:, 3584 + 2 * cc + 1:3584 + 2 * cc + 2], US.ap[:, cc, 0, NT:NT + 1], [US.r(cc)], [SLAB.r(SL_UB)])

    def uq_project(l, g):
        isS = (g == 1)
        W = wget('wuq')
        wq = W.ap[:, 0:2304].rearrange("p (k c) -> p k c", k=3)
        if isS:
            W2 = wget('wuqsw')
            wq2 = W2.ap[:, 0:2304].rearrange("p (k c) -> p k c", k=3)
        for h in range(8):
            b = bank('G')
            for k in range(3):
                mm(b, psum[0:96, b, :], wq[:, k, h * 96:(h + 1) * 96], CQN.ap[:, k, :], W.all() + [CQN.r(k)],
                   start=(k == 0), stop=(k == 2))
            if isS:
                b2 = bank('G')
                for k in range(3):
                    mm(b2, psum[0:96, b2, :], wq2[:, k, h * 96:(h + 1) * 96], CQN.ap[:, k, :], W2.all() + [CQN.r(k)],
                       start=(k == 0), stop=(k == 2))
                cp('act', QMH.ap[0:64, h, :], psum[0:64, b, :], [PSR[b]], [QMH.r(h)])
                t0 = tf()
                tt('dve', t0.ap[64:96], psum[64:96, b, :], COS[64:96], ALU.mult, [PSR[b]] + ROPE.all(), t0.all())
                rope_finish(QMH.ap[64:96, h, :], t0.ap[64:96], psum[64:96, b2, :], [PSR[b2]] + t0.all(), [QMH.r(h)], 64, 96)
            else:
                cp('act' if h % 2 else 'dve', QMH.ap[0:96, h, :], psum[0:96, b, :], [PSR[b]], [QMH.r(h)])

    def conv_ya(l, g):
        isS = (g == 1)
        U = US if isS else UP
        ns, sl = (1, NT) if isS else (2, 256)
        for cc in range(2):
            t1 = tf()
            t1v = t1.ap.rearrange("p (s t) -> p s t", s=ns)
            ts('dve', t1v, U.ap[:, cc, :, 0:sl], vcol(l * VPL + V_CONV + 0 * 2 + cc), None, ALU.mult, None,
               [U.r(cc)] + vres, t1.all())
            stt(t1v, U.ap[:, cc, :, 1:1 + sl], vcol(l * VPL + V_CONV + 1 * 2 + cc), t1v, ALU.mult, ALU.add,
                [U.r(cc)] + vres + t1.all(), t1.all())
            stt(t1v, U.ap[:, cc, :, 2:2 + sl], vcol(l * VPL + V_CONV + 2 * 2 + cc), t1v, ALU.mult, ALU.add,
                [U.r(cc)] + vres + t1.all(), t1.all())
            tt('dve', YA.ap[:, cc, :], t1.ap, AB.ap[:, cc, :], ALU.mult, t1.all() + [AB.r(cc)], [YA.r(cc)])

    def attention(l, g, segs, kt_of, kside_d, vside_d, kside_m, vside_m, prep_head=None, hooks=None):
        units = []
        for hm in range(16):
            for (q0, nq, kts) in segs:
                if hm < 8:
                    units.append(dict(d=True, h=hm // 2, m=hm % 2, c=hm // 4, j=hm % 4, q0=q0, nq=nq, kts=kts))
                else:
                    units.append(dict(d=False, h=hm - 8, q0=q0, nq=nq, kts=kts))
        stages = [(ui, i) for ui, u in enumerate(units) for i in range(len(u['kts']))]
        short_units = min(len(u['kts']) for u in units) < 16
        LAG = 2 if short_units else 3
        spool, pvpool = ('S', 'PV') if short_units else ('S4', 'PV2')
        nst = len(stages)
        first_unit_of_head = {}
        for ui, u in enumerate(units):
            if not u['d'] and u['h'] not in first_unit_of_head:
                first_unit_of_head[u['h']] = ui
        prep_at = {}
        if prep_head is not None:
            per_head = len(segs)
            for h, ui in first_unit_of_head.items():
                prep_at.setdefault(max(0, ui - per_head), []).append(h)
        ebuf = {}
        norm_due = {}

        def ph_a(u):
            nq, pv = u['nq'], u['pv']
            rl = tf()
            act(rl.ap[64:65, 0:nq], psum[64:65, pv, 0:nq], AF.Ln, [PSR[pv]], rl.all())
            rr = tb()
            act(rr.ap[64:65, 0:nq], rl.ap[64:65, 0:nq], AF.Exp, rl.all(), rr.all(), scale=-1.0)
            u['rr'] = rr

        def ph_b(u):
            d, h, q0, nq, pv, rr = u['d'], u['h'], u['q0'], u['nq'], u['pv'], u['rr']
            bb = bank('M')
            mm(bb, psum[0:64, bb, 0:nq], ONES.ap[64:65, 0:64], rr.ap[64:65, 0:nq], ONES.all() + rr.all(), True, True)
            rb = tf()
            cp('dve', rb.ap[0:64, 0:nq], psum[0:64, bb, 0:nq], [PSR[bb]], rb.all())
            if not d:
                tt('dve', YH.ap[:, 4 + h, q0:q0 + nq], psum[0:64, pv, 0:nq], rb.ap[0:64, 0:nq], ALU.mult,
                   [PSR[pv]] + rb.all(), [YH.r(4 + h)])
            elif u['m'] == 0:
                tt('dve', D0.ap[0:64, q0:q0 + nq], psum[0:64, pv, 0:nq], rb.ap[0:64, 0:nq], ALU.mult,
                   [PSR[pv]] + rb.all(), D0.all())
            else:
                dt_ = tf()
                tt('dve', dt_.ap[0:64, 0:nq], psum[0:64, pv, 0:nq], rb.ap[0:64, 0:nq], ALU.mult,
                   [PSR[pv]] + rb.all(), dt_.all())
                if short_units:
                    ad, ada = D1, D1.ap[0:64, q0:q0 + nq]
                else:
                    ad = tf()
                    ada = ad.ap[0:64, 0:nq]
                stt(ada, dt_.ap[0:64, 0:nq], NEGLAM.ap[:, l:l + 1], D0.ap[0:64, q0:q0 + nq],
                    ALU.mult, ALU.add, dt_.all() + D0.all() + NEGLAM.all(), ad.all())
                sq = tb()
                tt('dve', sq.ap[0:64, 0:nq], ada, ada, ALU.mult, ad.all(), sq.all())
                u['ad'], u['ada'], u['sq'] = ad, ada, sq

        def ph_c(u):
            if not (u['d'] and u['m'] == 1):
                return
            nq, sq = u['nq'], u['sq']
            bq = bank('M')
            mm(bq, psum[0:64, bq, 0:nq], ONES.ap[0:64, 0:64], sq.ap[0:64, 0:nq], ONES.all() + sq.all(), True, True)
            if short_units:
                q0 = u['q0']
                act(RL1.ap[0:64, q0:q0 + nq], psum[0:64, bq, 0:nq], AF.Ln, [PSR[bq]], RL1.all(), scale=1.0 / 64.0, bias=1e-6)
                act(RS1.ap[0:64, q0:q0 + nq], RL1.ap[0:64, q0:q0 + nq], AF.Exp, RL1.all(), RS1.all(), scale=-0.5)
                u['rs'], u['rsa'] = RS1, RS1.ap[0:64, q0:q0 + nq]
            else:
                u['rs'] = rstd_from_sumsq(bq, 64, 64.0, nq, 1e-6)
                u['rsa'] = u['rs'].ap[0:64, 0:nq]

        def ph_d(u):
            if not (u['d'] and u['m'] == 1):
                return
            h, q0, nq, ad, rs = u['h'], u['q0'], u['nq'], u['ad'], u['rs']
            stt(YH.ap[:, h, q0:q0 + nq], u['ada'], DNW.ap[:, l:l + 1], u['rsa'],
                ALU.mult, ALU.mult, ad.all() + DNW.all() + rs.all(), [YH.r(h)])

        phases = (ph_a, ph_b, ph_c, ph_d)
        delays = (1, 5, 8, 12) if not short_units else (1, 3, 5, 7)
        NDELAY = delays[-1]

        for s in range(nst + LAG + NDELAY + 1):
            if s < nst:
                ui, i = stages[s]
                u = units[ui]
                if i == 0 and ui in prep_at:
                    for h_ in prep_at[ui]:
                        prep_head(h_)
                if i == 0 and hooks and ui in hooks:
                    hooks[ui]()
                q0, nq = u['q0'], u['nq']
                kt = u['kts'][i]
                sb_ = bank(spool)
                so = psum[:, sb_, 0:nq]
                if u['d']:
                    ka, kr = kside_d(u['c'], kt)
                    mm(sb_, so, ka, QM.ap[:, u['c'], u['j'], q0:q0 + nq], [kr, QM.r(u['c'] * 4 + u['j'])], True, True)
                    sc = DIFF_SCALE
                else:
                    ka, kr = kside_m(u['h'], kt)
                    mm(sb_, so, ka, QMH.ap[0:96, u['h'], q0:q0 + nq], [kr, QMH.r(u['h'])], True, True)
                    sc = MLA_SCALE
                e_ = et()
                act(e_.ap[:, 0:nq], so, AF.Exp, [PSR[sb_]], e_.all(), scale=sc)
                ebuf[s] = e_
            sp_ = s - LAG
            if 0 <= sp_ < nst:
                ui, i = stages[sp_]
                u = units[ui]
                nq = u['nq']
                if i == 0:
                    u['pv'] = bank(pvpool)
                kt = u['kts'][i]
                va, vr = vside_d(u['h'], kt) if u['d'] else vside_m(u['h'], kt)
                e_ = ebuf.pop(sp_)
                last = (i == len(u['kts']) - 1)
                mm(u['pv'], psum[0:65, u['pv'], 0:nq], va, e_.ap[:, 0:nq], [vr] + e_.all(), start=(i == 0), stop=last)
                if last:
                    for ph, dl in zip(phases, delays):
                        norm_due.setdefault(s + dl, []).append((ph, u))
            for ph, u in norm_due.pop(s, []):
                ph(u)
        assert not norm_due and not ebuf

    def out_proj_ln1(l, g):
        for t in range(4):
            W = wget('wout')
            for ci in range(2):
                c = 2 * t + ci
                base = ci * 1792
                wya = W.ap[:, base:base + 256].rearrange("p (k c) -> p k c", k=2)
                wh = W.ap[0:64, base + 256:base + 1792].rearrange("p (h c) -> p h c", h=12)
                b = bank('G')
                for k in range(2):
                    mm(b, psum[:, b, :], wya[:, k, :], YA.ap[:, k, :], W.all() + [YA.r(k)], start=(k == 0), stop=False)
                for hh in range(12):
                    mm(b, psum[:, b, :], wh[:, hh, :], YH.ap[:, hh, :], W.all() + [YH.r(hh)], start=False, stop=(hh == 11))
                xr = X[g].r(c)
                xa = X[g].ap[:, c, :]
                ts('dve', xa, xa, ALPHA, None, ALU.mult, None, [xr], [xr])
                stt(xa, psum[:, b, :], modv(l, 2, c, g), xa, ALU.mult, ALU.add, [PSR[b], MOD.r(l * 6 + 2), xr], [xr])
        layer_norm(l, g, V_LN1G, V_LN1B)

    def ffn(l):
        for g in range(2):
            modulate(l, g, 2, HT[g])
        for t in range(11):
            W = wget('ffa')
            wv = W.ap[:, 0:4096].rearrange("p (j w k c) -> p j w k c", j=2, w=2, k=8)
            for ji in range(2):
                j = 2 * t + ji
                for g in range(2):
                    b1, b3 = bank('G'), bank('G')
                    for k in range(8):
                        mm(b1, psum[:, b1, :], wv[:, ji, 0, k, :], HT[g].ap[:, k, :], W.all() + [HT[g].r(k)],
                           start=(k == 0), stop=(k == 7))
                    for k in range(8):
                        mm(b3, psum[:, b3, :], wv[:, ji, 1, k, :], HT[g].ap[:, k, :], W.all() + [HT[g].r(k)],
                           start=(k == 0), stop=(k == 7))
                    sl = tf()
                    act(sl.ap, psum[:, b1, :], AF.Silu, [PSR[b1]], sl.all())
                    tt('dve', GT[g].ap[:, j, :], psum[:, b3, :], sl.ap, ALU.mult, [PSR[b3]] + sl.all(), [GT[g].r(j)])
        if l + 1 < L:
            mod_piece(l + 1, 0, 4)
        for g in range(2):
            for c in range(8):
                W = wget('ff2')
                wv = W.ap[:, 0:2816].rearrange("p (j c) -> p j c", j=NJ)
                b = bank('G')
                for j in range(NJ):
                    mm(b, psum[:, b, :], wv[:, j, :], GT[g].ap[:, j, :], W.all() + [GT[g].r(j)],
                       start=(j == 0), stop=(j == NJ - 1))
                xr = X[g].r(c)
                xa = X[g].ap[:, c, :]
                ts('dve', xa, xa, ALPHA, None, ALU.mult, None, [xr], [xr])
                stt(xa, psum[:, b, :], modv(l, 5, c, g), xa, ALU.mult, ALU.add, [PSR[b], MOD.r(l * 6 + 5), xr], [xr])
            if g == 1 and l + 1 < L:
                modulate(l + 1, 0, 1, HT[0])
                premod.add((l + 1, 0))
            layer_norm(l, g, V_LN2G, V_LN2B)

    def kv_prompt(l):
        memset('dve', VAP.ap[:, :, :, 64:65], 1.0, VAP.all())
        memset('dve', VMP.ap[:, :, :, 64:65], 1.0, VMP.all())
        W = wget('wukv')
        wk = W.ap[:, 0:1024].rearrange("p (k c) -> p k c", k=2)
        wvv = W.ap[:, 1024:2048].rearrange("p (k c) -> p k c", k=2)
        for h in range(8):
            b = bank('G')
            for k in range(2):
                mm(b, psum[0:64, b, :], wk[:, k, h * 64:(h + 1) * 64], CKVNP.ap[:, k, :], W.all() + [CKVNP.r(k)],
                   start=(k == 0), stop=(k == 1))
            cp('act' if h % 2 else 'dve', KMP.ap[0:64, h, :], psum[0:64, b, :], [PSR[b]], [KMP.r(h)])
        for t4 in range(4):
            b = bank('G')
            for k in range(2):
                mm(b, psum[:, b, :], CKVNP.ap[:, k, t4 * 128:(t4 + 1) * 128], wvv[:, k, :], W.all() + [CKVNP.r(k)],
                   start=(k == 0), stop=(k == 1))
            cp('act' if t4 % 2 else 'dve', VMP.ap[:, t4, :, 0:64], psum[:, b, :].rearrange("p (h d) -> p h d", h=8),
               [PSR[b]], [VMP.r(t4)])

    def attn_prompt(l):
        segs = [(0, 256, [0, 1]), (256, 256, [2, 3])]
        attention(l, 0, segs, None,
                  lambda c, kt: (KTP.ap[:, c, kt * 128:(kt + 1) * 128], KTP.r(c)),
                  lambda h, kt: (VAP.ap[:, kt, h, :], VAP.r(kt)),
                  lambda h, kt: (KMP.ap[0:96, h, kt * 128:(kt + 1) * 128], KMP.r(h)),
                  lambda h, kt: (VMP.ap[:, kt, h, :], VMP.r(kt)),
                  hooks={2 + 8 * k: (lambda k=k: mod_piece(l, 4 + 2 * k, 6 + 2 * k, pool='M')) for k in range(4)})

    def exchange_sample(l):
        S.dma('sp', slab_in[l].ap()[:, :], SLAB.ap, reads=SLAB.all(), writes=[slab_in_res[l]])
        S.collective(lambda e: e.collective_compute("AllGather", ALU.bypass,
                                                    replica_groups=[[0, 1, 2, 3], [4, 5, 6, 7]],
                                                    ins=[slab_in[l].ap().opt()], outs=[slab_all[l].ap().opt()]),
                     reads=[slab_in_res[l]], writes=[slab_all_res[l]])

    def exchange_load(l):
        sa = slab_all[l].ap()
        memset('dve', VAA.ap[:, :, :, 64:65], 1.0, VAA.all())
        memset('dve', VMA.ap[:, :, :, 64:65], 1.0, VMA.all())
        for r in range(4):
            rows = sa[r * 128:(r + 1) * 128, :]
            S.dma('sp', KTA.ap[:, :, r * NT:(r + 1) * NT], rows[:, 0:1024].rearrange("p (c t) -> p c t", c=2),
                  reads=[slab_all_res[l]], writes=[KTA.r(r), KTA.r(5 + r)])
            S.dma('sp', CKVTA.ap[:, :, r * NT:(r + 1) * NT], rows[:, 1024:2048].rearrange("p (c t) -> p c t", c=2),
                  reads=[slab_all_res[l]], writes=[CKVTA.r(r), CKVTA.r(5 + r)])
            S.dma('sp', VAA.ap[:, 4 * r:4 * r + 4, :, 0:64],
                  rows[:, 2048:3072].rearrange("p (t h d) -> p t h d", t=4, h=4),
                  reads=[slab_all_res[l]], writes=[VAA.r(4 * r + i) for i in range(4)])
            S.dma('sp', KPEA.ap[64:96, r * NT:(r + 1) * NT], sa[r * 128 + 64:r * 128 + 96, 3072:3072 + NT],
                  reads=[slab_all_res[l]], writes=[KPEA.r(r)])
            S.dma('sp', UBALL.ap[:, r, :], rows[:, 3584:3588], reads=[slab_all_res[l]], writes=UBALL.all())
        S.dma('pool', KTA.ap[:, :, 2048:2560], d_ckT[l], writes=[KTA.r(4), KTA.r(9)])
        S.dma('pool', CKVTA.ap[:, :, 2048:2560], d_cckvT[l], writes=[CKVTA.r(4), CKVTA.r(9)])
        S.dma('pool', VAA.ap[:, 16:20, :, 0:64], d_cv[l].rearrange("p t (h d) -> p t h d", h=4),
              writes=[VAA.r(16 + i) for i in range(4)])
        S.dma('pool', KPEA.ap[64:96, 2048:2560], d_ckpeT[l], writes=[KPEA.r(4)])
        cp('dve', UBF.ap, UBALL.ap, UBALL.all(), UBF.all())
        for cc in range(2):
            for side in range(2):
                dst = US.ap[:, cc, 0, 0:1] if side == 0 else US.ap[:, cc, 0, NT + 1:NT + 2]
                col = 2 * cc + (1 if side == 0 else 0)
                ts('dve', dst, UBF.ap[:, 0, col:col + 1], vcol(V_HALO + 4 * side + 0), None, ALU.mult, None,
                   UBF.all() + vres, [US.r(cc)])
                for r in range(1, 4):
                    stt(dst, UBF.ap[:, r, col:col + 1], vcol(V_HALO + 4 * side + r), dst, ALU.mult, ALU.add,
                        UBF.all() + vres + [US.r(cc)], [US.r(cc)])
        for i in range(2):
            cp('dve' if i else 'act', KMH[i].ap[64:96, :], KPEA.ap[64:96, :], KPEA.all(), KMH[i].all())

    def kv_sample(l):
        W = wget('wukv')
        wk = W.ap[:, 0:1024].rearrange("p (k c) -> p k c", k=2)
        wvv = W.ap[:, 1024:2048].rearrange("p (k c) -> p k c", k=2)
        for kt in range(20):
            b = bank('G')
            for k in range(2):
                mm(b, psum[:, b, :], CKVTA.ap[:, k, kt * 128:(kt + 1) * 128], wvv[:, k, :], W.all() + [CKVTA.r(k * 5 + kt // 4)],
                   start=(k == 0), stop=(k == 1))
            cp('act' if kt % 2 else 'dve', VMA.ap[:, kt, :, 0:64], psum[:, b, :].rearrange("p (h d) -> p h d", h=8),
               [PSR[b]], [VMA.r(kt)])

        def prep_head(h):
            kb_ = KMH[h % 2]
            for kb in range(5):
                b = bank('M')
                for k in range(2):
                    mm(b, psum[0:64, b, :], wk[:, k, h * 64:(h + 1) * 64], CKVTA.ap[:, k, kb * 512:(kb + 1) * 512],
                       W.all() + [CKVTA.r(k * 5 + kb)], start=(k == 0), stop=(k == 1))
                cp('dve', kb_.ap[0:64, kb * 512:(kb + 1) * 512], psum[0:64, b, :], [PSR[b]], kb_.all())
        return prep_head

    def attn_sample(l, prep_head):
        segs = [(0, NT, list(range(20)))]
        attention(l, 1, segs, None,
                  lambda c, kt: (KTA.ap[:, c, kt * 128:(kt + 1) * 128], KTA.r(c * 5 + kt // 4)),
                  lambda h, kt: (VAA.ap[:, kt, h, :], VAA.r(kt)),
                  lambda h, kt: (KMH[h % 2].ap[0:96, kt * 128:(kt + 1) * 128], KMH[h % 2].r(0)),
                  lambda h, kt: (VMA.ap[:, kt, h, :], VMA.r(kt)),
                  prep_head=prep_head)

    try:
      for l in range(L):
        chk(1)
        mixer_project(l, 0)
        chk(2)
        conv_ya(l, 0)
        kv_prompt(l)
        chk(3)
        attn_prompt(l)
        chk(5)
        mixer_project(l, 1, mid_hook=lambda: exchange_sample(l))
        chk(6)
        exchange_load(l)
        out_proj_ln1(l, 0)
        chk(7)
        conv_ya(l, 1)
        ph = kv_sample(l)
        chk(8)
        attn_sample(l, ph)
        chk(9)
        out_proj_ln1(l, 1)
        chk(10)
        ffn(l)
        chk(11)
    except _Stop:
        pass

    for g, dst in ((0, o_yp), (1, o_ys)):
        r = Res("out")
        S.dma('sp', dst, X[g].ap, reads=X[g].all(), writes=[r])
        out_res.append(r)
    S.final_wait('sp', out_res)
    assert DEBUG == 2 or STOP or wstate['next_get'] == len(order), (wstate, len(order))

    with nc.Block() as block:
        @block.tensor
        def _(e):
            for f in S.q['pe'].ops:
                f(e)

        @block.scalar
        def _(e):
            for f in S.q['act'].ops:
                f(e)

        @block.vector
        def _(e):
            for f in S.q['dve'].ops:
                f(e)

        @block.gpsimd
        def _(e):
            for f in S.q['pool'].ops:
                f(e)

        @block.sync
        def _(e):
            for f in S.q['sp'].ops:
                f(e)
    es.close()
    return nc


def _partner32():
    p = np.zeros(32, dtype=np.int64)
    for r in range(32):
        r16 = r % 16
        p[r] = r - r16 + ((r16 + 8) % 16)
    return p


def _fm(w, ncol_chunks=None):
    K, C = w.shape
    return w.reshape(K // 128, 128, C // 128, 128).transpose(1, 2, 0, 3)


def pack_weights(inp):
    ws = np.zeros((128, WT_TOTAL), dtype=np.float32)
    part = _partner32()

    def put(key, arr):
        off, n = WT_OFFS[key]
        a = np.ascontiguousarray(arr, dtype=np.float32).reshape(arr.shape[0], -1)
        assert a.shape[1] == n, (key, a.shape, n)
        ws[:a.shape[0], off:off + n] = a

    for l in range(L):
        wada = inp["w_ada"][l]
        fm = _fm(wada)
        for t in range(12):
            put(('wada', l, t), fm[:, 4 * t:4 * t + 4])
        w_in = inp["w_in"][l]
        a_x, a_b, a_c = w_in[:, 0:256], w_in[:, 256:512], w_in[:, 512:768]
        d_q, d_k, d_v = w_in[:, 768:1024], w_in[:, 1024:1280], w_in[:, 1280:1536]
        m_cq, m_ckv, m_kpe = w_in[:, 1536:1920], w_in[:, 1920:2176], w_in[:, 2176:2208]
        perm256 = np.concatenate([b * 32 + part for b in range(8)])
        kpe_pad = np.zeros((1024, 128), np.float32)
        kpe_pad[:, 64:96] = m_kpe
        kpesw_pad = np.zeros((1024, 128), np.float32)
        kpesw_pad[:, 64:96] = m_kpe[:, part]
        cols = np.concatenate([a_x, a_b, a_c, d_q, d_k, m_cq, m_ckv, kpe_pad, d_q[:, perm256], d_k[:, perm256],
                               kpesw_pad, np.zeros((1024, 384), np.float32)], axis=1)
        assert cols.shape[1] == 24 * 128
        fm = _fm(cols)
        sorder = (S_ORDER_A + S_ORDER_B + [21, 21, 21])
        for t in range(6):
            put(('wins', l, t), fm[:, sorder[4 * t:4 * t + 4]])
        put(('wv', l), d_v.reshape(8, 128, 256).transpose(1, 0, 2))
        wuq = inp["w_uq"][l]
        put(('wuq', l), wuq.reshape(3, 128, 768).transpose(1, 0, 2))
        permq = np.arange(768)
        for h in range(8):
            permq[h * 96 + 64:h * 96 + 96] = h * 96 + 64 + part
        put(('wuqsw', l), wuq[:, permq].reshape(3, 128, 768).transpose(1, 0, 2))
        wukv = inp["w_ukv"][l].reshape(256, 8, 128)
        wk = wukv[:, :, 0:64].reshape(256, 512)
        wvv = wukv[:, :, 64:128].reshape(256, 512)
        both = np.concatenate([wk.reshape(2, 128, 512).transpose(1, 0, 2).reshape(128, 1024),
                               wvv.reshape(2, 128, 512).transpose(1, 0, 2).reshape(128, 1024)], axis=1)
        put(('wukv', l), both)
        wout = inp["w_out"][l]
        for t in range(4):
            blk = np.zeros((128, 2, 1792), np.float32)
            for ci in range(2):
                c = 2 * t + ci
                wc = wout[:, c * 128:(c + 1) * 128]
                blk[:, ci, 0:256] = wc[0:256].reshape(2, 128, 128).transpose(1, 0, 2).reshape(128, 256)
                blk[0:64, ci, 256:1792] = wc[256:1024].reshape(12, 64, 128).transpose(1, 0, 2).reshape(64, 1536)
            put(('wout', l, t), blk)
        f1 = _fm(inp["w_ff1"][l])
        f3 = _fm(inp["w_ff3"][l])
        for t in range(11):
            blk = np.stack([np.stack([f1[:, 2 * t + ji], f3[:, 2 * t + ji]], axis=1) for ji in range(2)], axis=1)
            put(('ffa', l, t), blk)
        f2 = inp["w_ff2"][l]
        for c in range(8):
            put(('ff2', l, c), f2[:, c * 128:(c + 1) * 128].reshape(NJ, 128, 128).transpose(1, 0, 2))
    return ws


def rope_tables(qidx):
    t = np.arange(NT) + qidx * NT
    row = (t // 64).astype(np.float32)
    col = (t % 64).astype(np.float32)
    freqs = (np.float32(10000.0) ** (-np.arange(8, dtype=np.float32) / np.float32(8))).astype(np.float32)
    tab = np.zeros((128, 2, NT), np.float32)
    for r in range(32):
        pos = row if r < 16 else col
        r16 = r % 16
        ang = (pos * freqs[r16 % 8]).astype(np.float32)
        cs = np.cos(ang).astype(np.float32)
        sn = np.sin(ang).astype(np.float32)
        tab[r, 0] = cs
        tab[r, 1] = -sn if r16 < 8 else sn
    for b in range(1, 4):
        tab[b * 32:(b + 1) * 32] = tab[0:32]
    return tab


def make_vec(inp, qidx):
    v = np.zeros((128, NV), np.float32)
    for l in range(L):
        o = l * VPL
        for tap in range(3):
            for c in range(2):
                v[:, o + V_CONV + tap * 2 + c] = inp["conv_w"][l, tap, c * 128:(c + 1) * 128]
        v[:, o + V_QNW:o + V_QNW + 3] = inp["q_norm_w"][l].reshape(3, 128).T
        v[:, o + V_KVW:o + V_KVW + 2] = inp["kv_norm_w"][l].reshape(2, 128).T
        v[:, o + V_LN1G:o + V_LN1G + 8] = inp["ln1_g"][l].reshape(8, 128).T
        v[:, o + V_LN1B:o + V_LN1B + 8] = inp["ln1_b"][l].reshape(8, 128).T
        v[:, o + V_LN2G:o + V_LN2G + 8] = inp["ln2_g"][l].reshape(8, 128).T
        v[:, o + V_LN2B:o + V_LN2B + 8] = inp["ln2_b"][l].reshape(8, 128).T
        v[0:64, o + V_DNW] = inp["diff_norm_w"][l]
        v[64:128, o + V_DNW] = inp["diff_norm_w"][l]
        ba = inp["b_ada"][l].reshape(48, 128).T
        v[:, o + V_BADA:o + V_BADA + 96] = np.repeat(ba[:, :, None], 2, axis=2).reshape(128, 96)
    for j in range(4):
        v[j * 32:(j + 1) * 32, V_MASK + j] = 1.0
    for r in range(4):
        v[:, V_HALO + r] = 1.0 if r == qidx - 1 else 0.0
        v[:, V_HALO + 4 + r] = 1.0 if r == qidx + 1 else 0.0
    return v


_NC_CACHE = {}


def make_in_maps(inp):
    ws = pack_weights(inp)
    in_maps = []
    for c in range(8):
        b, qi = c // 4, c % 4
        xp = inp["x_prompt"][2 * c:2 * c + 2].reshape(NT, 8, 128).transpose(2, 1, 0)
        xs = inp["x_sample"][b, qi * NT:(qi + 1) * NT].reshape(NT, 8, 128).transpose(2, 1, 0)
        cond = np.stack([inp["c_ctx"].reshape(8, 128).T, inp["c"][b].reshape(8, 128).T], axis=2)
        lamv = np.stack([np.stack([inp["lam_q1"][l], inp["lam_k1"][l], inp["lam_q2"][l], inp["lam_k2"][l]])
                         for l in range(L)]).reshape(1, -1)
        ck = inp["cache_diff_k"][b]
        ckT = ck.transpose(0, 1, 2, 4, 3).reshape(L, 2, 128, 512).transpose(0, 2, 1, 3)
        cv = inp["cache_diff_v"][b].transpose(0, 2, 1, 3).reshape(L, 4, 128, 256).transpose(0, 2, 1, 3)
        cckvT = inp["cache_mla_ckv"][b].transpose(0, 2, 1).reshape(L, 2, 128, 512).transpose(0, 2, 1, 3)
        ckpeT = inp["cache_mla_kpe"][b].transpose(0, 2, 1)
        in_maps.append({
            "xp": np.ascontiguousarray(xp, np.float32),
            "xs": np.ascontiguousarray(xs, np.float32),
            "cond": np.ascontiguousarray(cond, np.float32),
            "vec": make_vec(inp, qi),
            "rope": rope_tables(qi),
            "lamv": np.ascontiguousarray(np.repeat(lamv, 64, axis=0), np.float32),
            "ws": ws,
            "ckT": np.ascontiguousarray(ckT, np.float32),
            "cv": np.ascontiguousarray(cv, np.float32),
            "cckvT": np.ascontiguousarray(cckvT, np.float32),
            "ckpeT": np.ascontiguousarray(ckpeT, np.float32),
        })
    return in_maps


def assemble(R):
    y_p = np.zeros((16, 256, D), np.float32)
    y_s = np.zeros((2, 2048, D), np.float32)
    st_k = np.zeros((16, L, 4, 2, 256, 32), np.float32)
    st_v = np.zeros((16, L, 4, 256, 64), np.float32)
    st_ckv = np.zeros((16, L, 256, 256), np.float32)
    st_kpe = np.zeros((16, L, 256, 32), np.float32)
    shp = {"yp": (128, 8, NT), "ys": (128, 8, NT), "sk": (L, 128, 2, NT), "sv": (L, 128, 4, 256),
           "sckv": (L, 128, 2, NT), "skpe": (L, 32, NT)}
    for c in range(8):
        b, qi = c // 4, c % 4
        r = {k: np.asarray(R[c][k]).reshape(shp[k]) for k in shp}
        y_p[2 * c:2 * c + 2] = r["yp"].transpose(2, 1, 0).reshape(2, 256, D)
        y_s[b, qi * NT:(qi + 1) * NT] = r["ys"].transpose(2, 1, 0).reshape(NT, D)
        sk = r["sk"].transpose(0, 2, 1, 3).reshape(L, 4, 2, 32, 2, 256)
        st_k[2 * c:2 * c + 2] = sk.transpose(4, 0, 1, 2, 5, 3)
        sv = r["sv"].transpose(0, 2, 1, 3).reshape(L, 2, 256, 4, 64)
        st_v[2 * c:2 * c + 2] = sv.transpose(1, 0, 3, 2, 4)
        sc = r["sckv"].transpose(0, 2, 1, 3).reshape(L, 256, 2, 256)
        st_ckv[2 * c:2 * c + 2] = sc.transpose(2, 0, 3, 1)
        sp = r["skpe"].reshape(L, 32, 2, 256)
        st_kpe[2 * c:2 * c + 2] = sp.transpose(2, 0, 3, 1)
    return (y_p, y_s, st_k, st_v, st_ckv, st_kpe)


def kernel(**inputs):
    inp = {k: np.asarray(v) for k, v in inputs.items()}
    if 'nc' not in _NC_CACHE:
        _NC_CACHE['nc'] = build_program()
    nc = _NC_CACHE['nc']
    in_maps = make_in_maps(inp)
    res = run_bass_kernel_spmd(nc, in_maps, core_ids=list(range(8)))
    return assemble(res.results)
```

```python
import math
from contextlib import ExitStack

import numpy as np
import concourse.bass as bass
import concourse.mybir as mybir
from concourse.bass_utils import run_bass_kernel_spmd

F32 = mybir.dt.float32
BF16 = mybir.dt.bfloat16
AF = mybir.ActivationFunctionType
ALU = mybir.AluOpType
AX = mybir.AxisListType

D = 1024
L = 2
NT = 512
DFF = 2816
NJ = DFF // 128
ALPHA = (2 * L) ** 0.25
DIFF_SCALE = 32 ** -0.5
MLA_SCALE = 96 ** -0.5
SLABW = 3588
S_ORDER_A = [8, 9, 18, 19, 13, 14, 15, 20, 0, 1, 4, 5]
S_ORDER_B = [2, 3, 6, 7, 16, 17, 10, 11, 12]
DEBUG = False
STOP = 0
SKIP1 = 0
NKEY_S = 2560

VPL = 140
V_CONV, V_QNW, V_KVW, V_LN1G, V_LN1B, V_LN2G, V_LN2B, V_DNW, V_BADA = 0, 6, 9, 11, 19, 27, 35, 43, 44
V_MASK = 2 * VPL
V_HALO = V_MASK + 4
NV = V_HALO + 8

def _wtiles():
    tiles = []
    for l in range(L):
        for t in range(12):
            tiles.append((('wada', l, t), 4096))
    for l in range(L):
        for t in range(6):
            tiles.append((('wins', l, t), 4096))
        tiles.append((('wv', l), 2048))
        tiles.append((('wuq', l), 2304))
        tiles.append((('wuqsw', l), 2304))
        tiles.append((('wukv', l), 2048))
        for t in range(4):
            tiles.append((('wout', l, t), 3584))
        for t in range(11):
            tiles.append((('ffa', l, t), 4096))
        for t in range(8):
            tiles.append((('ff2', l, t), 2816))
    offs = {}
    o = 0
    for k, n in tiles:
        offs[k] = (o, n)
        o += n
    return tiles, offs, o


WT_TILES, WT_OFFS, WT_TOTAL = _wtiles()


def _consumption_order():
    order = []
    for t in range(4):
        order.append(('wada', 0, t))
    for l in range(L):
        for t in range(6):
            order.append(('wins', l, t))
        order += [('wv', l), ('wuq', l), ('wukv', l)]
        for t in range(4, 12):
            order.append(('wada', l, t))
        for t in range(3):
            order.append(('wins', l, t))
        order.append(('wv', l))
        for t in range(3, 6):
            order.append(('wins', l, t))
        order += [('wuq', l), ('wuqsw', l)]
        for t in range(4):
            order.append(('wout', l, t))
        order.append(('wukv', l))
        for t in range(4):
            order.append(('wout', l, t))
        for t in range(11):
            order.append(('ffa', l, t))
        if l + 1 < L:
            for t in range(4):
                order.append(('wada', l + 1, t))
        for g in range(2):
            for t in range(8):
                order.append(('ff2', l, t))
    return order


class Res:
    __slots__ = ('name', 'lw', 'rd', 'ov', 'lo', 'hi', 'excl')

    def __init__(self, name, lo=0, hi=0, excl=False):
        self.name = name
        self.excl = excl
        self.lw = None
        self.rd = {}
        self.ov = [self]
        self.lo = lo
        self.hi = hi


class Queue:
    def __init__(self, name, sems):
        self.name = name
        self.sems = sems
        self.epoch = 0
        self.count = 0
        self.ops = []
        self.waited = {}


EPOCH_LIMIT = 12000


class Sched:
    def __init__(self, nc, es):
        self.nc = nc
        self.semh = {}
        self.q = {}
        for qn in ('pe', 'act', 'dve', 'pool', 'sp'):
            sems = []
            for e in range(4):
                key = ('q', qn, e)
                self.semh[key] = es.enter_context(nc.semaphore(f"s_{qn}_{e}"))
                sems.append(key)
            self.q[qn] = Queue(qn, sems)
        self.ndma = 24
        self.dma_keys = []
        self.dma_val = []
        for i in range(self.ndma):
            key = ('d', i)
            self.semh[key] = es.enter_context(nc.semaphore(f"s_dma_{i}"))
            self.dma_keys.append(key)
            self.dma_val.append(0)
        self.dma_pools = {'sp': [0, list(range(0, 16))], 'pool': [0, list(range(16, 24))]}
        self.cc_key = ('cc',)
        self.semh[self.cc_key] = es.enter_context(nc.semaphore("s_cc"))
        self.cc_val = 0

    def _wait(self, q, key, val):
        if q.waited.get(key, 0) >= val:
            return
        if key[0] == 'q':
            pq = self.q[key[1]]
            if pq.sems[pq.epoch] == key and val > pq.count:
                raise RuntimeError(f"forward wait: {q.name} waits {key} >= {val} but only {pq.count} signalled")
        q.waited[key] = val
        h = self.semh[key]
        q.ops.append(lambda e, h=h, v=val: e.wait_ge(h, v))

    def _deps(self, q, reads, writes):
        deps = set()
        for r in reads:
            for o in r.ov:
                if o.lw is not None:
                    deps.add(o.lw)
                if o.excl:
                    for qn_, x in o.rd.items():
                        if qn_ != q.name:
                            deps.add(x)
        for w in writes:
            for o in w.ov:
                if o.lw is not None:
                    deps.add(o.lw)
                for x in o.rd.values():
                    deps.add(x)
        return deps

    def _record(self, myid, qname, reads, writes):
        for r in reads:
            r.rd[qname] = myid
        for w in writes:
            w.lw = myid
            w.rd = {}

    def emit(self, qn, fn, reads=(), writes=(), signal=True, skip_self=False):
        q = self.q[qn]
        for (key, val) in self._deps(q, reads, writes):
            if skip_self and key[0] == 'q' and key[1] == qn:
                continue
            self._wait(q, key, val)
        if q.count >= EPOCH_LIMIT:
            q.epoch += 1
            q.count = 0
        key = q.sems[q.epoch]
        myid = (key, q.count + 1)
        if signal:
            q.count += 1
            h = self.semh[key]
            q.ops.append(lambda e, fn=fn, h=h: fn(e).then_inc(h, 1))
        else:
            q.ops.append(lambda e, fn=fn: fn(e))
        self._record(myid, qn, reads, writes)

    def dma(self, qn, out, in_, reads=(), writes=()):
        q = self.q[qn]
        for (key, val) in self._deps(q, reads, writes):
            self._wait(q, key, val)
        st = self.dma_pools[qn]
        i = st[1][st[0] % len(st[1])]
        st[0] += 1
        key = self.dma_keys[i]
        if self.dma_val[i] > 0:
            self._wait(q, key, self.dma_val[i])
        self.dma_val[i] += 16
        myid = (key, self.dma_val[i])
        h = self.semh[key]
        q.ops.append(lambda e, o=out, i_=in_, h=h: e.dma_start(out=o, in_=i_).then_inc(h, 16))
        self._record(myid, 'dma%d' % i, reads, writes)
        return myid

    def collective(self, fn, reads=(), writes=()):
        q = self.q['pool']
        for (key, val) in self._deps(q, reads, writes):
            self._wait(q, key, val)
        self.cc_val += 1
        myid = (self.cc_key, self.cc_val)
        h = self.semh[self.cc_key]
        q.ops.append(lambda e, fn=fn, h=h: fn(e).then_inc(h))
        self._record(myid, 'cc', reads, writes)

    def final_wait(self, qn, res_list):
        q = self.q[qn]
        for r in res_list:
            for o in r.ov:
                if o.lw is not None:
                    self._wait(q, o.lw[0], o.lw[1])


class Tile:
    def __init__(self, ap, nres, name, lo=None, nbytes=None, arena=None, ranges=None):
        self.ap = ap
        self.name = name
        self.res = []
        if ranges is not None:
            for i, (a, b) in enumerate(ranges):
                self.res.append(Res(f"{name}.{i}", lo + a, lo + b))
            nres = 0
        for i in range(nres):
            if lo is None:
                self.res.append(Res(f"{name}.{i}"))
            else:
                sz = nbytes // nres
                self.res.append(Res(f"{name}.{i}", lo + i * sz, lo + (i + 1) * sz))
        if arena is not None:
            for r in self.res:
                for o in arena:
                    if o.lo < r.hi and r.lo < o.hi:
                        o.ov.append(r)
                        r.ov.append(o)
                arena.append(r)

    def r(self, i=0):
        return self.res[i]

    def all(self):
        return list(self.res)


DT_SIZE = {F32: 4, BF16: 2}


def build_program():
    nc = bass.Bass("TRN2", target_bir_lowering=False)
    es = ExitStack()
    S = Sched(nc, es)

    def din(name, shape, dt=F32):
        return nc.dram_tensor(name, list(shape), dt, kind="ExternalInput").ap()

    def dout(name, shape, dt=F32):
        return nc.dram_tensor(name, list(shape), dt, kind="ExternalOutput").ap()

    d_xp = din("xp", [128, 8, NT])
    d_xs = din("xs", [128, 8, NT])
    d_cond = din("cond", [128, 8, 2])
    d_vec = din("vec", [128, NV])
    d_rope = din("rope", [128, 2, NT])
    d_lamv = din("lamv", [64, L * 4 * 32])
    d_ws = din("ws", [128, WT_TOTAL])
    d_ckT = din("ckT", [L, 128, 2, 512])
    d_cv = din("cv", [L, 128, 4, 256])
    d_cckvT = din("cckvT", [L, 128, 2, 512])
    d_ckpeT = din("ckpeT", [L, 32, 512])

    o_yp = dout("yp", [128, 8, NT])
    o_ys = dout("ys", [128, 8, NT])
    o_sk = dout("sk", [L, 128, 2, NT])
    o_sv = dout("sv", [L, 128, 4, 256])
    o_sckv = dout("sckv", [L, 128, 2, NT])
    o_skpe = dout("skpe", [L, 32, NT])
    out_res = []
    o_dbg = dout("dbg", [8, 128, 8, NT]) if DEBUG else None
    dbg_n = [0]

    o_dbgb = dout("dbgb", [12, 128, 6144], BF16) if DEBUG else None
    dbgb_n = [0]

    def tapb(ap2d, parts, n, reads):
        if not DEBUG:
            return
        r = Res("out")
        S.dma('sp', o_dbgb[dbgb_n[0], 0:parts, 0:n], ap2d, reads=reads, writes=[r])
        out_res.append(r)
        dbgb_n[0] += 1

    def tap(tile):
        if not DEBUG:
            return
        r = Res("out")
        S.dma('sp', o_dbg[dbg_n[0]], tile.ap, reads=tile.all(), writes=[r])
        out_res.append(r)
        dbg_n[0] += 1

    slab_in = [nc.dram_tensor(f"slab_in{l}", [128, SLABW], BF16) for l in range(L)]
    slab_all = [nc.dram_tensor(f"slab_all{l}", [512, SLABW], BF16) for l in range(L)]
    slab_in_res = [Res(f"slab_in{l}") for l in range(L)]
    slab_all_res = [Res(f"slab_all{l}") for l in range(L)]

    def sb(name, shape, dt):
        return es.enter_context(nc.sbuf_tensor(name, list(shape), dt))

    def T(name, shape, dt, nres=1):
        return Tile(sb(name, shape, dt)[:], nres, name)

    X = [T("XP", [128, 8, NT], F32, 8), T("XS", [128, 8, NT], F32, 8)]
    HT0 = T("HT0", [128, 8, NT], BF16, 8)
    NRING = 4
    RING = [T(f"RING{i}", [128, 4096], BF16, 1) for i in range(NRING)]
    QM = T("QM", [128, 2, 4, NT], BF16, 8)
    D0 = T("D0", [64, NT], F32, 1)
    QMH = T("QMH", [96, 8, NT], BF16, 8)
    YA = T("YA", [128, 2, NT], BF16, 2)
    YH = T("YH", [64, 12, NT], BF16, 12)
    ROPE = T("ROPE", [128, 2, NT], F32, 1)
    VEC = T("VEC", [128, NV], F32, 1)
    CND = T("CND", [128, 8, 2], F32, 1)
    SCB = T("SCB", [128, 8, 2], BF16, 1)
    MOD = T("MOD", [128, L * 6, 8, 2], F32, L * 6)
    LAMV = T("LAMV", [64, L * 4 * 32], F32, 1)
    LAMT = T("LAMT", [64, 2 * L * 32], F32, 1)
    LAMS = T("LAMS", [64, 2 * L], F32, 1)
    LAME = T("LAME", [64, 2 * L], F32, 1)
    NEGLAM = T("NEGLAM", [64, L], F32, 1)
    DNW = T("DNW", [64, L], F32, 1)
    ONES = T("ONES", [128, 128], BF16, 1)
    ET = [T(f"ET{i}", [128, NT], BF16, 1) for i in range(4)]
    TF = [T(f"TF{i}", [128, NT], F32, 1) for i in range(5)]
    TB = [T(f"TB{i}", [128, NT], BF16, 1) for i in range(4)]
    AB = T("AB", [128, 2, NT], F32, 2)
    UP = T("UP", [128, 2, 2, 258], F32, 2)
    US = T("US", [128, 2, 1, 514], F32, 2)
    UBALL = T("UBALL", [128, 4, 4], BF16, 1)
    UBF = T("UBF", [128, 4, 4], F32, 1)

    REG_BYTES = 67072
    reg = sb("REG", [128, REG_BYTES // 4], F32)
    arena = []

    def RT(name, off, shape, dt, nres=1, parts=128, ranges=None):
        n = 1
        for s_ in shape:
            n *= s_
        nbytes = n * DT_SIZE[dt]
        assert off % 4 == 0 and nbytes % 4 == 0 and off + nbytes <= REG_BYTES, (name, off, nbytes)
        v = reg[0:parts, off // 4:(off + nbytes) // 4]
        if dt == BF16:
            v = v.bitcast(BF16)
        if len(shape) == 2:
            v = v.rearrange("p (a b) -> p a b", a=shape[0])
        elif len(shape) == 3:
            v = v.rearrange("p (a b c) -> p a b c", a=shape[0], b=shape[1])
        return Tile(v, nres, name, lo=off, nbytes=nbytes, arena=arena, ranges=ranges), off + nbytes

    o = 0
    AXT, o = RT("AXT", o, [2, NT], F32, 2)
    CQ, o = RT("CQ", o, [3, NT], F32, 3)
    CKV, o = RT("CKV", o, [2, NT], F32, 2)
    CQN, o = RT("CQN", o, [3, NT], BF16, 3)
    QR, o = RT("QR", o, [2, NT], F32, 2)
    KR, o = RT("KR", o, [2, NT], F32, 2)
    KPR, o = RT("KPR", o, [NT], F32, 1)
    _sec = [(0, 512), (512, 1024), (1024, 1536), (1536, 2048), (2048, 2304), (2304, 2560), (2560, 2816),
            (2816, 3072), (3072, 3584), (3584, 3588)]
    SLAB, o = RT("SLAB", o, [SLABW], BF16, 1, ranges=[(2 * a, 2 * b) for a, b in _sec])
    SL_K, SL_CKV, SL_V, SL_KPE, SL_UB = 0, 2, 4, 8, 9
    KMP, o = RT("KMP", o, [8, NT], BF16, 8, parts=96)
    VMP, o = RT("VMP", o, [4, 8, 65], BF16, 4)
    VAP, o = RT("VAP", o, [4, 4, 65], BF16, 4)
    KTP, o = RT("KTP", o, [2, NT], BF16, 2)
    CKVNP, o = RT("CKVNP", o, [2, NT], BF16, 2)
    assert o <= REG_BYTES, o
    D1, o = RT("D1", o, [NT], F32, 1, parts=64)
    RS1, o = RT("RS1", o, [NT], F32, 1, parts=64)
    RL1, o = RT("RL1", o, [NT], F32, 1, parts=64)
    assert o <= REG_BYTES, o
    o = 0
    KTA, o = RT("KTA", o, [2, NKEY_S], BF16, 10)
    VAA, o = RT("VAA", o, [20, 4, 65], BF16, 20)
    CKVTA, o = RT("CKVTA", o, [2, NKEY_S], BF16, 10)
    KMH0, o = RT("KMH0", o, [NKEY_S], BF16, 1, parts=96)
    KMH1, o = RT("KMH1", o, [NKEY_S], BF16, 1, parts=96)
    KMH = [KMH0, KMH1]
    VMA, o = RT("VMA", o, [20, 8, 65], BF16, 20)
    KPEA, o = RT("KPEA", o, [NKEY_S], BF16, 5, parts=96)
    assert o <= REG_BYTES, o
    o = 0
    G0, o = RT("G0", o, [NJ, NT], BF16, NJ)
    G1, o = RT("G1", o, [NJ, NT], BF16, NJ)
    HT1, o = RT("HT1", o, [8, NT], BF16, 8)
    assert o <= REG_BYTES, o
    GT = [G0, G1]
    HT = [HT0, HT1]

    psum = es.enter_context(nc.psum_tensor("PS", [128, 8, 512], F32))
    PSR = [Res(f"bank{i}", excl=True) for i in range(8)]
    bank_rr = {'G': [0, list(range(8))], 'S': [0, [0, 1, 2]], 'PV': [0, [3, 4, 5]], 'M': [0, [6, 7]],
               'S4': [0, [0, 1, 2, 3]], 'PV2': [0, [4, 5]]}

    def bank(pool='G'):
        st = bank_rr[pool]
        b = st[1][st[0] % len(st[1])]
        st[0] += 1
        return b

    tf_rr = [0]

    def tf():
        t = TF[tf_rr[0] % len(TF)]
        tf_rr[0] += 1
        return t

    tb_rr = [0]

    def tb():
        t = TB[tb_rr[0] % len(TB)]
        tb_rr[0] += 1
        return t

    et_rr = [0]

    def et():
        t = ET[et_rr[0] % len(ET)]
        et_rr[0] += 1
        return t

    def mm(b, out_ap, lhsT, rhs, reads, start, stop, sig=False):
        S.emit('pe', lambda e: e.matmul(out_ap, lhsT=lhsT, rhs=rhs, start=start, stop=stop),
               reads=reads, writes=[PSR[b]], signal=(stop or sig), skip_self=True)

    def act(out, in_, func, reads, writes, scale=1.0, bias=0.0):
        S.emit('act', lambda e: e.activation(out=out, in_=in_, func=func, bias=bias, scale=scale),
               reads=reads, writes=writes)

    def tt(eng, out, in0, in1, op, reads, writes):
        S.emit(eng, lambda e: e.tensor_tensor(out=out, in0=in0, in1=in1, op=op), reads=reads, writes=writes)

    def ts(eng, out, in0, s1, s2, op0, op1, reads, writes):
        if op1 is None:
            S.emit(eng, lambda e: e.tensor_scalar(out=out, in0=in0, scalar1=s1, scalar2=None, op0=op0),
                   reads=reads, writes=writes)
        else:
            S.emit(eng, lambda e: e.tensor_scalar(out=out, in0=in0, scalar1=s1, scalar2=s2, op0=op0, op1=op1),
                   reads=reads, writes=writes)

    def stt(out, in0, scalar, in1, op0, op1, reads, writes):
        S.emit('dve', lambda e: e.scalar_tensor_tensor(out=out, in0=in0, scalar=scalar, in1=in1, op0=op0, op1=op1),
               reads=reads, writes=writes)

    def cp(eng, out, in_, reads, writes):
        if eng == 'act':
            S.emit('act', lambda e: e.copy(out=out, in_=in_), reads=reads, writes=writes)
        else:
            S.emit(eng, lambda e: e.tensor_copy(out=out, in_=in_), reads=reads, writes=writes)

    def memset(eng, ap, val, writes):
        S.emit(eng, lambda e: e.memset(ap, val), writes=writes)

    vres = VEC.all()

    def vcol(i, parts=128, p0=0):
        return VEC.ap[p0:p0 + parts, i:i + 1]

    class WView:
        def __init__(self, tile, idx):
            self.tile = tile
            self.idx = idx
            self.ap = tile.ap

        def all(self):
            assert self.tile.cur == self.idx, ("weight ring slot recycled while in use", self.idx, self.tile.cur)
            return self.tile.all()

    order = _consumption_order()
    wstate = {'next_emit': 0, 'next_get': 0}

    def w_emit_upto(n):
        while wstate['next_emit'] < min(n, len(order)):
            i = wstate['next_emit']
            key = order[i]
            off, ne = WT_OFFS[key]
            rb = RING[i % NRING]
            if ne > 2048:
                half = ne // 2
                src = d_ws[:, off:off + ne].rearrange("p (a b) -> p a b", b=half)
                dst = rb.ap[:, 0:ne].rearrange("p (a b) -> p a b", b=half)
            else:
                src = d_ws[:, off:off + ne]
                dst = rb.ap[:, 0:ne]
            S.dma('pool', dst, src, reads=[], writes=rb.all())
            rb.cur = i
            wstate['next_emit'] += 1

    def wget(expect):
        i = wstate['next_get']
        assert order[i][0] == expect, (order[i], expect)
        w_emit_upto(i + NRING - 1)
        wstate['next_get'] += 1
        return WView(RING[i % NRING], i)

    S.dma('sp', X[0].ap, d_xp, writes=X[0].all())
    S.dma('sp', X[1].ap, d_xs, writes=X[1].all())
    S.dma('sp', CND.ap, d_cond, writes=CND.all())
    S.dma('sp', VEC.ap, d_vec, writes=VEC.all())
    S.dma('sp', ROPE.ap, d_rope, writes=ROPE.all())
    S.dma('sp', LAMV.ap, d_lamv, writes=LAMV.all())
    w_emit_upto(3)

    memset('dve', ONES.ap, 1.0, ONES.all())
    nreg = REG_BYTES // 4
    for a0 in range(0, nreg, 2096):
        memset('dve', reg[:, a0:min(nreg, a0 + 2096)], 0.0, list(arena))
    memset('dve', UP.ap, 0.0, UP.all())
    memset('dve', US.ap, 0.0, US.all())
    COS = ROPE.ap[:, 0, :]
    SIN = ROPE.ap[:, 1, :]

    act(SCB.ap, CND.ap, AF.Silu, CND.all(), SCB.all())
    def mod_piece(l, t0, t1, pool='G'):
        b = bank(pool)
        for t in range(t0, t1):
            W = wget('wada')
            wv = W.ap[:, 0:4096].rearrange("p (j k c) -> p j k c", j=4, k=8)
            for ji in range(4):
                j = t * 4 + ji
                for k in range(8):
                    mm(b, psum[:, b, 2 * j:2 * j + 2], wv[:, ji, k, :], SCB.ap[:, k, :],
                       W.all() + SCB.all(), start=(k == 0), stop=(k == 7))
        for v in range(t0 // 2, t1 // 2):
            mr = MOD.r(l * 6 + v)
            tt('dve', MOD.ap[:, l * 6 + v].rearrange("p j c -> p (j c)"), psum[:, b, 16 * v:16 * v + 16],
               VEC.ap[:, l * VPL + V_BADA + 16 * v:l * VPL + V_BADA + 16 * v + 16], ALU.add, [PSR[b]] + vres, [mr])
            if v in (1, 4):
                ts('dve', MOD.ap[:, l * 6 + v], MOD.ap[:, l * 6 + v], 1.0, None, ALU.add, None, [mr], [mr])

    mod_piece(0, 0, 4)

    def modv(l, v, ch, g):
        return MOD.ap[:, l * 6 + v, ch, g:g + 1]

    lv = LAMV.ap.rearrange("p (l f d) -> p l f d", l=L, f=4)
    lt = LAMT.ap.rearrange("p (l t d) -> p l t d", l=L, t=2)
    for l in range(L):
        for t_ in range(2):
            tt('dve', lt[:, l, t_, :], lv[:, l, 2 * t_, :], lv[:, l, 2 * t_ + 1, :], ALU.mult,
               LAMV.all(), LAMT.all())
    S.emit('dve', lambda e: e.tensor_reduce(out=LAMS.ap, in_=LAMT.ap.rearrange("p (a d) -> p a d", d=32),
                                            axis=AX.X, op=ALU.add), reads=LAMT.all(), writes=LAMS.all())
    act(LAME.ap, LAMS.ap, AF.Exp, LAMS.all(), LAME.all())
    for l in range(L):
        lam_init = 0.8 - 0.6 * math.exp(-0.3 * l)
        tt('dve', NEGLAM.ap[:, l:l + 1], LAME.ap[:, 2 * l + 1:2 * l + 2], LAME.ap[:, 2 * l:2 * l + 1],
           ALU.subtract, LAME.all(), NEGLAM.all())
        ts('dve', NEGLAM.ap[:, l:l + 1], NEGLAM.ap[:, l:l + 1], -lam_init, None, ALU.add, None,
           NEGLAM.all(), NEGLAM.all())
        ts('dve', DNW.ap[:, l:l + 1], VEC.ap[0:64, l * VPL + V_DNW:l * VPL + V_DNW + 1], 1.0 - lam_init, None,
           ALU.mult, None, vres, DNW.all())

    class _Stop(Exception):
        pass

    def chk(stage):
        if STOP and stage >= STOP - 1e-9:
            raise _Stop()

    def modulate(l, g, which, dst):
        vs, vh = (1, 0) if which == 1 else (4, 3)
        for ch in range(8):
            if ch % 2 == 0:
                ts('dve', dst.ap[:, ch, :], X[g].ap[:, ch, :], modv(l, vs, ch, g), modv(l, vh, ch, g),
                   ALU.mult, ALU.add, [X[g].r(ch), MOD.r(l * 6 + vs), MOD.r(l * 6 + vh)], [dst.r(ch)])
            else:
                act(dst.ap[:, ch, :], X[g].ap[:, ch, :], AF.Identity, [X[g].r(ch), MOD.r(l * 6 + vs), MOD.r(l * 6 + vh)], [dst.r(ch)],
                    scale=modv(l, vs, ch, g), bias=modv(l, vh, ch, g))

    def rstd_from_sumsq(b, parts, n, ncols, eps):
        t1 = tf()
        act(t1.ap[0:parts, 0:ncols], psum[0:parts, b, 0:ncols], AF.Ln, [PSR[b]], t1.all(), scale=1.0 / n, bias=eps)
        t2 = tf()
        act(t2.ap[0:parts, 0:ncols], t1.ap[0:parts, 0:ncols], AF.Exp, t1.all(), t2.all(), scale=-0.5)
        return t2

    def layer_norm(l, g, vg, vb):
        bs, bq = bank('G'), bank('G')
        for ch in range(8):
            zb, zs = tb(), tb()
            cp('dve', zb.ap, X[g].ap[:, ch, :], [X[g].r(ch)], zb.all())
            act(zs.ap, X[g].ap[:, ch, :], AF.Square, [X[g].r(ch)], zs.all())
            mm(bs, psum[:, bs, :], ONES.ap, zb.ap, ONES.all() + zb.all(), start=(ch == 0), stop=(ch == 7), sig=True)
            mm(bq, psum[:, bq, :], ONES.ap, zs.ap, ONES.all() + zs.all(), start=(ch == 0), stop=(ch == 7), sig=True)
        mean = tf()
        act(mean.ap, psum[:, bs, :], AF.Copy, [PSR[bs]], mean.all(), scale=1.0 / D)
        msq = tf()
        tt('dve', msq.ap, mean.ap, mean.ap, ALU.mult, mean.all(), msq.all())
        var = tf()
        stt(var.ap, psum[:, bq, :], 1.0 / D, msq.ap, ALU.mult, ALU.subtract, [PSR[bq]] + msq.all(), var.all())
        lnv = tf()
        act(lnv.ap, var.ap, AF.Ln, var.all(), lnv.all(), bias=1e-5)
        rstd = tf()
        act(rstd.ap, lnv.ap, AF.Exp, lnv.all(), rstd.all(), scale=-0.5)
        for ch in range(8):
            xr = X[g].r(ch)
            xa = X[g].ap[:, ch, :]
            tt('dve', xa, xa, mean.ap, ALU.subtract, [xr] + mean.all(), [xr])
            tt('dve', xa, xa, rstd.ap, ALU.mult, [xr] + rstd.all(), [xr])
            act(xa, xa, AF.Identity, [xr] + vres, [xr], scale=vcol(l * VPL + vg + ch), bias=vcol(l * VPL + vb + ch))

    def stage_out(dst_dram, src_ap, reads, parts=128, ncols=NT):
        st = tf()
        cp('dve', st.ap[0:parts, 0:ncols], src_ap, reads, st.all())
        r = Res("out")
        S.dma('sp', dst_dram, st.ap[0:parts, 0:ncols], reads=st.all(), writes=[r])
        out_res.append(r)

    def rope_finish(dst_ap, part_ap, psw_ap, reads, writes, p0, p1):
        t1 = tf()
        tt('dve', t1.ap[p0:p1], psw_ap, SIN[p0:p1], ALU.mult, reads + ROPE.all(), t1.all())
        tt('dve', dst_ap, t1.ap[p0:p1], part_ap, ALU.add, t1.all() + reads, writes)

    premod = set()

    def mixer_project(l, g, mid_hook=None):
        H = HT[0]
        if (l, g) not in premod:
            modulate(l, g, 1, H)
        isS = (g == 1)
        U = US if isS else UP
        st_ = {'statq': None, 'statk': None}
        corder = S_ORDER_A + S_ORDER_B
        for pos, c in enumerate(corder):
            if isS and pos == len(S_ORDER_A):
                v_token_major(l, g, H)
                mid_hook()
            if pos % 4 == 0:
                W = wget('wins')
                wv = W.ap[:, 0:4096].rearrange("p (j k c) -> p j k c", j=4, k=8)
            if c >= 16 and not isS:
                continue
            b = bank('G')
            pb = psum[:, b, :]
            for k in range(8):
                mm(b, pb, wv[:, pos % 4, k, :], H.ap[:, k, :], W.all() + [H.r(k)], start=(k == 0), stop=(k == 7))
            R = [PSR[b]]
            statq, statk = st_['statq'], st_['statk']
            if c in (0, 1):
                cp('act', AXT.ap[:, c, :], pb, R, [AXT.r(c)])
            elif c in (2, 3):
                cp('act', AB.ap[:, c - 2, :], pb, R, [AB.r(c - 2)])
            elif c in (4, 5):
                cc = c - 4
                uin = U.ap[:, cc, :, 1:1 + (NT if isS else 256)]
                tt('dve', uin, pb.rearrange("p (s t) -> p s t", s=(1 if isS else 2)),
                   AXT.ap[:, cc, :].rearrange("p (s t) -> p s t", s=(1 if isS else 2)), ALU.mult,
                   R + [AXT.r(cc)], [U.r(cc)])
            elif c in (6, 7):
                cc = c - 6
                if isS:
                    tt('dve', QR.ap[:, cc, :], pb, COS, ALU.mult, R + ROPE.all(), [QR.r(cc)])
                else:
                    for j in range(4):
                        act(QM.ap[:, cc, j, :], pb, AF.Identity, R + vres, [QM.r(cc * 4 + j)], scale=vcol(V_MASK + j))
            elif c in (8, 9):
                cc = c - 8
                if isS:
                    tt('dve', KR.ap[:, cc, :], pb, COS, ALU.mult, R + ROPE.all(), [KR.r(cc)])
                else:
                    cp('act', KTP.ap[:, cc, :], pb, R, [KTP.r(cc)])
                    if not SKIP1:
                        stage_out(o_sk[l, :, cc, :], pb, R)
            elif c in (10, 11, 12):
                cc = c - 10
                cp('act', CQ.ap[:, cc, :], pb, R, [CQ.r(cc)])
                sq = tb()
                act(sq.ap, pb, AF.Square, R, sq.all())
                if cc == 0:
                    statq = st_['statq'] = bank('G')
                mm(statq, psum[:, statq, :], ONES.ap, sq.ap, ONES.all() + sq.all(), start=(cc == 0), stop=(cc == 2), sig=True)
                if cc == 2:
                    rs = rstd_from_sumsq(statq, 128, 384.0, NT, 1e-6)
                    for c3 in range(3):
                        stt(CQN.ap[:, c3, :], CQ.ap[:, c3, :], vcol(l * VPL + V_QNW + c3), rs.ap, ALU.mult, ALU.mult,
                            [CQ.r(c3)] + vres + rs.all(), [CQN.r(c3)])
            elif c in (13, 14):
                cc = c - 13
                cp('act', CKV.ap[:, cc, :], pb, R, [CKV.r(cc)])
                sq = tb()
                act(sq.ap, pb, AF.Square, R, sq.all())
                if cc == 0:
                    statk = st_['statk'] = bank('G')
                mm(statk, psum[:, statk, :], ONES.ap, sq.ap, ONES.all() + sq.all(), start=(cc == 0), stop=(cc == 1), sig=True)
                if cc == 1:
                    rs = rstd_from_sumsq(statk, 128, 256.0, NT, 1e-6)
                    for c2 in range(2):
                        if isS:
                            stt(SLAB.ap[:, 1024 + c2 * NT:1024 + (c2 + 1) * NT], CKV.ap[:, c2, :],
                                vcol(l * VPL + V_KVW + c2), rs.ap, ALU.mult, ALU.mult,
                                [CKV.r(c2)] + vres + rs.all(), [SLAB.r(SL_CKV + c2)])
                        else:
                            stt(CKV.ap[:, c2, :], CKV.ap[:, c2, :], vcol(l * VPL + V_KVW + c2), rs.ap,
                                ALU.mult, ALU.mult, [CKV.r(c2)] + vres + rs.all(), [CKV.r(c2)])
                            cp('act', CKVNP.ap[:, c2, :], CKV.ap[:, c2, :], [CKV.r(c2)], [CKVNP.r(c2)])
                            r = Res("out")
                            S.dma('sp', o_sckv[l, :, c2, :], CKV.ap[:, c2, :], reads=[CKV.r(c2)], writes=[r])
                            out_res.append(r)
            elif c == 15:
                if isS:
                    tt('dve', KPR.ap[64:96, :], pb[64:96], COS[64:96], ALU.mult, R + ROPE.all(), KPR.all())
                else:
                    for h in range(8):
                        cp('act' if h % 2 else 'dve', KMP.ap[64:96, h, :], pb[64:96], R, [KMP.r(h)])
                    st = tf()
                    cp('dve', st.ap[64:96, :], pb[64:96], R, st.all())
                    r = Res("out")
                    S.dma('sp', o_skpe[l], st.ap[64:96, :], reads=st.all(), writes=[r])
                    out_res.append(r)
            elif c in (16, 17):
                cc = c - 16
                rope_finish(QR.ap[:, cc, :], QR.ap[:, cc, :], pb, R + [QR.r(cc)], [QR.r(cc)], 0, 128)
                for j in range(4):
                    act(QM.ap[:, cc, j, :], QR.ap[:, cc, :], AF.Identity, [QR.r(cc)] + vres, [QM.r(cc * 4 + j)],
                        scale=vcol(V_MASK + j))
            elif c in (18, 19):
                cc = c - 18
                rope_finish(SLAB.ap[:, cc * NT:(cc + 1) * NT], KR.ap[:, cc, :], pb, R + [KR.r(cc)], [SLAB.r(SL_K + cc)], 0, 128)
            elif c == 20:
                rope_finish(SLAB.ap[64:96, 3072:3072 + NT], KPR.ap[64:96, :], pb[64:96], R + KPR.all(), [SLAB.r(SL_KPE)], 64, 96)
            chk((1 if g == 0 else 5) + (c + 1) / 100.0)
        if not isS:
            v_token_major(l, g, H)
        uq_project(l, g)

    def v_token_major(l, g, H):
        isS = (g == 1)
        W = wget('wv')
        wvv = W.ap[:, 0:2048].rearrange("p (k c) -> p k c", k=8)
        for t4 in range(4):
            b = bank('G')
            for k in range(8):
                mm(b, psum[:, b, 0:256], H.ap[:, k, t4 * 128:(t4 + 1) * 128], wvv[:, k, :], W.all() + [H.r(k)],
                   start=(k == 0), stop=(k == 7))
            if isS:
                cp('act', SLAB.ap[:, 2048 + t4 * 256:2048 + (t4 + 1) * 256], psum[:, b, 0:256], [PSR[b]], [SLAB.r(SL_V + t4)])
            else:
                cp('act', VAP.ap[:, t4, :, 0:64], psum[:, b, 0:256].rearrange("p (h d) -> p h d", h=4), [PSR[b]], [VAP.r(t4)])
                st = tf()
                cp('dve', st.ap[:, 0:256], psum[:, b, 0:256], [PSR[b]], st.all())
                r = Res("out")
                S.dma('sp', o_sv[l, :, t4, :], st.ap[:, 0:256], reads=st.all(), writes=[r])
                out_res.append(r)
        if isS:
            for cc in range(2):
                cp('dve', SLAB.ap[:, 3584 + 2 * cc:3584 + 2 * cc + 1], US.ap[:, cc, 0, 1:2], [US.r(cc)], [SLAB.r(SL_UB)])
                cp('dve', SLAB.ap[:, 3584 + 2 * cc + 1:3584 + 2 * cc + 2], US.ap[:, cc, 0, NT:NT + 1], [US.r(cc)], [SLAB.r(SL_UB)])

    def uq_project(l, g):
        isS = (g == 1)
        W = wget('wuq')
        wq = W.ap[:, 0:2304].rearrange("p (k c) -> p k c", k=3)
        if isS:
            W2 = wget('wuqsw')
            wq2 = W2.ap[:, 0:2304].rearrange("p (k c) -> p k c", k=3)
        for h in range(8):
            b = bank('G')
            for k in range(3):
                mm(b, psum[0:96, b, :], wq[:, k, h * 96:(h + 1) * 96], CQN.ap[:, k, :], W.all() + [CQN.r(k)],
                   start=(k == 0), stop=(k == 2))
            if isS:
                b2 = bank('G')
                for k in range(3):
                    mm(b2, psum[0:96, b2, :], wq2[:, k, h * 96:(h + 1) * 96], CQN.ap[:, k, :], W2.all() + [CQN.r(k)],
                       start=(k == 0), stop=(k == 2))
                cp('act', QMH.ap[0:64, h, :], psum[0:64, b, :], [PSR[b]], [QMH.r(h)])
                t0 = tf()
                tt('dve', t0.ap[64:96], psum[64:96, b, :], COS[64:96], ALU.mult, [PSR[b]] + ROPE.all(), t0.all())
                rope_finish(QMH.ap[64:96, h, :], t0.ap[64:96], psum[64:96, b2, :], [PSR[b2]] + t0.all(), [QMH.r(h)], 64, 96)
            else:
                cp('act' if h % 2 else 'dve', QMH.ap[0:96, h, :], psum[0:96, b, :], [PSR[b]], [QMH.r(h)])

    def conv_ya(l, g):
        isS = (g == 1)
        U = US if isS else UP
        ns, sl = (1, NT) if isS else (2, 256)
        for cc in range(2):
            t1 = tf()
            t1v = t1.ap.rearrange("p (s t) -> p s t", s=ns)
            ts('dve', t1v, U.ap[:, cc, :, 0:sl], vcol(l * VPL + V_CONV + 0 * 2 + cc), None, ALU.mult, None,
               [U.r(cc)] + vres, t1.all())
            stt(t1v, U.ap[:, cc, :, 1:1 + sl], vcol(l * VPL + V_CONV + 1 * 2 + cc), t1v, ALU.mult, ALU.add,
                [U.r(cc)] + vres + t1.all(), t1.all())
            stt(t1v, U.ap[:, cc, :, 2:2 + sl], vcol(l * VPL + V_CONV + 2 * 2 + cc), t1v, ALU.mult, ALU.add,
                [U.r(cc)] + vres + t1.all(), t1.all())
            tt('dve', YA.ap[:, cc, :], t1.ap, AB.ap[:, cc, :], ALU.mult, t1.all() + [AB.r(cc)], [YA.r(cc)])

    def attention(l, g, segs, kt_of, kside_d, vside_d, kside_m, vside_m, prep_head=None, hooks=None):
        units = []
        for hm in range(16):
            for (q0, nq, kts) in segs:
                if hm < 8:
                    units.append(dict(d=True, h=hm // 2, m=hm % 2, c=hm // 4, j=hm % 4, q0=q0, nq=nq, kts=kts))
                else:
                    units.append(dict(d=False, h=hm - 8, q0=q0, nq=nq, kts=kts))
        stages = [(ui, i) for ui, u in enumerate(units) for i in range(len(u['kts']))]
        short_units = min(len(u['kts']) for u in units) < 16
        LAG = 3
        spool, pvpool = ('S', 'PV') if short_units else ('S4', 'PV2')
        nst = len(stages)
        first_unit_of_head = {}
        for ui, u in enumerate(units):
            if not u['d'] and u['h'] not in first_unit_of_head:
                first_unit_of_head[u['h']] = ui
        prep_at = {}
        if prep_head is not None:
            per_head = len(segs)
            for h, ui in first_unit_of_head.items():
                prep_at.setdefault(max(0, ui - per_head), []).append(h)
        ebuf = {}
        norm_due = {}

        def ph_a(u):
            nq, pv = u['nq'], u['pv']
            rl = tf()
            act(rl.ap[64:65, 0:nq], psum[64:65, pv, 0:nq], AF.Ln, [PSR[pv]], rl.all())
            rr = tb()
            act(rr.ap[64:65, 0:nq], rl.ap[64:65, 0:nq], AF.Exp, rl.all(), rr.all(), scale=-1.0)
            u['rr'] = rr

        def ph_b(u):
            d, h, q0, nq, pv, rr = u['d'], u['h'], u['q0'], u['nq'], u['pv'], u['rr']
            bb = bank('M')
            mm(bb, psum[0:64, bb, 0:nq], ONES.ap[64:65, 0:64], rr.ap[64:65, 0:nq], ONES.all() + rr.all(), True, True)
            rb = tf()
            cp('dve', rb.ap[0:64, 0:nq], psum[0:64, bb, 0:nq], [PSR[bb]], rb.all())
            if not d:
                tt('dve', YH.ap[:, 4 + h, q0:q0 + nq], psum[0:64, pv, 0:nq], rb.ap[0:64, 0:nq], ALU.mult,
                   [PSR[pv]] + rb.all(), [YH.r(4 + h)])
            elif u['m'] == 0:
                tt('dve', D0.ap[0:64, q0:q0 + nq], psum[0:64, pv, 0:nq], rb.ap[0:64, 0:nq], ALU.mult,
                   [PSR[pv]] + rb.all(), D0.all())
            else:
                dt_ = tf()
                tt('dve', dt_.ap[0:64, 0:nq], psum[0:64, pv, 0:nq], rb.ap[0:64, 0:nq], ALU.mult,
                   [PSR[pv]] + rb.all(), dt_.all())
                if short_units:
                    ad, ada = D1, D1.ap[0:64, q0:q0 + nq]
                else:
                    ad = tf()
                    ada = ad.ap[0:64, 0:nq]
                stt(ada, dt_.ap[0:64, 0:nq], NEGLAM.ap[:, l:l + 1], D0.ap[0:64, q0:q0 + nq],
                    ALU.mult, ALU.add, dt_.all() + D0.all() + NEGLAM.all(), ad.all())
                sq = tb()
                tt('dve', sq.ap[0:64, 0:nq], ada, ada, ALU.mult, ad.all(), sq.all())
                u['ad'], u['ada'], u['sq'] = ad, ada, sq

        def ph_c(u):
            if not (u['d'] and u['m'] == 1):
                return
            nq, sq = u['nq'], u['sq']
            bq = bank('M')
            mm(bq, psum[0:64, bq, 0:nq], ONES.ap[0:64, 0:64], sq.ap[0:64, 0:nq], ONES.all() + sq.all(), True, True)
            if short_units:
                q0 = u['q0']
                act(RL1.ap[0:64, q0:q0 + nq], psum[0:64, bq, 0:nq], AF.Ln, [PSR[bq]], RL1.all(), scale=1.0 / 64.0, bias=1e-6)
                act(RS1.ap[0:64, q0:q0 + nq], RL1.ap[0:64, q0:q0 + nq], AF.Exp, RL1.all(), RS1.all(), scale=-0.5)
                u['rs'], u['rsa'] = RS1, RS1.ap[0:64, q0:q0 + nq]
            else:
                u['rs'] = rstd_from_sumsq(bq, 64, 64.0, nq, 1e-6)
                u['rsa'] = u['rs'].ap[0:64, 0:nq]

        def ph_d(u):
            if not (u['d'] and u['m'] == 1):
                return
            h, q0, nq, ad, rs = u['h'], u['q0'], u['nq'], u['ad'], u['rs']
            stt(YH.ap[:, h, q0:q0 + nq], u['ada'], DNW.ap[:, l:l + 1], u['rsa'],
                ALU.mult, ALU.mult, ad.all() + DNW.all() + rs.all(), [YH.r(h)])

        phases = (ph_a, ph_b, ph_c, ph_d)
        delays = (1, 5, 8, 12) if not short_units else (1, 3, 5, 7)
        NDELAY = delays[-1]

        for s in range(nst + LAG + NDELAY + 1):
            if s < nst:
                ui, i = stages[s]
                u = units[ui]
                if i == 0 and ui in prep_at:
                    for h_ in prep_at[ui]:
                        prep_head(h_)
                if i == 0 and hooks and ui in hooks:
                    hooks[ui]()
                q0, nq = u['q0'], u['nq']
                kt = u['kts'][i]
                sb_ = bank(spool)
                so = psum[:, sb_, 0:nq]
                if u['d']:
                    ka, kr = kside_d(u['c'], kt)
                    mm(sb_, so, ka, QM.ap[:, u['c'], u['j'], q0:q0 + nq], [kr, QM.r(u['c'] * 4 + u['j'])], True, True)
                    sc = DIFF_SCALE
                else:
                    ka, kr = kside_m(u['h'], kt)
                    mm(sb_, so, ka, QMH.ap[0:96, u['h'], q0:q0 + nq], [kr, QMH.r(u['h'])], True, True)
                    sc = MLA_SCALE
                e_ = et()
                act(e_.ap[:, 0:nq], so, AF.Exp, [PSR[sb_]], e_.all(), scale=sc)
                ebuf[s] = e_
            sp_ = s - LAG
            if 0 <= sp_ < nst:
                ui, i = stages[sp_]
                u = units[ui]
                nq = u['nq']
                if i == 0:
                    u['pv'] = bank(pvpool)
                kt = u['kts'][i]
                va, vr = vside_d(u['h'], kt) if u['d'] else vside_m(u['h'], kt)
                e_ = ebuf.pop(sp_)
                last = (i == len(u['kts']) - 1)
                mm(u['pv'], psum[0:65, u['pv'], 0:nq], va, e_.ap[:, 0:nq], [vr] + e_.all(), start=(i == 0), stop=last)
                if last:
                    for ph, dl in zip(phases, delays):
                        norm_due.setdefault(s + dl, []).append((ph, u))
            for ph, u in norm_due.pop(s, []):
                ph(u)
        assert not norm_due and not ebuf

    def out_proj_ln1(l, g):
        for t in range(4):
            W = wget('wout')
            for ci in range(2):
                c = 2 * t + ci
                base = ci * 1792
                wya = W.ap[:, base:base + 256].rearrange("p (k c) -> p k c", k=2)
                wh = W.ap[0:64, base + 256:base + 1792].rearrange("p (h c) -> p h c", h=12)
                b = bank('G')
                for k in range(2):
                    mm(b, psum[:, b, :], wya[:, k, :], YA.ap[:, k, :], W.all() + [YA.r(k)], start=(k == 0), stop=False)
                for hh in range(12):
                    mm(b, psum[:, b, :], wh[:, hh, :], YH.ap[:, hh, :], W.all() + [YH.r(hh)], start=False, stop=(hh == 11))
                xr = X[g].r(c)
                xa = X[g].ap[:, c, :]
                ts('dve', xa, xa, ALPHA, None, ALU.mult, None, [xr], [xr])
                stt(xa, psum[:, b, :], modv(l, 2, c, g), xa, ALU.mult, ALU.add, [PSR[b], MOD.r(l * 6 + 2), xr], [xr])
        layer_norm(l, g, V_LN1G, V_LN1B)

    def ffn(l):
        for g in range(2):
            modulate(l, g, 2, HT[g])
        for t in range(11):
            W = wget('ffa')
            wv = W.ap[:, 0:4096].rearrange("p (j w k c) -> p j w k c", j=2, w=2, k=8)
            for ji in range(2):
                j = 2 * t + ji
                for g in range(2):
                    b1, b3 = bank('G'), bank('G')
                    for k in range(8):
                        mm(b1, psum[:, b1, :], wv[:, ji, 0, k, :], HT[g].ap[:, k, :], W.all() + [HT[g].r(k)],
                           start=(k == 0), stop=(k == 7))
                    for k in range(8):
                        mm(b3, psum[:, b3, :], wv[:, ji, 1, k, :], HT[g].ap[:, k, :], W.all() + [HT[g].r(k)],
                           start=(k == 0), stop=(k == 7))
                    sl = tf()
                    act(sl.ap, psum[:, b1, :], AF.Silu, [PSR[b1]], sl.all())
                    tt('dve', GT[g].ap[:, j, :], psum[:, b3, :], sl.ap, ALU.mult, [PSR[b3]] + sl.all(), [GT[g].r(j)])
        if l + 1 < L:
            mod_piece(l + 1, 0, 4)
        for g in range(2):
            for c in range(8):
                W = wget('ff2')
                wv = W.ap[:, 0:2816].rearrange("p (j c) -> p j c", j=NJ)
                b = bank('G')
                for j in range(NJ):
                    mm(b, psum[:, b, :], wv[:, j, :], GT[g].ap[:, j, :], W.all() + [GT[g].r(j)],
                       start=(j == 0), stop=(j == NJ - 1))
                xr = X[g].r(c)
                xa = X[g].ap[:, c, :]
                ts('dve', xa, xa, ALPHA, None, ALU.mult, None, [xr], [xr])
                stt(xa, psum[:, b, :], modv(l, 5, c, g), xa, ALU.mult, ALU.add, [PSR[b], MOD.r(l * 6 + 5), xr], [xr])
            if g == 1 and l + 1 < L:
                modulate(l + 1, 0, 1, HT[0])
                premod.add((l + 1, 0))
            layer_norm(l, g, V_LN2G, V_LN2B)

    def kv_prompt(l):
        memset('dve', VAP.ap[:, :, :, 64:65], 1.0, VAP.all())
        memset('dve', VMP.ap[:, :, :, 64:65], 1.0, VMP.all())
        W = wget('wukv')
        wk = W.ap[:, 0:1024].rearrange("p (k c) -> p k c", k=2)
        wvv = W.ap[:, 1024:2048].rearrange("p (k c) -> p k c", k=2)
        for h in range(8):
            b = bank('G')
            for k in range(2):
                mm(b, psum[0:64, b, :], wk[:, k, h * 64:(h + 1) * 64], CKVNP.ap[:, k, :], W.all() + [CKVNP.r(k)],
                   start=(k == 0), stop=(k == 1))
            cp('act' if h % 2 else 'dve', KMP.ap[0:64, h, :], psum[0:64, b, :], [PSR[b]], [KMP.r(h)])
        for t4 in range(4):
            b = bank('G')
            for k in range(2):
                mm(b, psum[:, b, :], CKVNP.ap[:, k, t4 * 128:(t4 + 1) * 128], wvv[:, k, :], W.all() + [CKVNP.r(k)],
                   start=(k == 0), stop=(k == 1))
            cp('act' if t4 % 2 else 'dve', VMP.ap[:, t4, :, 0:64], psum[:, b, :].rearrange("p (h d) -> p h d", h=8),
               [PSR[b]], [VMP.r(t4)])

    def attn_prompt(l):
        segs = [(0, 256, [0, 1]), (256, 256, [2, 3])]
        attention(l, 0, segs, None,
                  lambda c, kt: (KTP.ap[:, c, kt * 128:(kt + 1) * 128], KTP.r(c)),
                  lambda h, kt: (VAP.ap[:, kt, h, :], VAP.r(kt)),
                  lambda h, kt: (KMP.ap[0:96, h, kt * 128:(kt + 1) * 128], KMP.r(h)),
                  lambda h, kt: (VMP.ap[:, kt, h, :], VMP.r(kt)),
                  hooks={2 + 8 * k: (lambda k=k: mod_piece(l, 4 + 2 * k, 6 + 2 * k, pool='M')) for k in range(4)})

    def exchange_sample(l):
        S.dma('sp', slab_in[l].ap()[:, :], SLAB.ap, reads=SLAB.all(), writes=[slab_in_res[l]])
        S.collective(lambda e: e.collective_compute("AllGather", ALU.bypass,
                                                    replica_groups=[[0, 1, 2, 3], [4, 5, 6, 7]],
                                                    ins=[slab_in[l].ap().opt()], outs=[slab_all[l].ap().opt()]),
                     reads=[slab_in_res[l]], writes=[slab_all_res[l]])

    def exchange_load(l):
        sa = slab_all[l].ap()
        memset('dve', VAA.ap[:, :, :, 64:65], 1.0, VAA.all())
        memset('dve', VMA.ap[:, :, :, 64:65], 1.0, VMA.all())
        for r in range(4):
            rows = sa[r * 128:(r + 1) * 128, :]
            S.dma('sp', KTA.ap[:, :, r * NT:(r + 1) * NT], rows[:, 0:1024].rearrange("p (c t) -> p c t", c=2),
                  reads=[slab_all_res[l]], writes=[KTA.r(r), KTA.r(5 + r)])
            S.dma('sp', CKVTA.ap[:, :, r * NT:(r + 1) * NT], rows[:, 1024:2048].rearrange("p (c t) -> p c t", c=2),
                  reads=[slab_all_res[l]], writes=[CKVTA.r(r), CKVTA.r(5 + r)])
            S.dma('sp', VAA.ap[:, 4 * r:4 * r + 4, :, 0:64],
                  rows[:, 2048:3072].rearrange("p (t h d) -> p t h d", t=4, h=4),
                  reads=[slab_all_res[l]], writes=[VAA.r(4 * r + i) for i in range(4)])
            S.dma('sp', KPEA.ap[64:96, r * NT:(r + 1) * NT], sa[r * 128 + 64:r * 128 + 96, 3072:3072 + NT],
                  reads=[slab_all_res[l]], writes=[KPEA.r(r)])
            S.dma('sp', UBALL.ap[:, r, :], rows[:, 3584:3588], reads=[slab_all_res[l]], writes=UBALL.all())
        S.dma('pool', KTA.ap[:, :, 2048:2560], d_ckT[l], writes=[KTA.r(4), KTA.r(9)])
        S.dma('pool', CKVTA.ap[:, :, 2048:2560], d_cckvT[l], writes=[CKVTA.r(4), CKVTA.r(9)])
        S.dma('pool', VAA.ap[:, 16:20, :, 0:64], d_cv[l].rearrange("p t (h d) -> p t h d", h=4),
              writes=[VAA.r(16 + i) for i in range(4)])
        S.dma('pool', KPEA.ap[64:96, 2048:2560], d_ckpeT[l], writes=[KPEA.r(4)])
        cp('dve', UBF.ap, UBALL.ap, UBALL.all(), UBF.all())
        for cc in range(2):
            for side in range(2):
                dst = US.ap[:, cc, 0, 0:1] if side == 0 else US.ap[:, cc, 0, NT + 1:NT + 2]
                col = 2 * cc + (1 if side == 0 else 0)
                ts('dve', dst, UBF.ap[:, 0, col:col + 1], vcol(V_HALO + 4 * side + 0), None, ALU.mult, None,
                   UBF.all() + vres, [US.r(cc)])
                for r in range(1, 4):
                    stt(dst, UBF.ap[:, r, col:col + 1], vcol(V_HALO + 4 * side + r), dst, ALU.mult, ALU.add,
                        UBF.all() + vres + [US.r(cc)], [US.r(cc)])
        for i in range(2):
            cp('dve' if i else 'act', KMH[i].ap[64:96, :], KPEA.ap[64:96, :], KPEA.all(), KMH[i].all())

    def kv_sample(l):
        W = wget('wukv')
        wk = W.ap[:, 0:1024].rearrange("p (k c) -> p k c", k=2)
        wvv = W.ap[:, 1024:2048].rearrange("p (k c) -> p k c", k=2)
        for kt in range(20):
            b = bank('G')
            for k in range(2):
                mm(b, psum[:, b, :], CKVTA.ap[:, k, kt * 128:(kt + 1) * 128], wvv[:, k, :], W.all() + [CKVTA.r(k * 5 + kt // 4)],
                   start=(k == 0), stop=(k == 1))
            cp('act' if kt % 2 else 'dve', VMA.ap[:, kt, :, 0:64], psum[:, b, :].rearrange("p (h d) -> p h d", h=8),
               [PSR[b]], [VMA.r(kt)])

        def prep_head(h):
            kb_ = KMH[h % 2]
            for kb in range(5):
                b = bank('M')
                for k in range(2):
                    mm(b, psum[0:64, b, :], wk[:, k, h * 64:(h + 1) * 64], CKVTA.ap[:, k, kb * 512:(kb + 1) * 512],
                       W.all() + [CKVTA.r(k * 5 + kb)], start=(k == 0), stop=(k == 1))
                cp('dve', kb_.ap[0:64, kb * 512:(kb + 1) * 512], psum[0:64, b, :], [PSR[b]], kb_.all())
        return prep_head

    def attn_sample(l, prep_head):
        segs = [(0, NT, list(range(20)))]
        attention(l, 1, segs, None,
                  lambda c, kt: (KTA.ap[:, c, kt * 128:(kt + 1) * 128], KTA.r(c * 5 + kt // 4)),
                  lambda h, kt: (VAA.ap[:, kt, h, :], VAA.r(kt)),
                  lambda h, kt: (KMH[h % 2].ap[0:96, kt * 128:(kt + 1) * 128], KMH[h % 2].r(0)),
                  lambda h, kt: (VMA.ap[:, kt, h, :], VMA.r(kt)),
                  prep_head=prep_head)

    try:
      for l in range(L):
        chk(1)
        mixer_project(l, 0)
        chk(2)
        conv_ya(l, 0)
        kv_prompt(l)
        chk(3)
        attn_prompt(l)
        chk(5)
        mixer_project(l, 1, mid_hook=lambda: exchange_sample(l))
        chk(6)
        exchange_load(l)
        out_proj_ln1(l, 0)
        chk(7)
        conv_ya(l, 1)
        ph = kv_sample(l)
        chk(8)
        attn_sample(l, ph)
        chk(9)
        out_proj_ln1(l, 1)
        chk(10)
        ffn(l)
        chk(11)
    except _Stop:
        pass

    for g, dst in ((0, o_yp), (1, o_ys)):
        r = Res("out")
        S.dma('sp', dst, X[g].ap, reads=X[g].all(), writes=[r])
        out_res.append(r)
    S.final_wait('sp', out_res)
    assert DEBUG == 2 or STOP or wstate['next_get'] == len(order), (wstate, len(order))

    with nc.Block() as block:
        @block.tensor
        def _(e):
            for f in S.q['pe'].ops:
                f(e)

        @block.scalar
        def _(e):
            for f in S.q['act'].ops:
                f(e)

        @block.vector
        def _(e):
            for f in S.q['dve'].ops:
                f(e)

        @block.gpsimd
        def _(e):
            for f in S.q['pool'].ops:
                f(e)

        @block.sync
        def _(e):
            for f in S.q['sp'].ops:
                f(e)
    es.close()
    return nc


def _partner32():
    p = np.zeros(32, dtype=np.int64)
    for r in range(32):
        r16 = r % 16
        p[r] = r - r16 + ((r16 + 8) % 16)
    return p


def _fm(w, ncol_chunks=None):
    K, C = w.shape
    return w.reshape(K // 128, 128, C // 128, 128).transpose(1, 2, 0, 3)


def pack_weights(inp):
    ws = np.zeros((128, WT_TOTAL), dtype=np.float32)
    part = _partner32()

    def put(key, arr):
        off, n = WT_OFFS[key]
        a = np.ascontiguousarray(arr, dtype=np.float32).reshape(arr.shape[0], -1)
        assert a.shape[1] == n, (key, a.shape, n)
        ws[:a.shape[0], off:off + n] = a

    for l in range(L):
        wada = inp["w_ada"][l]
        fm = _fm(wada)
        for t in range(12):
            put(('wada', l, t), fm[:, 4 * t:4 * t + 4])
        w_in = inp["w_in"][l]
        a_x, a_b, a_c = w_in[:, 0:256], w_in[:, 256:512], w_in[:, 512:768]
        d_q, d_k, d_v = w_in[:, 768:1024], w_in[:, 1024:1280], w_in[:, 1280:1536]
        m_cq, m_ckv, m_kpe = w_in[:, 1536:1920], w_in[:, 1920:2176], w_in[:, 2176:2208]
        perm256 = np.concatenate([b * 32 + part for b in range(8)])
        kpe_pad = np.zeros((1024, 128), np.float32)
        kpe_pad[:, 64:96] = m_kpe
        kpesw_pad = np.zeros((1024, 128), np.float32)
        kpesw_pad[:, 64:96] = m_kpe[:, part]
        cols = np.concatenate([a_x, a_b, a_c, d_q, d_k, m_cq, m_ckv, kpe_pad, d_q[:, perm256], d_k[:, perm256],
                               kpesw_pad, np.zeros((1024, 384), np.float32)], axis=1)
        assert cols.shape[1] == 24 * 128
        fm = _fm(cols)
        sorder = (S_ORDER_A + S_ORDER_B + [21, 21, 21])
        for t in range(6):
            put(('wins', l, t), fm[:, sorder[4 * t:4 * t + 4]])
        put(('wv', l), d_v.reshape(8, 128, 256).transpose(1, 0, 2))
        wuq = inp["w_uq"][l]
        put(('wuq', l), wuq.reshape(3, 128, 768).transpose(1, 0, 2))
        permq = np.arange(768)
        for h in range(8):
            permq[h * 96 + 64:h * 96 + 96] = h * 96 + 64 + part
        put(('wuqsw', l), wuq[:, permq].reshape(3, 128, 768).transpose(1, 0, 2))
        wukv = inp["w_ukv"][l].reshape(256, 8, 128)
        wk = wukv[:, :, 0:64].reshape(256, 512)
        wvv = wukv[:, :, 64:128].reshape(256, 512)
        both = np.concatenate([wk.reshape(2, 128, 512).transpose(1, 0, 2).reshape(128, 1024),
                               wvv.reshape(2, 128, 512).transpose(1, 0, 2).reshape(128, 1024)], axis=1)
        put(('wukv', l), both)
        wout = inp["w_out"][l]
        for t in range(4):
            blk = np.zeros((128, 2, 1792), np.float32)
            for ci in range(2):
                c = 2 * t + ci
                wc = wout[:, c * 128:(c + 1) * 128]
                blk[:, ci, 0:256] = wc[0:256].reshape(2, 128, 128).transpose(1, 0, 2).reshape(128, 256)
                blk[0:64, ci, 256:1792] = wc[256:1024].reshape(12, 64, 128).transpose(1, 0, 2).reshape(64, 1536)
            put(('wout', l, t), blk)
        f1 = _fm(inp["w_ff1"][l])
        f3 = _fm(inp["w_ff3"][l])
        for t in range(11):
            blk = np.stack([np.stack([f1[:, 2 * t + ji], f3[:, 2 * t + ji]], axis=1) for ji in range(2)], axis=1)
            put(('ffa', l, t), blk)
        f2 = inp["w_ff2"][l]
        for c in range(8):
            put(('ff2', l, c), f2[:, c * 128:(c + 1) * 128].reshape(NJ, 128, 128).transpose(1, 0, 2))
    return ws


def rope_tables(qidx):
    t = np.arange(NT) + qidx * NT
    row = (t // 64).astype(np.float32)
    col = (t % 64).astype(np.float32)
    freqs = (np.float32(10000.0) ** (-np.arange(8, dtype=np.float32) / np.float32(8))).astype(np.float32)
    tab = np.zeros((128, 2, NT), np.float32)
    for r in range(32):
        pos = row if r < 16 else col
        r16 = r % 16
        ang = (pos * freqs[r16 % 8]).astype(np.float32)
        cs = np.cos(ang).astype(np.float32)
        sn = np.sin(ang).astype(np.float32)
        tab[r, 0] = cs
        tab[r, 1] = -sn if r16 < 8 else sn
    for b in range(1, 4):
        tab[b * 32:(b + 1) * 32] = tab[0:32]
    return tab


def make_vec(inp, qidx):
    v = np.zeros((128, NV), np.float32)
    for l in range(L):
        o = l * VPL
        for tap in range(3):
            for c in range(2):
                v[:, o + V_CONV + tap * 2 + c] = inp["conv_w"][l, tap, c * 128:(c + 1) * 128]
        v[:, o + V_QNW:o + V_QNW + 3] = inp["q_norm_w"][l].reshape(3, 128).T
        v[:, o + V_KVW:o + V_KVW + 2] = inp["kv_norm_w"][l].reshape(2, 128).T
        v[:, o + V_LN1G:o + V_LN1G + 8] = inp["ln1_g"][l].reshape(8, 128).T
        v[:, o + V_LN1B:o + V_LN1B + 8] = inp["ln1_b"][l].reshape(8, 128).T
        v[:, o + V_LN2G:o + V_LN2G + 8] = inp["ln2_g"][l].reshape(8, 128).T
        v[:, o + V_LN2B:o + V_LN2B + 8] = inp["ln2_b"][l].reshape(8, 128).T
        v[0:64, o + V_DNW] = inp["diff_norm_w"][l]
        v[64:128, o + V_DNW] = inp["diff_norm_w"][l]
        ba = inp["b_ada"][l].reshape(48, 128).T
        v[:, o + V_BADA:o + V_BADA + 96] = np.repeat(ba[:, :, None], 2, axis=2).reshape(128, 96)
    for j in range(4):
        v[j * 32:(j + 1) * 32, V_MASK + j] = 1.0
    for r in range(4):
        v[:, V_HALO + r] = 1.0 if r == qidx - 1 else 0.0
        v[:, V_HALO + 4 + r] = 1.0 if r == qidx + 1 else 0.0
    return v


_NC_CACHE = {}


def make_in_maps(inp):
    ws = pack_weights(inp)
    in_maps = []
    for c in range(8):
        b, qi = c // 4, c % 4
        xp = inp["x_prompt"][2 * c:2 * c + 2].reshape(NT, 8, 128).transpose(2, 1, 0)
        xs = inp["x_sample"][b, qi * NT:(qi + 1) * NT].reshape(NT, 8, 128).transpose(2, 1, 0)
        cond = np.stack([inp["c_ctx"].reshape(8, 128).T, inp["c"][b].reshape(8, 128).T], axis=2)
        lamv = np.stack([np.stack([inp["lam_q1"][l], inp["lam_k1"][l], inp["lam_q2"][l], inp["lam_k2"][l]])
                         for l in range(L)]).reshape(1, -1)
        ck = inp["cache_diff_k"][b]
        ckT = ck.transpose(0, 1, 2, 4, 3).reshape(L, 2, 128, 512).transpose(0, 2, 1, 3)
        cv = inp["cache_diff_v"][b].transpose(0, 2, 1, 3).reshape(L, 4, 128, 256).transpose(0, 2, 1, 3)
        cckvT = inp["cache_mla_ckv"][b].transpose(0, 2, 1).reshape(L, 2, 128, 512).transpose(0, 2, 1, 3)
        ckpeT = inp["cache_mla_kpe"][b].transpose(0, 2, 1)
        in_maps.append({
            "xp": np.ascontiguousarray(xp, np.float32),
            "xs": np.ascontiguousarray(xs, np.float32),
            "cond": np.ascontiguousarray(cond, np.float32),
            "vec": make_vec(inp, qi),
            "rope": rope_tables(qi),
            "lamv": np.ascontiguousarray(np.repeat(lamv, 64, axis=0), np.float32),
            "ws": ws,
            "ckT": np.ascontiguousarray(ckT, np.float32),
            "cv": np.ascontiguousarray(cv, np.float32),
            "cckvT": np.ascontiguousarray(cckvT, np.float32),
            "ckpeT": np.ascontiguousarray(ckpeT, np.float32),
        })
    return in_maps


def assemble(R):
    y_p = np.zeros((16, 256, D), np.float32)
    y_s = np.zeros((2, 2048, D), np.float32)
    st_k = np.zeros((16, L, 4, 2, 256, 32), np.float32)
    st_v = np.zeros((16, L, 4, 256, 64), np.float32)
    st_ckv = np.zeros((16, L, 256, 256), np.float32)
    st_kpe = np.zeros((16, L, 256, 32), np.float32)
    shp = {"yp": (128, 8, NT), "ys": (128, 8, NT), "sk": (L, 128, 2, NT), "sv": (L, 128, 4, 256),
           "sckv": (L, 128, 2, NT), "skpe": (L, 32, NT)}
    for c in range(8):
        b, qi = c // 4, c % 4
        r = {k: np.asarray(R[c][k]).reshape(shp[k]) for k in shp}
        y_p[2 * c:2 * c + 2] = r["yp"].transpose(2, 1, 0).reshape(2, 256, D)
        y_s[b, qi * NT:(qi + 1) * NT] = r["ys"].transpose(2, 1, 0).reshape(NT, D)
        sk = r["sk"].transpose(0, 2, 1, 3).reshape(L, 4, 2, 32, 2, 256)
        st_k[2 * c:2 * c + 2] = sk.transpose(4, 0, 1, 2, 5, 3)
        sv = r["sv"].transpose(0, 2, 1, 3).reshape(L, 2, 256, 4, 64)
        st_v[2 * c:2 * c + 2] = sv.transpose(1, 0, 3, 2, 4)
        sc = r["sckv"].transpose(0, 2, 1, 3).reshape(L, 256, 2, 256)
        st_ckv[2 * c:2 * c + 2] = sc.transpose(2, 0, 3, 1)
        sp = r["skpe"].reshape(L, 32, 2, 256)
        st_kpe[2 * c:2 * c + 2] = sp.transpose(2, 0, 3, 1)
    return (y_p, y_s, st_k, st_v, st_ckv, st_kpe)


def kernel(**inputs):
    inp = {k: np.asarray(v) for k, v in inputs.items()}
    if 'nc' not in _NC_CACHE:
        _NC_CACHE['nc'] = build_program()
    nc = _NC_CACHE['nc']
    in_maps = make_in_maps(inp)
    res = run_bass_kernel_spmd(nc, in_maps, core_ids=list(range(8)))
    return assemble(res.results)
```

```python
import math
from contextlib import ExitStack

import numpy as np
import concourse.bass as bass
import concourse.mybir as mybir
from concourse.bass_utils import run_bass_kernel_spmd

F32 = mybir.dt.float32
BF16 = mybir.dt.bfloat16
AF = mybir.ActivationFunctionType
ALU = mybir.AluOpType
AX = mybir.AxisListType

D = 1024
L = 2
NT = 512
DFF = 2816
NJ = DFF // 128
ALPHA = (2 * L) ** 0.25
DIFF_SCALE = 32 ** -0.5
MLA_SCALE = 96 ** -0.5
SLABW = 3588
S_ORDER_A = [8, 9, 18, 19, 13, 14, 15, 20, 0, 1, 4, 5]
S_ORDER_B = [2, 3, 6, 7, 16, 17, 10, 11, 12]
DEBUG = False
STOP = 0
SKIP1 = 0
NKEY_S = 2560

VPL = 140
V_CONV, V_QNW, V_KVW, V_LN1G, V_LN1B, V_LN2G, V_LN2B, V_DNW, V_BADA = 0, 6, 9, 11, 19, 27, 35, 43, 44
V_MASK = 2 * VPL
V_HALO = V_MASK + 4
NV = V_HALO + 8

def _wtiles():
    tiles = []
    for l in range(L):
        for t in range(12):
            tiles.append((('wada', l, t), 4096))
    for l in range(L):
        for t in range(6):
            tiles.append((('wins', l, t), 4096))
        tiles.append((('wv', l), 2048))
        tiles.append((('wuq', l), 2304))
        tiles.append((('wuqsw', l), 2304))
        tiles.append((('wukv', l), 2048))
        for t in range(4):
            tiles.append((('wout', l, t), 3584))
        for t in range(11):
            tiles.append((('ffa', l, t), 4096))
        for t in range(8):
            tiles.append((('ff2', l, t), 2816))
    offs = {}
    o = 0
    for k, n in tiles:
        offs[k] = (o, n)
        o += n
    return tiles, offs, o


WT_TILES, WT_OFFS, WT_TOTAL = _wtiles()


def _consumption_order():
    order = []
    for t in range(4):
        order.append(('wada', 0, t))
    for l in range(L):
        for t in range(6):
            order.append(('wins', l, t))
        order += [('wv', l), ('wuq', l), ('wukv', l)]
        for t in range(4, 12):
            order.append(('wada', l, t))
        for t in range(3):
            order.append(('wins', l, t))
        order.append(('wv', l))
        for t in range(3, 6):
            order.append(('wins', l, t))
        order += [('wuq', l), ('wuqsw', l)]
        for t in range(4):
            order.append(('wout', l, t))
        order.append(('wukv', l))
        for t in range(4):
            order.append(('wout', l, t))
        for t in range(11):
            order.append(('ffa', l, t))
        if l + 1 < L:
            for t in range(4):
                order.append(('wada', l + 1, t))
        for g in range(2):
            for t in range(8):
                order.append(('ff2', l, t))
    return order


class Res:
    __slots__ = ('name', 'lw', 'rd', 'ov', 'lo', 'hi', 'excl')

    def __init__(self, name, lo=0, hi=0, excl=False):
        self.name = name
        self.excl = excl
        self.lw = None
        self.rd = {}
        self.ov = [self]
        self.lo = lo
        self.hi = hi


class Queue:
    def __init__(self, name, sems):
        self.name = name
        self.sems = sems
        self.epoch = 0
        self.count = 0
        self.ops = []
        self.waited = {}


EPOCH_LIMIT = 12000


class Sched:
    def __init__(self, nc, es):
        self.nc = nc
        self.semh = {}
        self.q = {}
        for qn in ('pe', 'act', 'dve', 'pool', 'sp'):
            sems = []
            for e in range(4):
                key = ('q', qn, e)
                self.semh[key] = es.enter_context(nc.semaphore(f"s_{qn}_{e}"))
                sems.append(key)
            self.q[qn] = Queue(qn, sems)
        self.ndma = 24
        self.dma_keys = []
        self.dma_val = []
        for i in range(self.ndma):
            key = ('d', i)
            self.semh[key] = es.enter_context(nc.semaphore(f"s_dma_{i}"))
            self.dma_keys.append(key)
            self.dma_val.append(0)
        self.dma_pools = {'sp': [0, list(range(0, 16))], 'pool': [0, list(range(16, 24))]}
        self.cc_key = ('cc',)
        self.semh[self.cc_key] = es.enter_context(nc.semaphore("s_cc"))
        self.cc_val = 0

    def _wait(self, q, key, val):
        if q.waited.get(key, 0) >= val:
            return
        if key[0] == 'q':
            pq = self.q[key[1]]
            if pq.sems[pq.epoch] == key and val > pq.count:
                raise RuntimeError(f"forward wait: {q.name} waits {key} >= {val} but only {pq.count} signalled")
        q.waited[key] = val
        h = self.semh[key]
        q.ops.append(lambda e, h=h, v=val: e.wait_ge(h, v))

    def _deps(self, q, reads, writes):
        deps = set()
        for r in reads:
            for o in r.ov:
                if o.lw is not None:
                    deps.add(o.lw)
                if o.excl:
                    for qn_, x in o.rd.items():
                        if qn_ != q.name:
                            deps.add(x)
        for w in writes:
            for o in w.ov:
                if o.lw is not None:
                    deps.add(o.lw)
                for x in o.rd.values():
                    deps.add(x)
        return deps

    def _record(self, myid, qname, reads, writes):
        for r in reads:
            r.rd[qname] = myid
        for w in writes:
            w.lw = myid
            w.rd = {}

    def emit(self, qn, fn, reads=(), writes=(), signal=True, skip_self=False):
        q = self.q[qn]
        for (key, val) in self._deps(q, reads, writes):
            if skip_self and key[0] == 'q' and key[1] == qn:
                continue
            self._wait(q, key, val)
        if q.count >= EPOCH_LIMIT:
            q.epoch += 1
            q.count = 0
        key = q.sems[q.epoch]
        myid = (key, q.count + 1)
        if signal:
            q.count += 1
            h = self.semh[key]
            q.ops.append(lambda e, fn=fn, h=h: fn(e).then_inc(h, 1))
        else:
            q.ops.append(lambda e, fn=fn: fn(e))
        self._record(myid, qn, reads, writes)

    def dma(self, qn, out, in_, reads=(), writes=()):
        q = self.q[qn]
        for (key, val) in self._deps(q, reads, writes):
            self._wait(q, key, val)
        st = self.dma_pools[qn]
        i = st[1][st[0] % len(st[1])]
        st[0] += 1
        key = self.dma_keys[i]
        if self.dma_val[i] > 0:
            self._wait(q, key, self.dma_val[i])
        self.dma_val[i] += 16
        myid = (key, self.dma_val[i])
        h = self.semh[key]
        q.ops.append(lambda e, o=out, i_=in_, h=h: e.dma_start(out=o, in_=i_).then_inc(h, 16))
        self._record(myid, 'dma%d' % i, reads, writes)
        return myid

    def collective(self, fn, reads=(), writes=()):
        q = self.q['pool']
        for (key, val) in self._deps(q, reads, writes):
            self._wait(q, key, val)
        self.cc_val += 1
        myid = (self.cc_key, self.cc_val)
        h = self.semh[self.cc_key]
        q.ops.append(lambda e, fn=fn, h=h: fn(e).then_inc(h))
        self._record(myid, 'cc', reads, writes)

    def final_wait(self, qn, res_list):
        q = self.q[qn]
        for r in res_list:
            for o in r.ov:
                if o.lw is not None:
                    self._wait(q, o.lw[0], o.lw[1])


class Tile:
    def __init__(self, ap, nres, name, lo=None, nbytes=None, arena=None, ranges=None):
        self.ap = ap
        self.name = name
        self.res = []
        if ranges is not None:
            for i, (a, b) in enumerate(ranges):
                self.res.append(Res(f"{name}.{i}", lo + a, lo + b))
            nres = 0
        for i in range(nres):
            if lo is None:
                self.res.append(Res(f"{name}.{i}"))
            else:
                sz = nbytes // nres
                self.res.append(Res(f"{name}.{i}", lo + i * sz, lo + (i + 1) * sz))
        if arena is not None:
            for r in self.res:
                for o in arena:
                    if o.lo < r.hi and r.lo < o.hi:
                        o.ov.append(r)
                        r.ov.append(o)
                arena.append(r)

    def r(self, i=0):
        return self.res[i]

    def all(self):
        return list(self.res)


DT_SIZE = {F32: 4, BF16: 2}


def build_program():
    nc = bass.Bass("TRN2", target_bir_lowering=False)
    es = ExitStack()
    S = Sched(nc, es)

    def din(name, shape, dt=F32):
        return nc.dram_tensor(name, list(shape), dt, kind="ExternalInput").ap()

    def dout(name, shape, dt=F32):
        return nc.dram_tensor(name, list(shape), dt, kind="ExternalOutput").ap()

    d_xp = din("xp", [128, 8, NT])
    d_xs = din("xs", [128, 8, NT])
    d_cond = din("cond", [128, 8, 2])
    d_vec = din("vec", [128, NV])
    d_rope = din("rope", [128, 2, NT])
    d_lamv = din("lamv", [64, L * 4 * 32])
    d_ws = din("ws", [128, WT_TOTAL])
    d_ckT = din("ckT", [L, 128, 2, 512])
    d_cv = din("cv", [L, 128, 4, 256])
    d_cckvT = din("cckvT", [L, 128, 2, 512])
    d_ckpeT = din("ckpeT", [L, 32, 512])

    o_yp = dout("yp", [128, 8, NT])
    o_ys = dout("ys", [128, 8, NT])
    o_sk = dout("sk", [L, 128, 2, NT])
    o_sv = dout("sv", [L, 128, 4, 256])
    o_sckv = dout("sckv", [L, 128, 2, NT])
    o_skpe = dout("skpe", [L, 32, NT])
    out_res = []
    o_dbg = dout("dbg", [8, 128, 8, NT]) if DEBUG else None
    dbg_n = [0]

    o_dbgb = dout("dbgb", [12, 128, 6144], BF16) if DEBUG else None
    dbgb_n = [0]

    def tapb(ap2d, parts, n, reads):
        if not DEBUG:
            return
        r = Res("out")
        S.dma('sp', o_dbgb[dbgb_n[0], 0:parts, 0:n], ap2d, reads=reads, writes=[r])
        out_res.append(r)
        dbgb_n[0] += 1

    def tap(tile):
        if not DEBUG:
            return
        r = Res("out")
        S.dma('sp', o_dbg[dbg_n[0]], tile.ap, reads=tile.all(), writes=[r])
        out_res.append(r)
        dbg_n[0] += 1

    slab_in = [nc.dram_tensor(f"slab_in{l}", [128, SLABW], BF16) for l in range(L)]
    slab_all = [nc.dram_tensor(f"slab_all{l}", [512, SLABW], BF16) for l in range(L)]
    slab_in_res = [Res(f"slab_in{l}") for l in range(L)]
    slab_all_res = [Res(f"slab_all{l}") for l in range(L)]

    def sb(name, shape, dt):
        return es.enter_context(nc.sbuf_tensor(name, list(shape), dt))

    def T(name, shape, dt, nres=1):
        return Tile(sb(name, shape, dt)[:], nres, name)

    X = [T("XP", [128, 8, NT], F32, 8), T("XS", [128, 8, NT], F32, 8)]
    HT0 = T("HT0", [128, 8, NT], BF16, 8)
    NRING = 4
    RING = [T(f"RING{i}", [128, 4096], BF16, 1) for i in range(NRING)]
    QM = T("QM", [128, 2, 4, NT], BF16, 8)
    D0 = T("D0", [64, NT], F32, 1)
    QMH = T("QMH", [96, 8, NT], BF16, 8)
    YA = T("YA", [128, 2, NT], BF16, 2)
    YH = T("YH", [64, 12, NT], BF16, 12)
    ROPE = T("ROPE", [128, 2, NT], F32, 1)
    VEC = T("VEC", [128, NV], F32, 1)
    CND = T("CND", [128, 8, 2], F32, 1)
    SCB = T("SCB", [128, 8, 2], BF16, 1)
    MOD = T("MOD", [128, L * 6, 8, 2], F32, L * 6)
    LAMV = T("LAMV", [64, L * 4 * 32], F32, 1)
    LAMT = T("LAMT", [64, 2 * L * 32], F32, 1)
    LAMS = T("LAMS", [64, 2 * L], F32, 1)
    LAME = T("LAME", [64, 2 * L], F32, 1)
    NEGLAM = T("NEGLAM", [64, L], F32, 1)
    DNW = T("DNW", [64, L], F32, 1)
    ONES = T("ONES", [128, 128], BF16, 1)
    ET = [T(f"ET{i}", [128, NT], BF16, 1) for i in range(4)]
    TF = [T(f"TF{i}", [128, NT], F32, 1) for i in range(5)]
    TB = [T(f"TB{i}", [128, NT], BF16, 1) for i in range(4)]
    AB = T("AB", [128, 2, NT], F32, 2)
    UP = T("UP", [128, 2, 2, 258], F32, 2)
    US = T("US", [128, 2, 1, 514], F32, 2)
    UBALL = T("UBALL", [128, 4, 4], BF16, 1)
    UBF = T("UBF", [128, 4, 4], F32, 1)

    REG_BYTES = 67072
    reg = sb("REG", [128, REG_BYTES // 4], F32)
    arena = []

    def RT(name, off, shape, dt, nres=1, parts=128, ranges=None):
        n = 1
        for s_ in shape:
            n *= s_
        nbytes = n * DT_SIZE[dt]
        assert off % 4 == 0 and nbytes % 4 == 0 and off + nbytes <= REG_BYTES, (name, off, nbytes)
        v = reg[0:parts, off // 4:(off + nbytes) // 4]
        if dt == BF16:
            v = v.bitcast(BF16)
        if len(shape) == 2:
            v = v.rearrange("p (a b) -> p a b", a=shape[0])
        elif len(shape) == 3:
            v = v.rearrange("p (a b c) -> p a b c", a=shape[0], b=shape[1])
        return Tile(v, nres, name, lo=off, nbytes=nbytes, arena=arena, ranges=ranges), off + nbytes

    o = 0
    AXT, o = RT("AXT", o, [2, NT], F32, 2)
    CQ, o = RT("CQ", o, [3, NT], F32, 3)
    CKV, o = RT("CKV", o, [2, NT], F32, 2)
    CQN, o = RT("CQN", o, [3, NT], BF16, 3)
    QR, o = RT("QR", o, [2, NT], F32, 2)
    KR, o = RT("KR", o, [2, NT], F32, 2)
    KPR, o = RT("KPR", o, [NT], F32, 1)
    _sec = [(0, 512), (512, 1024), (1024, 1536), (1536, 2048), (2048, 2304), (2304, 2560), (2560, 2816),
            (2816, 3072), (3072, 3584), (3584, 3588)]
    SLAB, o = RT("SLAB", o, [SLABW], BF16, 1, ranges=[(2 * a, 2 * b) for a, b in _sec])
    SL_K, SL_CKV, SL_V, SL_KPE, SL_UB = 0, 2, 4, 8, 9
    KMP, o = RT("KMP", o, [8, NT], BF16, 8, parts=96)
    VMP, o = RT("VMP", o, [4, 8, 65], BF16, 4)
    VAP, o = RT("VAP", o, [4, 4, 65], BF16, 4)
    KTP, o = RT("KTP", o, [2, NT], BF16, 2)
    CKVNP, o = RT("CKVNP", o, [2, NT], BF16, 2)
    assert o <= REG_BYTES, o
    D1, o = RT("D1", o, [NT], F32, 1, parts=64)
    RS1, o = RT("RS1", o, [NT], F32, 1, parts=64)
    RL1, o = RT("RL1", o, [NT], F32, 1, parts=64)
    assert o <= REG_BYTES, o
    o = 0
    KTA, o = RT("KTA", o, [2, NKEY_S], BF16, 10)
    VAA, o = RT("VAA", o, [20, 4, 65], BF16, 20)
    CKVTA, o = RT("CKVTA", o, [2, NKEY_S], BF16, 10)
    KMH0, o = RT("KMH0", o, [NKEY_S], BF16, 1, parts=96)
    KMH1, o = RT("KMH1", o, [NKEY_S], BF16, 1, parts=96)
    KMH = [KMH0, KMH1]
    VMA, o = RT("VMA", o, [20, 8, 65], BF16, 20)
    KPEA, o = RT("KPEA", o, [NKEY_S], BF16, 5, parts=96)
    assert o <= REG_BYTES, o
    o = 0
    G0, o = RT("G0", o, [NJ, NT], BF16, NJ)
    G1, o = RT("G1", o, [NJ, NT], BF16, NJ)
    HT1, o = RT("HT1", o, [8, NT], BF16, 8)
    assert o <= REG_BYTES, o
    GT = [G0, G1]
    HT = [HT0, HT1]

    psum = es.enter_context(nc.psum_tensor("PS", [128, 8, 512], F32))
    PSR = [Res(f"bank{i}", excl=True) for i in range(8)]
    bank_rr = {'G': [0, list(range(8))], 'S': [0, [0, 1, 2]], 'PV': [0, [3, 4, 5]], 'M': [0, [6, 7]],
               'S4': [0, [0, 1, 2, 3]], 'PV2': [0, [4, 5]]}

    def bank(pool='G'):
        st = bank_rr[pool]
        b = st[1][st[0] % len(st[1])]
        st[0] += 1
        return b

    tf_rr = [0]

    def tf():
        t = TF[tf_rr[0] % len(TF)]
        tf_rr[0] += 1
        return t

    tb_rr = [0]

    def tb():
        t = TB[tb_rr[0] % len(TB)]
        tb_rr[0] += 1
        return t

    et_rr = [0]

    def et():
        t = ET[et_rr[0] % len(ET)]
        et_rr[0] += 1
        return t

    def mm(b, out_ap, lhsT, rhs, reads, start, stop, sig=False):
        S.emit('pe', lambda e: e.matmul(out_ap, lhsT=lhsT, rhs=rhs, start=start, stop=stop),
               reads=reads, writes=[PSR[b]], signal=(stop or sig), skip_self=True)

    def act(out, in_, func, reads, writes, scale=1.0, bias=0.0):
        S.emit('act', lambda e: e.activation(out=out, in_=in_, func=func, bias=bias, scale=scale),
               reads=reads, writes=writes)

    def tt(eng, out, in0, in1, op, reads, writes):
        S.emit(eng, lambda e: e.tensor_tensor(out=out, in0=in0, in1=in1, op=op), reads=reads, writes=writes)

    def ts(eng, out, in0, s1, s2, op0, op1, reads, writes):
        if op1 is None:
            S.emit(eng, lambda e: e.tensor_scalar(out=out, in0=in0, scalar1=s1, scalar2=None, op0=op0),
                   reads=reads, writes=writes)
        else:
            S.emit(eng, lambda e: e.tensor_scalar(out=out, in0=in0, scalar1=s1, scalar2=s2, op0=op0, op1=op1),
                   reads=reads, writes=writes)

    def stt(out, in0, scalar, in1, op0, op1, reads, writes):
        S.emit('dve', lambda e: e.scalar_tensor_tensor(out=out, in0=in0, scalar=scalar, in1=in1, op0=op0, op1=op1),
               reads=reads, writes=writes)

    def cp(eng, out, in_, reads, writes):
        if eng == 'act':
            S.emit('act', lambda e: e.copy(out=out, in_=in_), reads=reads, writes=writes)
        else:
            S.emit(eng, lambda e: e.tensor_copy(out=out, in_=in_), reads=reads, writes=writes)

    def memset(eng, ap, val, writes):
        S.emit(eng, lambda e: e.memset(ap, val), writes=writes)

    vres = VEC.all()

    def vcol(i, parts=128, p0=0):
        return VEC.ap[p0:p0 + parts, i:i + 1]

    class WView:
        def __init__(self, tile, idx):
            self.tile = tile
            self.idx = idx
            self.ap = tile.ap

        def all(self):
            assert self.tile.cur == self.idx, ("weight ring slot recycled while in use", self.idx, self.tile.cur)
            return self.tile.all()

    order = _consumption_order()
    wstate = {'next_emit': 0, 'next_get': 0}

    def w_emit_upto(n):
        while wstate['next_emit'] < min(n, len(order)):
            i = wstate['next_emit']
            key = order[i]
            off, ne = WT_OFFS[key]
            rb = RING[i % NRING]
            if ne > 2048:
                half = ne // 2
                src = d_ws[:, off:off + ne].rearrange("p (a b) -> p a b", b=half)
                dst = rb.ap[:, 0:ne].rearrange("p (a b) -> p a b", b=half)
            else:
                src = d_ws[:, off:off + ne]
                dst = rb.ap[:, 0:ne]
            S.dma('pool', dst, src, reads=[], writes=rb.all())
            rb.cur = i
            wstate['next_emit'] += 1

    def wget(expect):
        i = wstate['next_get']
        assert order[i][0] == expect, (order[i], expect)
        w_emit_upto(i + NRING - 1)
        wstate['next_get'] += 1
        return WView(RING[i % NRING], i)

    S.dma('sp', X[0].ap, d_xp, writes=X[0].all())
    S.dma('sp', X[1].ap, d_xs, writes=X[1].all())
    S.dma('sp', CND.ap, d_cond, writes=CND.all())
    S.dma('sp', VEC.ap, d_vec, writes=VEC.all())
    S.dma('sp', ROPE.ap, d_rope, writes=ROPE.all())
    S.dma('sp', LAMV.ap, d_lamv, writes=LAMV.all())
    w_emit_upto(3)

    memset('dve', ONES.ap, 1.0, ONES.all())
    nreg = REG_BYTES // 4
    for a0 in range(0, nreg, 2096):
        memset('dve', reg[:, a0:min(nreg, a0 + 2096)], 0.0, list(arena))
    memset('dve', UP.ap, 0.0, UP.all())
    memset('dve', US.ap, 0.0, US.all())
    COS = ROPE.ap[:, 0, :]
    SIN = ROPE.ap[:, 1, :]

    act(SCB.ap, CND.ap, AF.Silu, CND.all(), SCB.all())
    def mod_piece(l, t0, t1, pool='G'):
        b = bank(pool)
        for t in range(t0, t1):
            W = wget('wada')
            wv = W.ap[:, 0:4096].rearrange("p (j k c) -> p j k c", j=4, k=8)
            for ji in range(4):
                j = t * 4 + ji
                for k in range(8):
                    mm(b, psum[:, b, 2 * j:2 * j + 2], wv[:, ji, k, :], SCB.ap[:, k, :],
                       W.all() + SCB.all(), start=(k == 0), stop=(k == 7))
        for v in range(t0 // 2, t1 // 2):
            mr = MOD.r(l * 6 + v)
            tt('dve', MOD.ap[:, l * 6 + v].rearrange("p j c -> p (j c)"), psum[:, b, 16 * v:16 * v + 16],
               VEC.ap[:, l * VPL + V_BADA + 16 * v:l * VPL + V_BADA + 16 * v + 16], ALU.add, [PSR[b]] + vres, [mr])
            if v in (1, 4):
                ts('dve', MOD.ap[:, l * 6 + v], MOD.ap[:, l * 6 + v], 1.0, None, ALU.add, None, [mr], [mr])

    mod_piece(0, 0, 4)

    def modv(l, v, ch, g):
        return MOD.ap[:, l * 6 + v, ch, g:g + 1]

    lv = LAMV.ap.rearrange("p (l f d) -> p l f d", l=L, f=4)
    lt = LAMT.ap.rearrange("p (l t d) -> p l t d", l=L, t=2)
    for l in range(L):
        for t_ in range(2):
            tt('dve', lt[:, l, t_, :], lv[:, l, 2 * t_, :], lv[:, l, 2 * t_ + 1, :], ALU.mult,
               LAMV.all(), LAMT.all())
    S.emit('dve', lambda e: e.tensor_reduce(out=LAMS.ap, in_=LAMT.ap.rearrange("p (a d) -> p a d", d=32),
                                            axis=AX.X, op=ALU.add), reads=LAMT.all(), writes=LAMS.all())
    act(LAME.ap, LAMS.ap, AF.Exp, LAMS.all(), LAME.all())
    for l in range(L):
        lam_init = 0.8 - 0.6 * math.exp(-0.3 * l)
        tt('dve', NEGLAM.ap[:, l:l + 1], LAME.ap[:, 2 * l + 1:2 * l + 2], LAME.ap[:, 2 * l:2 * l + 1],
           ALU.subtract, LAME.all(), NEGLAM.all())
        ts('dve', NEGLAM.ap[:, l:l + 1], NEGLAM.ap[:, l:l + 1], -lam_init, None, ALU.add, None,
           NEGLAM.all(), NEGLAM.all())
        ts('dve', DNW.ap[:, l:l + 1], VEC.ap[0:64, l * VPL + V_DNW:l * VPL + V_DNW + 1], 1.0 - lam_init, None,
           ALU.mult, None, vres, DNW.all())

    class _Stop(Exception):
        pass

    def chk(stage):
        if STOP and stage >= STOP - 1e-9:
            raise _Stop()

    def modulate(l, g, which, dst):
        vs, vh = (1, 0) if which == 1 else (4, 3)
        for ch in range(8):
            if ch % 2 == 0:
                ts('dve', dst.ap[:, ch, :], X[g].ap[:, ch, :], modv(l, vs, ch, g), modv(l, vh, ch, g),
                   ALU.mult, ALU.add, [X[g].r(ch), MOD.r(l * 6 + vs), MOD.r(l * 6 + vh)], [dst.r(ch)])
            else:
                act(dst.ap[:, ch, :], X[g].ap[:, ch, :], AF.Identity, [X[g].r(ch), MOD.r(l * 6 + vs), MOD.r(l * 6 + vh)], [dst.r(ch)],
                    scale=modv(l, vs, ch, g), bias=modv(l, vh, ch, g))

    def rstd_from_sumsq(b, parts, n, ncols, eps):
        t1 = tf()
        act(t1.ap[0:parts, 0:ncols], psum[0:parts, b, 0:ncols], AF.Ln, [PSR[b]], t1.all(), scale=1.0 / n, bias=eps)
        t2 = tf()
        act(t2.ap[0:parts, 0:ncols], t1.ap[0:parts, 0:ncols], AF.Exp, t1.all(), t2.all(), scale=-0.5)
        return t2

    def layer_norm(l, g, vg, vb):
        bs, bq = bank('G'), bank('G')
        for ch in range(8):
            zb, zs = tb(), tb()
            cp('dve', zb.ap, X[g].ap[:, ch, :], [X[g].r(ch)], zb.all())
            act(zs.ap, X[g].ap[:, ch, :], AF.Square, [X[g].r(ch)], zs.all())
            mm(bs, psum[:, bs, :], ONES.ap, zb.ap, ONES.all() + zb.all(), start=(ch == 0), stop=(ch == 7), sig=True)
            mm(bq, psum[:, bq, :], ONES.ap, zs.ap, ONES.all() + zs.all(), start=(ch == 0), stop=(ch == 7), sig=True)
        mean = tf()
        act(mean.ap, psum[:, bs, :], AF.Copy, [PSR[bs]], mean.all(), scale=1.0 / D)
        msq = tf()
        tt('dve', msq.ap, mean.ap, mean.ap, ALU.mult, mean.all(), msq.all())
        var = tf()
        stt(var.ap, psum[:, bq, :], 1.0 / D, msq.ap, ALU.mult, ALU.subtract, [PSR[bq]] + msq.all(), var.all())
        lnv = tf()
        act(lnv.ap, var.ap, AF.Ln, var.all(), lnv.all(), bias=1e-5)
        rstd = tf()
        act(rstd.ap, lnv.ap, AF.Exp, lnv.all(), rstd.all(), scale=-0.5)
        for ch in range(8):
            xr = X[g].r(ch)
            xa = X[g].ap[:, ch, :]
            tt('dve', xa, xa, mean.ap, ALU.subtract, [xr] + mean.all(), [xr])
            tt('dve', xa, xa, rstd.ap, ALU.mult, [xr] + rstd.all(), [xr])
            act(xa, xa, AF.Identity, [xr] + vres, [xr], scale=vcol(l * VPL + vg + ch), bias=vcol(l * VPL + vb + ch))

    def stage_out(dst_dram, src_ap, reads, parts=128, ncols=NT):
        st = tf()
        cp('dve', st.ap[0:parts, 0:ncols], src_ap, reads, st.all())
        r = Res("out")
        S.dma('sp', dst_dram, st.ap[0:parts, 0:ncols], reads=st.all(), writes=[r])
        out_res.append(r)

    def rope_finish(dst_ap, part_ap, psw_ap, reads, writes, p0, p1):
        t1 = tf()
        tt('dve', t1.ap[p0:p1], psw_ap, SIN[p0:p1], ALU.mult, reads + ROPE.all(), t1.all())
        tt('dve', dst_ap, t1.ap[p0:p1], part_ap, ALU.add, t1.all() + reads, writes)

    premod = set()

    def mixer_project(l, g, mid_hook=None):
        H = HT[0]
        if (l, g) not in premod:
            modulate(l, g, 1, H)
        isS = (g == 1)
        U = US if isS else UP
        st_ = {'statq': None, 'statk': None}
        corder = S_ORDER_A + S_ORDER_B
        for pos, c in enumerate(corder):
            if isS and pos == len(S_ORDER_A):
                v_token_major(l, g, H)
                mid_hook()
            if pos % 4 == 0:
                W = wget('wins')
                wv = W.ap[:, 0:4096].rearrange("p (j k c) -> p j k c", j=4, k=8)
            if c >= 16 and not isS:
                continue
            b = bank('G')
            pb = psum[:, b, :]
            for k in range(8):
                mm(b, pb, wv[:, pos % 4, k, :], H.ap[:, k, :], W.all() + [H.r(k)], start=(k == 0), stop=(k == 7))
            R = [PSR[b]]
            statq, statk = st_['statq'], st_['statk']
            if c in (0, 1):
                cp('act', AXT.ap[:, c, :], pb, R, [AXT.r(c)])
            elif c in (2, 3):
                cp('act', AB.ap[:, c - 2, :], pb, R, [AB.r(c - 2)])
            elif c in (4, 5):
                cc = c - 4
                uin = U.ap[:, cc, :, 1:1 + (NT if isS else 256)]
                tt('dve', uin, pb.rearrange("p (s t) -> p s t", s=(1 if isS else 2)),
                   AXT.ap[:, cc, :].rearrange("p (s t) -> p s t", s=(1 if isS else 2)), ALU.mult,
                   R + [AXT.r(cc)], [U.r(cc)])
            elif c in (6, 7):
                cc = c - 6
                if isS:
                    tt('dve', QR.ap[:, cc, :], pb, COS, ALU.mult, R + ROPE.all(), [QR.r(cc)])
                else:
                    for j in range(4):
                        act(QM.ap[:, cc, j, :], pb, AF.Identity, R + vres, [QM.r(cc * 4 + j)], scale=vcol(V_MASK + j))
            elif c in (8, 9):
                cc = c - 8
                if isS:
                    tt('dve', KR.ap[:, cc, :], pb, COS, ALU.mult, R + ROPE.all(), [KR.r(cc)])
                else:
                    cp('act', KTP.ap[:, cc, :], pb, R, [KTP.r(cc)])
                    if not SKIP1:
                        stage_out(o_sk[l, :, cc, :], pb, R)
            elif c in (10, 11, 12):
                cc = c - 10
                cp('act', CQ.ap[:, cc, :], pb, R, [CQ.r(cc)])
                sq = tb()
                act(sq.ap, pb, AF.Square, R, sq.all())
                if cc == 0:
                    statq = st_['statq'] = bank('G')
                mm(statq, psum[:, statq, :], ONES.ap, sq.ap, ONES.all() + sq.all(), start=(cc == 0), stop=(cc == 2), sig=True)
                if cc == 2:
                    rs = rstd_from_sumsq(statq, 128, 384.0, NT, 1e-6)
                    for c3 in range(3):
                        stt(CQN.ap[:, c3, :], CQ.ap[:, c3, :], vcol(l * VPL + V_QNW + c3), rs.ap, ALU.mult, ALU.mult,
                            [CQ.r(c3)] + vres + rs.all(), [CQN.r(c3)])
            elif c in (13, 14):
                cc = c - 13
                cp('act', CKV.ap[:, cc, :], pb, R, [CKV.r(cc)])
                sq = tb()
                act(sq.ap, pb, AF.Square, R, sq.all())
                if cc == 0:
                    statk = st_['statk'] = bank('G')
                mm(statk, psum[:, statk, :], ONES.ap, sq.ap, ONES.all() + sq.all(), start=(cc == 0), stop=(cc == 1), sig=True)
                if cc == 1:
                    rs = rstd_from_sumsq(statk, 128, 256.0, NT, 1e-6)
                    for c2 in range(2):
                        if isS:
                            stt(SLAB.ap[:, 1024 + c2 * NT:1024 + (c2 + 1) * NT], CKV.ap[:, c2, :],
                                vcol(l * VPL + V_KVW + c2), rs.ap, ALU.mult, ALU.mult,
                                [CKV.r(c2)] + vres + rs.all(), [SLAB.r(SL_CKV + c2)])
                        else:
                            stt(CKV.ap[:, c2, :], CKV.ap[:, c2, :], vcol(l * VPL + V_KVW + c2), rs.ap,
                                ALU.mult, ALU.mult, [CKV.r(c2)] + vres + rs.all(), [CKV.r(c2)])
                            cp('act', CKVNP.ap[:, c2, :], CKV.ap[:, c2, :], [CKV.r(c2)], [CKVNP.r(c2)])
                            r = Res("out")
                            S.dma('sp', o_sckv[l, :, c2, :], CKV.ap[:, c2, :], reads=[CKV.r(c2)], writes=[r])
                            out_res.append(r)
            elif c == 15:
                if isS:
                    tt('dve', KPR.ap[64:96, :], pb[64:96], COS[64:96], ALU.mult, R + ROPE.all(), KPR.all())
                else:
                    for h in range(8):
                        cp('act' if h % 2 else 'dve', KMP.ap[64:96, h, :], pb[64:96], R, [KMP.r(h)])
                    st = tf()
                    cp('dve', st.ap[64:96, :], pb[64:96], R, st.all())
                    r = Res("out")
                    S.dma('sp', o_skpe[l], st.ap[64:96, :], reads=st.all(), writes=[r])
                    out_res.append(r)
            elif c in (16, 17):
                cc = c - 16
                rope_finish(QR.ap[:, cc, :], QR.ap[:, cc, :], pb, R + [QR.r(cc)], [QR.r(cc)], 0, 128)
                for j in range(4):
                    act(QM.ap[:, cc, j, :], QR.ap[:, cc, :], AF.Identity, [QR.r(cc)] + vres, [QM.r(cc * 4 + j)],
                        scale=vcol(V_MASK + j))
            elif c in (18, 19):
                cc = c - 18
                rope_finish(SLAB.ap[:, cc * NT:(cc + 1) * NT], KR.ap[:, cc, :], pb, R + [KR.r(cc)], [SLAB.r(SL_K + cc)], 0, 128)
            elif c == 20:
                rope_finish(SLAB.ap[64:96, 3072:3072 + NT], KPR.ap[64:96, :], pb[64:96], R + KPR.all(), [SLAB.r(SL_KPE)], 64, 96)
            chk((1 if g == 0 else 5) + (c + 1) / 100.0)
        if not isS:
            v_token_major(l, g, H)
        uq_project(l, g)

    def v_token_major(l, g, H):
        isS = (g == 1)
        W = wget('wv')
        wvv = W.ap[:, 0:2048].rearrange("p (k c) -> p k c", k=8)
        for t4 in range(4):
            b = bank('G')
            for k in range(8):
                mm(b, psum[:, b, 0:256], H.ap[:, k, t4 * 128:(t4 + 1) * 128], wvv[:, k, :], W.all() + [H.r(k)],
                   start=(k == 0), stop=(k == 7))
            if isS:
                cp('act', SLAB.ap[:, 2048 + t4 * 256:2048 + (t4 + 1) * 256], psum[:, b, 0:256], [PSR[b]], [SLAB.r(SL_V + t4)])
            else:
                cp('act', VAP.ap[:, t4, :, 0:64], psum[:, b, 0:256].rearrange("p (h d) -> p h d", h=4), [PSR[b]], [VAP.r(t4)])
                st = tf()
                cp('dve', st.ap[:, 0:256], psum[:, b, 0:256], [PSR[b]], st.all())
                r = Res("out")
                S.dma('sp', o_sv[l, :, t4, :], st.ap[:, 0:256], reads=st.all(), writes=[r])
                out_res.append(r)
        if isS:
            for cc in range(2):
                cp('dve', SLAB.ap[:, 3584 + 2 * cc:3584 + 2 * cc + 1], US.ap[:, cc, 0, 1:2], [US.r(cc)], [SLAB.r(SL_UB)])
                cp('dve', SLAB.ap[:, 3584 + 2 * cc + 1:3584 + 2 * cc + 2], US.ap[:, cc, 0, NT:NT + 1], [US.r(cc)], [SLAB.r(SL_UB)])

    def uq_project(l, g):
        isS = (g == 1)
        W = wget('wuq')
        wq = W.ap[:, 0:2304].rearrange("p (k c) -> p k c", k=3)
        if isS:
            W2 = wget('wuqsw')
            wq2 = W2.ap[:, 0:2304].rearrange("p (k c) -> p k c", k=3)
        for h in range(8):
            b = bank('G')
            for k in range(3):
                mm(b, psum[0:96, b, :], wq[:, k, h * 96:(h + 1) * 96], CQN.ap[:, k, :], W.all() + [CQN.r(k)],
                   start=(k == 0), stop=(k == 2))
            if isS:
                b2 = bank('G')
                for k in range(3):
                    mm(b2, psum[0:96, b2, :], wq2[:, k, h * 96:(h + 1) * 96], CQN.ap[:, k, :], W2.all() + [CQN.r(k)],
                       start=(k == 0), stop=(k == 2))
                cp('act', QMH.ap[0:64, h, :], psum[0:64, b, :], [PSR[b]], [QMH.r(h)])
                t0 = tf()
                tt('dve', t0.ap[64:96], psum[64:96, b, :], COS[64:96], ALU.mult, [PSR[b]] + ROPE.all(), t0.all())
                rope_finish(QMH.ap[64:96, h, :], t0.ap[64:96], psum[64:96, b2, :], [PSR[b2]] + t0.all(), [QMH.r(h)], 64, 96)
            else:
                cp('act' if h % 2 else 'dve', QMH.ap[0:96, h, :], psum[0:96, b, :], [PSR[b]], [QMH.r(h)])

    def conv_ya(l, g):
        isS = (g == 1)
        U = US if isS else UP
        ns, sl = (1, NT) if isS else (2, 256)
        for cc in range(2):
            t1 = tf()
            t1v = t1.ap.rearrange("p (s t) -> p s t", s=ns)
            ts('dve', t1v, U.ap[:, cc, :, 0:sl], vcol(l * VPL + V_CONV + 0 * 2 + cc), None, ALU.mult, None,
               [U.r(cc)] + vres, t1.all())
            stt(t1v, U.ap[:, cc, :, 1:1 + sl], vcol(l * VPL + V_CONV + 1 * 2 + cc), t1v, ALU.mult, ALU.add,
                [U.r(cc)] + vres + t1.all(), t1.all())
            stt(t1v, U.ap[:, cc, :, 2:2 + sl], vcol(l * VPL + V_CONV + 2 * 2 + cc), t1v, ALU.mult, ALU.add,
                [U.r(cc)] + vres + t1.all(), t1.all())
            tt('dve', YA.ap[:, cc, :], t1.ap, AB.ap[:, cc, :], ALU.mult, t1.all() + [AB.r(cc)], [YA.r(cc)])

    def attention(l, g, segs, kt_of, kside_d, vside_d, kside_m, vside_m, prep_head=None, hooks=None):
        units = []
        for hm in range(16):
            for (q0, nq, kts) in segs:
                if hm < 8:
                    units.append(dict(d=True, h=hm // 2, m=hm % 2, c=hm // 4, j=hm % 4, q0=q0, nq=nq, kts=kts))
                else:
                    units.append(dict(d=False, h=hm - 8, q0=q0, nq=nq, kts=kts))
        stages = [(ui, i) for ui, u in enumerate(units) for i in range(len(u['kts']))]
        short_units = min(len(u['kts']) for u in units) < 16
        LAG = 2 if short_units else 3
        spool, pvpool = ('S', 'PV') if short_units else ('S4', 'PV2')
        nst = len(stages)
        first_unit_of_head = {}
        for ui, u in enumerate(units):
            if not u['d'] and u['h'] not in first_unit_of_head:
                first_unit_of_head[u['h']] = ui
        prep_at = {}
        if prep_head is not None:
            per_head = len(segs)
            for h, ui in first_unit_of_head.items():
                prep_at.setdefault(max(0, ui - per_head), []).append(h)
        ebuf = {}
        norm_due = {}

        def ph_a(u):
            nq, pv = u['nq'], u['pv']
            rl = tf()
            act(rl.ap[64:65, 0:nq], psum[64:65, pv, 0:nq], AF.Ln, [PSR[pv]], rl.all())
            rr = tb()
            act(rr.ap[64:65, 0:nq], rl.ap[64:65, 0:nq], AF.Exp, rl.all(), rr.all(), scale=-1.0)
            u['rr'] = rr

        def ph_b(u):
            d, h, q0, nq, pv, rr = u['d'], u['h'], u['q0'], u['nq'], u['pv'], u['rr']
            bb = bank('M')
            mm(bb, psum[0:64, bb, 0:nq], ONES.ap[64:65, 0:64], rr.ap[64:65, 0:nq], ONES.all() + rr.all(), True, True)
            rb = tf()
            cp('dve', rb.ap[0:64, 0:nq], psum[0:64, bb, 0:nq], [PSR[bb]], rb.all())
            if not d:
                tt('dve', YH.ap[:, 4 + h, q0:q0 + nq], psum[0:64, pv, 0:nq], rb.ap[0:64, 0:nq], ALU.mult,
                   [PSR[pv]] + rb.all(), [YH.r(4 + h)])
            elif u['m'] == 0:
                tt('dve', D0.ap[0:64, q0:q0 + nq], psum[0:64, pv, 0:nq], rb.ap[0:64, 0:nq], ALU.mult,
                   [PSR[pv]] + rb.all(), D0.all())
            else:
                dt_ = tf()
                tt('dve', dt_.ap[0:64, 0:nq], psum[0:64, pv, 0:nq], rb.ap[0:64, 0:nq], ALU.mult,
                   [PSR[pv]] + rb.all(), dt_.all())
                if short_units:
                    ad, ada = D1, D1.ap[0:64, q0:q0 + nq]
                else:
                    ad = tf()
                    ada = ad.ap[0:64, 0:nq]
                stt(ada, dt_.ap[0:64, 0:nq], NEGLAM.ap[:, l:l + 1], D0.ap[0:64, q0:q0 + nq],
                    ALU.mult, ALU.add, dt_.all() + D0.all() + NEGLAM.all(), ad.all())
                sq = tb()
                tt('dve', sq.ap[0:64, 0:nq], ada, ada, ALU.mult, ad.all(), sq.all())
                u['ad'], u['ada'], u['sq'] = ad, ada, sq

        def ph_c(u):
            if not (u['d'] and u['m'] == 1):
                return
            nq, sq = u['nq'], u['sq']
            bq = bank('M')
            mm(bq, psum[0:64, bq, 0:nq], ONES.ap[0:64, 0:64], sq.ap[0:64, 0:nq], ONES.all() + sq.all(), True, True)
            if short_units:
                q0 = u['q0']
                act(RL1.ap[0:64, q0:q0 + nq], psum[0:64, bq, 0:nq], AF.Ln, [PSR[bq]], RL1.all(), scale=1.0 / 64.0, bias=1e-6)
                act(RS1.ap[0:64, q0:q0 + nq], RL1.ap[0:64, q0:q0 + nq], AF.Exp, RL1.all(), RS1.all(), scale=-0.5)
                u['rs'], u['rsa'] = RS1, RS1.ap[0:64, q0:q0 + nq]
            else:
                u['rs'] = rstd_from_sumsq(bq, 64, 64.0, nq, 1e-6)
                u['rsa'] = u['rs'].ap[0:64, 0:nq]

        def ph_d(u):
            if not (u['d'] and u['m'] == 1):
                return
            h, q0, nq, ad, rs = u['h'], u['q0'], u['nq'], u['ad'], u['rs']
            stt(YH.ap[:, h, q0:q0 + nq], u['ada'], DNW.ap[:, l:l + 1], u['rsa'],
                ALU.mult, ALU.mult, ad.all() + DNW.all() + rs.all(), [YH.r(h)])

        phases = (ph_a, ph_b, ph_c, ph_d)
        delays = (1, 5, 8, 12) if not short_units else (1, 3, 5, 7)
        NDELAY = delays[-1]

        for s in range(nst + LAG + NDELAY + 1):
            if s < nst:
                ui, i = stages[s]
                u = units[ui]
                if i == 0 and ui in prep_at:
                    for h_ in prep_at[ui]:
                        prep_head(h_)
                if i == 0 and hooks and ui in hooks:
                    hooks[ui]()
                q0, nq = u['q0'], u['nq']
                kt = u['kts'][i]
                sb_ = bank(spool)
                so = psum[:, sb_, 0:nq]
                if u['d']:
                    ka, kr = kside_d(u['c'], kt)
                    mm(sb_, so, ka, QM.ap[:, u['c'], u['j'], q0:q0 + nq], [kr, QM.r(u['c'] * 4 + u['j'])], True, True)
                    sc = DIFF_SCALE
                else:
                    ka, kr = kside_m(u['h'], kt)
                    mm(sb_, so, ka, QMH.ap[0:96, u['h'], q0:q0 + nq], [kr, QMH.r(u['h'])], True, True)
                    sc = MLA_SCALE
                e_ = et()
                act(e_.ap[:, 0:nq], so, AF.Exp, [PSR[sb_]], e_.all(), scale=sc)
                ebuf[s] = e_
            sp_ = s - LAG
            if 0 <= sp_ < nst:
                ui, i = stages[sp_]
                u = units[ui]
                nq = u['nq']
                if i == 0:
                    u['pv'] = bank(pvpool)
                kt = u['kts'][i]
                va, vr = vside_d(u['h'], kt) if u['d'] else vside_m(u['h'], kt)
                e_ = ebuf.pop(sp_)
                last = (i == len(u['kts']) - 1)
                mm(u['pv'], psum[0:65, u['pv'], 0:nq], va, e_.ap[:, 0:nq], [vr] + e_.all(), start=(i == 0), stop=last)
                if last:
                    for ph, dl in zip(phases, delays):
                        norm_due.setdefault(s + dl, []).append((ph, u))
            for ph, u in norm_due.pop(s, []):
                ph(u)
        assert not norm_due and not ebuf

    def out_proj_ln1(l, g):
        for t in range(4):
            W = wget('wout')
            for ci in range(2):
                c = 2 * t + ci
                base = ci * 1792
                wya = W.ap[:, base:base + 256].rearrange("p (k c) -> p k c", k=2)
                wh = W.ap[0:64, base + 256:base + 1792].rearrange("p (h c) -> p h c", h=12)
                b = bank('G')
                for k in range(2):
                    mm(b, psum[:, b, :], wya[:, k, :], YA.ap[:, k, :], W.all() + [YA.r(k)], start=(k == 0), stop=False)
                for hh in range(12):
                    mm(b, psum[:, b, :], wh[:, hh, :], YH.ap[:, hh, :], W.all() + [YH.r(hh)], start=False, stop=(hh == 11))
                xr = X[g].r(c)
                xa = X[g].ap[:, c, :]
                ts('dve', xa, xa, ALPHA, None, ALU.mult, None, [xr], [xr])
                stt(xa, psum[:, b, :], modv(l, 2, c, g), xa, ALU.mult, ALU.add, [PSR[b], MOD.r(l * 6 + 2), xr], [xr])
        layer_norm(l, g, V_LN1G, V_LN1B)

    def ffn(l):
        modulate(l, 1, 2, HT[1])
        for t in range(11):
            W = wget('ffa')
            wv = W.ap[:, 0:4096].rearrange("p (j w k c) -> p j w k c", j=2, w=2, k=8)
            for g, ji in ((0, 0), (0, 1), (1, 0), (1, 1)):
                j = 2 * t + ji
                if True:
                    b1, b3 = bank('G'), bank('G')
                    for k in range(8):
                        mm(b1, psum[:, b1, :], wv[:, ji, 0, k, :], HT[g].ap[:, k, :], W.all() + [HT[g].r(k)],
                           start=(k == 0), stop=(k == 7))
                    for k in range(8):
                        mm(b3, psum[:, b3, :], wv[:, ji, 1, k, :], HT[g].ap[:, k, :], W.all() + [HT[g].r(k)],
                           start=(k == 0), stop=(k == 7))
                    sl = tf()
                    act(sl.ap, psum[:, b1, :], AF.Silu, [PSR[b1]], sl.all())
                    tt('dve', GT[g].ap[:, j, :], psum[:, b3, :], sl.ap, ALU.mult, [PSR[b3]] + sl.all(), [GT[g].r(j)])
        if l + 1 < L:
            mod_piece(l + 1, 0, 4)
        for g in range(2):
            for c in range(8):
                W = wget('ff2')
                wv = W.ap[:, 0:2816].rearrange("p (j c) -> p j c", j=NJ)
                b = bank('G')
                for j in range(NJ):
                    mm(b, psum[:, b, :], wv[:, j, :], GT[g].ap[:, j, :], W.all() + [GT[g].r(j)],
                       start=(j == 0), stop=(j == NJ - 1))
                xr = X[g].r(c)
                xa = X[g].ap[:, c, :]
                ts('dve', xa, xa, ALPHA, None, ALU.mult, None, [xr], [xr])
                stt(xa, psum[:, b, :], modv(l, 5, c, g), xa, ALU.mult, ALU.add, [PSR[b], MOD.r(l * 6 + 5), xr], [xr])
            if g == 1 and l + 1 < L:
                modulate(l + 1, 0, 1, HT[0])
                premod.add((l + 1, 0))
            layer_norm(l, g, V_LN2G, V_LN2B)

    def kv_prompt(l):
        memset('dve', VAP.ap[:, :, :, 64:65], 1.0, VAP.all())
        memset('dve', VMP.ap[:, :, :, 64:65], 1.0, VMP.all())
        W = wget('wukv')
        wk = W.ap[:, 0:1024].rearrange("p (k c) -> p k c", k=2)
        wvv = W.ap[:, 1024:2048].rearrange("p (k c) -> p k c", k=2)
        for h in range(8):
            b = bank('G')
            for k in range(2):
                mm(b, psum[0:64, b, :], wk[:, k, h * 64:(h + 1) * 64], CKVNP.ap[:, k, :], W.all() + [CKVNP.r(k)],
                   start=(k == 0), stop=(k == 1))
            cp('act' if h % 2 else 'dve', KMP.ap[0:64, h, :], psum[0:64, b, :], [PSR[b]], [KMP.r(h)])
        for t4 in range(4):
            b = bank('G')
            for k in range(2):
                mm(b, psum[:, b, :], CKVNP.ap[:, k, t4 * 128:(t4 + 1) * 128], wvv[:, k, :], W.all() + [CKVNP.r(k)],
                   start=(k == 0), stop=(k == 1))
            cp('act' if t4 % 2 else 'dve', VMP.ap[:, t4, :, 0:64], psum[:, b, :].rearrange("p (h d) -> p h d", h=8),
               [PSR[b]], [VMP.r(t4)])

    def attn_prompt(l):
        segs = [(0, 256, [0, 1]), (256, 256, [2, 3])]
        attention(l, 0, segs, None,
                  lambda c, kt: (KTP.ap[:, c, kt * 128:(kt + 1) * 128], KTP.r(c)),
                  lambda h, kt: (VAP.ap[:, kt, h, :], VAP.r(kt)),
                  lambda h, kt: (KMP.ap[0:96, h, kt * 128:(kt + 1) * 128], KMP.r(h)),
                  lambda h, kt: (VMP.ap[:, kt, h, :], VMP.r(kt)),
                  hooks={2 + 8 * k: (lambda k=k: mod_piece(l, 4 + 2 * k, 6 + 2 * k, pool='M')) for k in range(4)})

    def exchange_sample(l):
        S.dma('sp', slab_in[l].ap()[:, :], SLAB.ap, reads=SLAB.all(), writes=[slab_in_res[l]])
        S.collective(lambda e: e.collective_compute("AllGather", ALU.bypass,
                                                    replica_groups=[[0, 1, 2, 3], [4, 5, 6, 7]],
                                                    ins=[slab_in[l].ap().opt()], outs=[slab_all[l].ap().opt()]),
                     reads=[slab_in_res[l]], writes=[slab_all_res[l]])

    def exchange_load(l):
        sa = slab_all[l].ap()
        memset('dve', VAA.ap[:, :, :, 64:65], 1.0, VAA.all())
        memset('dve', VMA.ap[:, :, :, 64:65], 1.0, VMA.all())
        for r in range(4):
            rows = sa[r * 128:(r + 1) * 128, :]
            S.dma('sp', KTA.ap[:, :, r * NT:(r + 1) * NT], rows[:, 0:1024].rearrange("p (c t) -> p c t", c=2),
                  reads=[slab_all_res[l]], writes=[KTA.r(r), KTA.r(5 + r)])
            S.dma('sp', CKVTA.ap[:, :, r * NT:(r + 1) * NT], rows[:, 1024:2048].rearrange("p (c t) -> p c t", c=2),
                  reads=[slab_all_res[l]], writes=[CKVTA.r(r), CKVTA.r(5 + r)])
            S.dma('sp', VAA.ap[:, 4 * r:4 * r + 4, :, 0:64],
                  rows[:, 2048:3072].rearrange("p (t h d) -> p t h d", t=4, h=4),
                  reads=[slab_all_res[l]], writes=[VAA.r(4 * r + i) for i in range(4)])
            S.dma('sp', KPEA.ap[64:96, r * NT:(r + 1) * NT], sa[r * 128 + 64:r * 128 + 96, 3072:3072 + NT],
                  reads=[slab_all_res[l]], writes=[KPEA.r(r)])
            S.dma('sp', UBALL.ap[:, r, :], rows[:, 3584:3588], reads=[slab_all_res[l]], writes=UBALL.all())
        S.dma('pool', KTA.ap[:, :, 2048:2560], d_ckT[l], writes=[KTA.r(4), KTA.r(9)])
        S.dma('pool', CKVTA.ap[:, :, 2048:2560], d_cckvT[l], writes=[CKVTA.r(4), CKVTA.r(9)])
        S.dma('pool', VAA.ap[:, 16:20, :, 0:64], d_cv[l].rearrange("p t (h d) -> p t h d", h=4),
              writes=[VAA.r(16 + i) for i in range(4)])
        S.dma('pool', KPEA.ap[64:96, 2048:2560], d_ckpeT[l], writes=[KPEA.r(4)])
        cp('dve', UBF.ap, UBALL.ap, UBALL.all(), UBF.all())
        for cc in range(2):
            for side in range(2):
                dst = US.ap[:, cc, 0, 0:1] if side == 0 else US.ap[:, cc, 0, NT + 1:NT + 2]
                col = 2 * cc + (1 if side == 0 else 0)
                ts('dve', dst, UBF.ap[:, 0, col:col + 1], vcol(V_HALO + 4 * side + 0), None, ALU.mult, None,
                   UBF.all() + vres, [US.r(cc)])
                for r in range(1, 4):
                    stt(dst, UBF.ap[:, r, col:col + 1], vcol(V_HALO + 4 * side + r), dst, ALU.mult, ALU.add,
                        UBF.all() + vres + [US.r(cc)], [US.r(cc)])
        for i in range(2):
            cp('dve' if i else 'act', KMH[i].ap[64:96, :], KPEA.ap[64:96, :], KPEA.all(), KMH[i].all())

    def kv_sample(l):
        W = wget('wukv')
        wk = W.ap[:, 0:1024].rearrange("p (k c) -> p k c", k=2)
        wvv = W.ap[:, 1024:2048].rearrange("p (k c) -> p k c", k=2)
        for kt in range(20):
            b = bank('G')
            for k in range(2):
                mm(b, psum[:, b, :], CKVTA.ap[:, k, kt * 128:(kt + 1) * 128], wvv[:, k, :], W.all() + [CKVTA.r(k * 5 + kt // 4)],
                   start=(k == 0), stop=(k == 1))
            cp('act' if kt % 2 else 'dve', VMA.ap[:, kt, :, 0:64], psum[:, b, :].rearrange("p (h d) -> p h d", h=8),
               [PSR[b]], [VMA.r(kt)])

        def prep_head(h):
            kb_ = KMH[h % 2]
            for kb in range(5):
                b = bank('M')
                for k in range(2):
                    mm(b, psum[0:64, b, :], wk[:, k, h * 64:(h + 1) * 64], CKVTA.ap[:, k, kb * 512:(kb + 1) * 512],
                       W.all() + [CKVTA.r(k * 5 + kb)], start=(k == 0), stop=(k == 1))
                cp('dve', kb_.ap[0:64, kb * 512:(kb + 1) * 512], psum[0:64, b, :], [PSR[b]], kb_.all())
        return prep_head

    def attn_sample(l, prep_head):
        segs = [(0, NT, list(range(20)))]
        attention(l, 1, segs, None,
                  lambda c, kt: (KTA.ap[:, c, kt * 128:(kt + 1) * 128], KTA.r(c * 5 + kt // 4)),
                  lambda h, kt: (VAA.ap[:, kt, h, :], VAA.r(kt)),
                  lambda h, kt: (KMH[h % 2].ap[0:96, kt * 128:(kt + 1) * 128], KMH[h % 2].r(0)),
                  lambda h, kt: (VMA.ap[:, kt, h, :], VMA.r(kt)),
                  prep_head=prep_head)

    try:
      for l in range(L):
        chk(1)
        mixer_project(l, 0)
        chk(2)
        conv_ya(l, 0)
        kv_prompt(l)
        chk(3)
        attn_prompt(l)
        chk(5)
        mixer_project(l, 1, mid_hook=lambda: exchange_sample(l))
        chk(6)
        exchange_load(l)
        out_proj_ln1(l, 0)
        chk(7)
        conv_ya(l, 1)
        ph = kv_sample(l)
        chk(8)
        attn_sample(l, ph)
        chk(9)
        modulate(l, 0, 2, HT[0])
        out_proj_ln1(l, 1)
        chk(10)
        ffn(l)
        chk(11)
    except _Stop:
        pass

    for g, dst in ((0, o_yp), (1, o_ys)):
        r = Res("out")
        S.dma('sp', dst, X[g].ap, reads=X[g].all(), writes=[r])
        out_res.append(r)
    S.final_wait('sp', out_res)
    assert DEBUG == 2 or STOP or wstate['next_get'] == len(order), (wstate, len(order))

    with nc.Block() as block:
        @block.tensor
        def _(e):
            for f in S.q['pe'].ops:
                f(e)

        @block.scalar
        def _(e):
            for f in S.q['act'].ops:
                f(e)

        @block.vector
        def _(e):
            for f in S.q['dve'].ops:
                f(e)

        @block.gpsimd
        def _(e):
            for f in S.q['pool'].ops:
                f(e)

        @block.sync
        def _(e):
            for f in S.q['sp'].ops:
                f(e)
    es.close()
    return nc


def _partner32():
    p = np.zeros(32, dtype=np.int64)
    for r in range(32):
        r16 = r % 16
        p[r] = r - r16 + ((r16 + 8) % 16)
    return p


def _fm(w, ncol_chunks=None):
    K, C = w.shape
    return w.reshape(K // 128, 128, C // 128, 128).transpose(1, 2, 0, 3)


def pack_weights(inp):
    ws = np.zeros((128, WT_TOTAL), dtype=np.float32)
    part = _partner32()

    def put(key, arr):
        off, n = WT_OFFS[key]
        a = np.ascontiguousarray(arr, dtype=np.float32).reshape(arr.shape[0], -1)
        assert a.shape[1] == n, (key, a.shape, n)
        ws[:a.shape[0], off:off + n] = a

    for l in range(L):
        wada = inp["w_ada"][l]
        fm = _fm(wada)
        for t in range(12):
            put(('wada', l, t), fm[:, 4 * t:4 * t + 4])
        w_in = inp["w_in"][l]
        a_x, a_b, a_c = w_in[:, 0:256], w_in[:, 256:512], w_in[:, 512:768]
        d_q, d_k, d_v = w_in[:, 768:1024], w_in[:, 1024:1280], w_in[:, 1280:1536]
        m_cq, m_ckv, m_kpe = w_in[:, 1536:1920], w_in[:, 1920:2176], w_in[:, 2176:2208]
        perm256 = np.concatenate([b * 32 + part for b in range(8)])
        kpe_pad = np.zeros((1024, 128), np.float32)
        kpe_pad[:, 64:96] = m_kpe
        kpesw_pad = np.zeros((1024, 128), np.float32)
        kpesw_pad[:, 64:96] = m_kpe[:, part]
        cols = np.concatenate([a_x, a_b, a_c, d_q, d_k, m_cq, m_ckv, kpe_pad, d_q[:, perm256], d_k[:, perm256],
                               kpesw_pad, np.zeros((1024, 384), np.float32)], axis=1)
        assert cols.shape[1] == 24 * 128
        fm = _fm(cols)
        sorder = (S_ORDER_A + S_ORDER_B + [21, 21, 21])
        for t in range(6):
            put(('wins', l, t), fm[:, sorder[4 * t:4 * t + 4]])
        put(('wv', l), d_v.reshape(8, 128, 256).transpose(1, 0, 2))
        wuq = inp["w_uq"][l]
        put(('wuq', l), wuq.reshape(3, 128, 768).transpose(1, 0, 2))
        permq = np.arange(768)
        for h in range(8):
            permq[h * 96 + 64:h * 96 + 96] = h * 96 + 64 + part
        put(('wuqsw', l), wuq[:, permq].reshape(3, 128, 768).transpose(1, 0, 2))
        wukv = inp["w_ukv"][l].reshape(256, 8, 128)
        wk = wukv[:, :, 0:64].reshape(256, 512)
        wvv = wukv[:, :, 64:128].reshape(256, 512)
        both = np.concatenate([wk.reshape(2, 128, 512).transpose(1, 0, 2).reshape(128, 1024),
                               wvv.reshape(2, 128, 512).transpose(1, 0, 2).reshape(128, 1024)], axis=1)
        put(('wukv', l), both)
        wout = inp["w_out"][l]
        for t in range(4):
            blk = np.zeros((128, 2, 1792), np.float32)
            for ci in range(2):
                c = 2 * t + ci
                wc = wout[:, c * 128:(c + 1) * 128]
                blk[:, ci, 0:256] = wc[0:256].reshape(2, 128, 128).transpose(1, 0, 2).reshape(128, 256)
                blk[0:64, ci, 256:1792] = wc[256:1024].reshape(12, 64, 128).transpose(1, 0, 2).reshape(64, 1536)
            put(('wout', l, t), blk)
        f1 = _fm(inp["w_ff1"][l])
        f3 = _fm(inp["w_ff3"][l])
        for t in range(11):
            blk = np.stack([np.stack([f1[:, 2 * t + ji], f3[:, 2 * t + ji]], axis=1) for ji in range(2)], axis=1)
            put(('ffa', l, t), blk)
        f2 = inp["w_ff2"][l]
        for c in range(8):
            put(('ff2', l, c), f2[:, c * 128:(c + 1) * 128].reshape(NJ, 128, 128).transpose(1, 0, 2))
    return ws


def rope_tables(qidx):
    t = np.arange(NT) + qidx * NT
    row = (t // 64).astype(np.float32)
    col = (t % 64).astype(np.float32)
    freqs = (np.float32(10000.0) ** (-np.arange(8, dtype=np.float32) / np.float32(8))).astype(np.float32)
    tab = np.zeros((128, 2, NT), np.float32)
    for r in range(32):
        pos = row if r < 16 else col
        r16 = r % 16
        ang = (pos * freqs[r16 % 8]).astype(np.float32)
        cs = np.cos(ang).astype(np.float32)
        sn = np.sin(ang).astype(np.float32)
        tab[r, 0] = cs
        tab[r, 1] = -sn if r16 < 8 else sn
    for b in range(1, 4):
        tab[b * 32:(b + 1) * 32] = tab[0:32]
    return tab


def make_vec(inp, qidx):
    v = np.zeros((128, NV), np.float32)
    for l in range(L):
        o = l * VPL
        for tap in range(3):
            for c in range(2):
                v[:, o + V_CONV + tap * 2 + c] = inp["conv_w"][l, tap, c * 128:(c + 1) * 128]
        v[:, o + V_QNW:o + V_QNW + 3] = inp["q_norm_w"][l].reshape(3, 128).T
        v[:, o + V_KVW:o + V_KVW + 2] = inp["kv_norm_w"][l].reshape(2, 128).T
        v[:, o + V_LN1G:o + V_LN1G + 8] = inp["ln1_g"][l].reshape(8, 128).T
        v[:, o + V_LN1B:o + V_LN1B + 8] = inp["ln1_b"][l].reshape(8, 128).T
        v[:, o + V_LN2G:o + V_LN2G + 8] = inp["ln2_g"][l].reshape(8, 128).T
        v[:, o + V_LN2B:o + V_LN2B + 8] = inp["ln2_b"][l].reshape(8, 128).T
        v[0:64, o + V_DNW] = inp["diff_norm_w"][l]
        v[64:128, o + V_DNW] = inp["diff_norm_w"][l]
        ba = inp["b_ada"][l].reshape(48, 128).T
        v[:, o + V_BADA:o + V_BADA + 96] = np.repeat(ba[:, :, None], 2, axis=2).reshape(128, 96)
    for j in range(4):
        v[j * 32:(j + 1) * 32, V_MASK + j] = 1.0
    for r in range(4):
        v[:, V_HALO + r] = 1.0 if r == qidx - 1 else 0.0
        v[:, V_HALO + 4 + r] = 1.0 if r == qidx + 1 else 0.0
    return v


_NC_CACHE = {}


def make_in_maps(inp):
    ws = pack_weights(inp)
    in_maps = []
    for c in range(8):
        b, qi = c // 4, c % 4
        xp = inp["x_prompt"][2 * c:2 * c + 2].reshape(NT, 8, 128).transpose(2, 1, 0)
        xs = inp["x_sample"][b, qi * NT:(qi + 1) * NT].reshape(NT, 8, 128).transpose(2, 1, 0)
        cond = np.stack([inp["c_ctx"].reshape(8, 128).T, inp["c"][b].reshape(8, 128).T], axis=2)
        lamv = np.stack([np.stack([inp["lam_q1"][l], inp["lam_k1"][l], inp["lam_q2"][l], inp["lam_k2"][l]])
                         for l in range(L)]).reshape(1, -1)
        ck = inp["cache_diff_k"][b]
        ckT = ck.transpose(0, 1, 2, 4, 3).reshape(L, 2, 128, 512).transpose(0, 2, 1, 3)
        cv = inp["cache_diff_v"][b].transpose(0, 2, 1, 3).reshape(L, 4, 128, 256).transpose(0, 2, 1, 3)
        cckvT = inp["cache_mla_ckv"][b].transpose(0, 2, 1).reshape(L, 2, 128, 512).transpose(0, 2, 1, 3)
        ckpeT = inp["cache_mla_kpe"][b].transpose(0, 2, 1)
        in_maps.append({
            "xp": np.ascontiguousarray(xp, np.float32),
            "xs": np.ascontiguousarray(xs, np.float32),
            "cond": np.ascontiguousarray(cond, np.float32),
            "vec": make_vec(inp, qi),
            "rope": rope_tables(qi),
            "lamv": np.ascontiguousarray(np.repeat(lamv, 64, axis=0), np.float32),
            "ws": ws,
            "ckT": np.ascontiguousarray(ckT, np.float32),
            "cv": np.ascontiguousarray(cv, np.float32),
            "cckvT": np.ascontiguousarray(cckvT, np.float32),
            "ckpeT": np.ascontiguousarray(ckpeT, np.float32),
        })
    return in_maps


def assemble(R):
    y_p = np.zeros((16, 256, D), np.float32)
    y_s = np.zeros((2, 2048, D), np.float32)
    st_k = np.zeros((16, L, 4, 2, 256, 32), np.float32)
    st_v = np.zeros((16, L, 4, 256, 64), np.float32)
    st_ckv = np.zeros((16, L, 256, 256), np.float32)
    st_kpe = np.zeros((16, L, 256, 32), np.float32)
    shp = {"yp": (128, 8, NT), "ys": (128, 8, NT), "sk": (L, 128, 2, NT), "sv": (L, 128, 4, 256),
           "sckv": (L, 128, 2, NT), "skpe": (L, 32, NT)}
    for c in range(8):
        b, qi = c // 4, c % 4
        r = {k: np.asarray(R[c][k]).reshape(shp[k]) for k in shp}
        y_p[2 * c:2 * c + 2] = r["yp"].transpose(2, 1, 0).reshape(2, 256, D)
        y_s[b, qi * NT:(qi + 1) * NT] = r["ys"].transpose(2, 1, 0).reshape(NT, D)
        sk = r["sk"].transpose(0, 2, 1, 3).reshape(L, 4, 2, 32, 2, 256)
        st_k[2 * c:2 * c + 2] = sk.transpose(4, 0, 1, 2, 5, 3)
        sv = r["sv"].transpose(0, 2, 1, 3).reshape(L, 2, 256, 4, 64)
        st_v[2 * c:2 * c + 2] = sv.transpose(1, 0, 3, 2, 4)
        sc = r["sckv"].transpose(0, 2, 1, 3).reshape(L, 256, 2, 256)
        st_ckv[2 * c:2 * c + 2] = sc.transpose(2, 0, 3, 1)
        sp = r["skpe"].reshape(L, 32, 2, 256)
        st_kpe[2 * c:2 * c + 2] = sp.transpose(2, 0, 3, 1)
    return (y_p, y_s, st_k, st_v, st_ckv, st_kpe)


def kernel(**inputs):
    inp = {k: np.asarray(v) for k, v in inputs.items()}
    if 'nc' not in _NC_CACHE:
        _NC_CACHE['nc'] = build_program()
    nc = _NC_CACHE['nc']
    in_maps = make_in_maps(inp)
    res = run_bass_kernel_spmd(nc, in_maps, core_ids=list(range(8)))
    return assemble(res.results)
```
